# Optimizing a Trainium2 kernel written in Bass

```python
import jax, jax.numpy as jnp
from jax import lax
import numpy as np

D_MODEL = 2048
BATCH = 4
SEQ = 2048
DEPTH = 2
DEC_BATCH = 128
DEC_SEQ = 4
PAST_LEN = 16384
PAGE_SIZE = 128

N_MIXERS = 2
N_LAYERS_A = (DEPTH + 1) // 2
N_LAYERS_B = DEPTH // 2
D_PLE = 256
D_FF = 5632
CHUNK_A = 128
D_A = D_MODEL
GROUPS_A = 16
GROUP_DIM_A = D_A // GROUPS_A
D_B = D_MODEL
HEADS_B = D_B // 128
DK_B = D_B // HEADS_B
DV_B = D_B // HEADS_B
CHUNK_B = 64
EPS = 1e-6

kernel_name = "macaron_gmlp_hgrn2_hybrid_step"


def rms_norm(x, g):
    xf = x.astype(jnp.float32)
    y = xf * lax.rsqrt(jnp.mean(xf * xf, axis=-1, keepdims=True) + EPS)
    return (y * g.astype(jnp.float32)).astype(x.dtype)


def layer_norm(x, g, b):
    xf = x.astype(jnp.float32)
    mu = jnp.mean(xf, axis=-1, keepdims=True)
    xc = xf - mu
    var = jnp.mean(xc * xc, axis=-1, keepdims=True)
    y = xc * lax.rsqrt(var + EPS) * g.astype(jnp.float32) + b.astype(jnp.float32)
    return y.astype(x.dtype)


def swiglu_ffn(x, w_in, w_out):
    gate, up = jnp.split(x @ w_in, 2, axis=-1)
    return (jax.nn.silu(gate) * up) @ w_out


def spatial_gating(v, w_s, b_s):
    bn, L, G, GD = v.shape
    if L <= CHUNK_A:
        c = L
        w = w_s[:, :L, :L]
        bias = b_s[:, :L]
        vv = v[:, None]
    else:
        n = -(-L // CHUNK_A)
        pad = n * CHUNK_A - L
        c = CHUNK_A
        w = w_s
        bias = b_s
        vv = jnp.pad(v, ((0, 0), (0, pad), (0, 0), (0, 0))).reshape(bn, n, CHUNK_A, G, GD)
    mask = jnp.tril(jnp.ones((c, c), dtype=bool))
    w = jnp.where(mask[None], w, jnp.zeros_like(w))
    s = jnp.einsum('gts,bnsgd->bntgd', w, vv) + bias.T[None, None, :, :, None]
    return s.reshape(bn, -1, G, GD)[:, :L]


def mixer_gmlp(xn, w_in, ln_g, ln_b, w_s, b_s, w_out):
    bn, L, _ = xn.shape
    z = jax.nn.gelu(xn @ w_in, approximate=False)
    u, v = jnp.split(z, 2, axis=-1)
    v = layer_norm(v, ln_g, ln_b)
    s = spatial_gating(v.reshape(bn, L, GROUPS_A, GROUP_DIM_A), w_s, b_s).reshape(bn, L, D_A)
    return (u * s) @ w_out, v


def hgrn2_recurrence(q, k, logf, v, s0):
    bn, L, H, K = q.shape
    V = v.shape[-1]
    c = CHUNK_B if L % CHUNK_B == 0 else L
    n = L // c

    def blocks(t):
        return t.reshape(bn, n, c, H, t.shape[-1]).transpose(1, 0, 2, 3, 4)

    mask = jnp.tril(jnp.ones((c, c), dtype=bool))[None, :, :, None, None]

    def step(S, inp):
        qc, kc, fc, vc = inp
        b = jnp.cumsum(fc, axis=1)
        inter = jnp.einsum('bthk,bhkv->bthv', qc * jnp.exp(b), S)
        diff = b[:, :, None] - b[:, None, :]
        decay = jnp.exp(jnp.where(mask, diff, -jnp.inf))
        att = jnp.sum(qc[:, :, None] * decay * kc[:, None], axis=-1)
        intra = jnp.einsum('btsh,bshv->bthv', att, vc)
        b_last = b[:, -1]
        k_dec = kc * jnp.exp(b_last[:, None] - b)
        S_new = jnp.exp(b_last)[..., None] * S + jnp.einsum('bshk,bshv->bhkv', k_dec, vc)
        return S_new, inter + intra

    s_final, out = lax.scan(step, s0, (blocks(q), blocks(k), blocks(logf), blocks(v)))
    return out.transpose(1, 0, 2, 3, 4).reshape(bn, L, H, V), s_final


def mixer_hgrn2(xn, s0, w_in, lb, norm_g, w_out):
    bn, L, _ = xn.shape
    q, fz, i, g = jnp.split(xn @ w_in, 4, axis=-1)
    q = jax.nn.silu(q.astype(jnp.float32)).reshape(bn, L, HEADS_B, DK_B)
    f = lb + (1.0 - lb) * jax.nn.sigmoid(fz.astype(jnp.float32))
    logf = jnp.log(f).reshape(bn, L, HEADS_B, DK_B)
    k = (1.0 - f).reshape(bn, L, HEADS_B, DK_B)
    vi = i.astype(jnp.float32).reshape(bn, L, HEADS_B, DV_B)
    o, s_final = hgrn2_recurrence(q, k, logf, vi, s0.astype(jnp.float32))
    o = o * lax.rsqrt(jnp.mean(o * o, axis=-1, keepdims=True) + EPS)
    o = o * norm_g.astype(jnp.float32).reshape(HEADS_B, DV_B)
    o = o.reshape(bn, L, D_B).astype(xn.dtype) * jax.nn.silu(g)
    return o @ w_out, s_final


def trunk(x, p, states_b, ffn1_w_in, ffn1_w_out, ffn2_w_in, ffn2_w_out,
          norm_ffn1, norm_mix, norm_ffn2, norm_ple,
          a_w_in, a_ln_g, a_ln_b, a_w_s, a_b_s, a_w_out,
          b_w_in, b_lb_logits, b_norm_g, b_w_out,
          ple_w_proj, ple_w_gate, final_norm):
    sm = jax.nn.softmax(b_lb_logits.astype(jnp.float32), axis=0)
    lbs = jnp.cumsum(sm, axis=0) - sm[0]
    h = x
    new_v, new_s = [], []
    for li in range(DEPTH):
        h = h + 0.5 * swiglu_ffn(rms_norm(h, norm_ffn1[li]), ffn1_w_in[li], ffn1_w_out[li])
        xn = rms_norm(h, norm_mix[li])
        j = li // N_MIXERS
        if li % N_MIXERS == 0:
            y, v = mixer_gmlp(xn, a_w_in[j], a_ln_g[j], a_ln_b[j], a_w_s[j], a_b_s[j], a_w_out[j])
            new_v.append(v)
        else:
            y, s_fin = mixer_hgrn2(xn, states_b[j], b_w_in[j], lbs[li], b_norm_g[j], b_w_out[j])
            new_s.append(s_fin)
        h = h + y
        h = h + 0.5 * swiglu_ffn(rms_norm(h, norm_ffn2[li]), ffn2_w_in[li], ffn2_w_out[li])
        gate = jax.nn.sigmoid(rms_norm(h, norm_ple[li]) @ ple_w_gate[li])
        h = h + (p[li] @ ple_w_proj[li]) * gate
    return rms_norm(h, final_norm), jnp.stack(new_v), jnp.stack(new_s)


def setup_inputs(seed: int = 0) -> dict:
    key = jax.random.key(seed)
    ks = iter(jax.random.split(key, 32))

    def nrm(shape, scale):
        return jax.random.normal(next(ks), shape, dtype=jnp.float32) * scale

    def gain(shape):
        return 1.0 + nrm(shape, 0.1)

    d = D_MODEL
    return {
        "x_prompt": nrm((BATCH, SEQ, d), 1.0),
        "x_sample": nrm((DEC_BATCH, DEC_SEQ, d), 1.0),
        "state_hgrn": nrm((N_LAYERS_B, DEC_BATCH, HEADS_B, DK_B, DV_B), 0.5),
        "p_prompt": nrm((DEPTH, BATCH, SEQ, D_PLE), 1.0),
        "p_sample": nrm((DEPTH, DEC_BATCH, DEC_SEQ, D_PLE), 1.0),
        "ffn1_w_in": nrm((DEPTH, d, 2 * D_FF), d ** -0.5),
        "ffn1_w_out": nrm((DEPTH, D_FF, d), D_FF ** -0.5),
        "ffn2_w_in": nrm((DEPTH, d, 2 * D_FF), d ** -0.5),
        "ffn2_w_out": nrm((DEPTH, D_FF, d), D_FF ** -0.5),
        "norm_ffn1": gain((DEPTH, d)),
        "norm_mix": gain((DEPTH, d)),
        "norm_ffn2": gain((DEPTH, d)),
        "norm_ple": gain((DEPTH, d)),
        "a_w_in": nrm((N_LAYERS_A, d, 2 * D_A), d ** -0.5),
        "a_ln_g": gain((N_LAYERS_A, D_A)),
        "a_ln_b": nrm((N_LAYERS_A, D_A), 0.02),
        "a_w_s": nrm((N_LAYERS_A, GROUPS_A, CHUNK_A, CHUNK_A), CHUNK_A ** -0.5),
        "a_b_s": gain((N_LAYERS_A, GROUPS_A, CHUNK_A)),
        "a_w_out": nrm((N_LAYERS_A, D_A, d), D_A ** -0.5),
        "b_w_in": nrm((N_LAYERS_B, d, 4 * D_B), d ** -0.5),
        "b_lb_logits": nrm((DEPTH, D_B), 0.5),
        "b_norm_g": gain((N_LAYERS_B, D_B)),
        "b_w_out": nrm((N_LAYERS_B, D_B, d), D_B ** -0.5),
        "ple_w_proj": nrm((DEPTH, D_PLE, d), D_PLE ** -0.5),
        "ple_w_gate": nrm((DEPTH, d, d), d ** -0.5),
        "final_norm": gain((d,)),
    }


def reference(x_prompt, x_sample, state_hgrn, p_prompt, p_sample,
              ffn1_w_in, ffn1_w_out, ffn2_w_in, ffn2_w_out,
              norm_ffn1, norm_mix, norm_ffn2, norm_ple,
              a_w_in, a_ln_g, a_ln_b, a_w_s, a_b_s, a_w_out,
              b_w_in, b_lb_logits, b_norm_g, b_w_out,
              ple_w_proj, ple_w_gate, final_norm):
    weights = (ffn1_w_in, ffn1_w_out, ffn2_w_in, ffn2_w_out,
               norm_ffn1, norm_mix, norm_ffn2, norm_ple,
               a_w_in, a_ln_g, a_ln_b, a_w_s, a_b_s, a_w_out,
               b_w_in, b_lb_logits, b_norm_g, b_w_out,
               ple_w_proj, ple_w_gate, final_norm)
    bsz = x_prompt.shape[0]
    s0_prompt = jnp.zeros((N_LAYERS_B, bsz, HEADS_B, DK_B, DV_B), dtype=jnp.float32)
    y_prompt, _, state_hgrn_prompt = trunk(x_prompt, p_prompt, s0_prompt, *weights)
    y_sample, state_gmlp_v_sample, state_hgrn_sample = trunk(x_sample, p_sample, state_hgrn, *weights)
    return (y_prompt, y_sample, state_hgrn_prompt, state_hgrn_sample, state_gmlp_v_sample)
```

```python
import numpy as np
from contextlib import ExitStack
import ml_dtypes
import concourse.bass as bass
import concourse.mybir as mybir
from concourse.bass_utils import run_bass_kernel_spmd

F32 = mybir.dt.float32
BF16 = mybir.dt.bfloat16
AF = mybir.ActivationFunctionType
ALU = mybir.AluOpType

NCORES = 8
D = 2048
KC = 16
NT = 1088
NPR = 1024
NSM = 64
DFF = 5632
NJ = 44
NQ = 4
JQ = 11
DPLE = 256
EPS = 1e-6
TBS = [(0, 512), (512, 512), (1024, 64)]
NTILE = 9
RING = 3
SLOT = 4096

V_NF1, V_NMIX, V_NF2, V_NPLE = 0, 2, 4, 6
V_FIN, V_BNG, V_L0, V_L1 = 8, 9, 10, 11
NV = 12

_DBG = {"stop": None}
TWO_LAUNCH = False


class Buf:
    __slots__ = ("w", "r")

    def __init__(self):
        self.w = None
        self.r = {}


class Prog:
    def __init__(self, nc, es):
        self.nc = nc
        self.es = es
        self.eng = {"pe": nc.tensor, "act": nc.scalar, "dve": nc.vector, "pool": nc.gpsimd, "sp": nc.sync}
        self.sems = {}
        self.cnt = {}
        self.seen = {}
        self.nsig = 0
        self.pending = {}
        self.T = []
        for e in self.eng:
            self._mk(e)

    def phase(self):
        self.T = [(k, v) for k, v in self.cnt.items() if v > 0]
        self.pending = {e: True for e in self.eng}

    def _mk(self, name):
        self.sems[name] = self.es.enter_context(self.nc.semaphore("s_" + name))
        self.cnt[name] = 0

    def chan(self, name):
        if name not in self.sems:
            self._mk(name)
        return name

    def wait(self, eng, t):
        if t is None:
            return
        p, n = t
        if p == eng and eng == "pe":
            return
        k = (eng, p)
        if self.seen.get(k, 0) >= n:
            return
        self.seen[k] = n
        self.eng[eng].wait_ge(self.sems[p], n)

    def _deps(self, eng, reads, writes, extra, skip_phase=False):
        if self.pending.get(eng) and not skip_phase:
            self.pending[eng] = False
            for t in self.T:
                self.wait(eng, t)
        for t in extra:
            self.wait(eng, t)
        for b in reads:
            self.wait(eng, b.w)
        for b in writes:
            self.wait(eng, b.w)
            for t in list(b.r.values()):
                self.wait(eng, t)

    def _commit(self, t, reads, writes):
        for b in reads:
            b.r[t[0]] = t
        for b in writes:
            b.w = t
            b.r = {}

    def op(self, eng, fn, reads=(), writes=(), extra=()):
        self._deps(eng, reads, writes, extra)
        ins = fn(self.eng[eng])
        self.cnt[eng] += 1
        ins.then_inc(self.sems[eng], 1)
        t = (eng, self.cnt[eng])
        self._commit(t, reads, writes)
        return t

    def dma(self, q, ch, out, in_, reads=(), writes=(), extra=(), n=1, fn=None, skip_phase=False):
        if ch.endswith("*"):
            self.nsig += 1
            ch = ch[:-1] + str(self.nsig)
        self.chan(ch)
        self._deps(q, reads, writes, extra, skip_phase)
        e = self.eng[q]
        if fn is None:
            e.dma_start(out=out, in_=in_).then_inc(self.sems[ch], 16)
        else:
            n = fn(e, self.sems[ch])
        self.cnt[ch] += 16 * n
        t = (ch, self.cnt[ch])
        self._commit(t, reads, writes)
        return t


def build_program(mode="fused"):
    nc = bass.Bass("TRN2", target_bir_lowering=False)
    es = ExitStack()
    with es:
        _build(nc, es, mode)
    return nc


def _build(nc, es, mode="fused"):
    P = Prog(nc, es)
    stop = _DBG["stop"]

    def din(name, shape, dt=F32):
        return nc.dram_tensor(name, list(shape), dt, kind="ExternalInput").ap()

    def dout(name, shape, dt=F32):
        return nc.dram_tensor(name, list(shape), dt, kind="ExternalOutput").ap()

    xT = din("xT", [D, NT])
    pT = din("pT", [2, DPLE, NT])
    s0 = din("s0", [16, 16, 128, 128])
    vfm = din("vfm", [128, NV, KC])
    cmask = din("cmask", [128, 8, 128])
    par = din("par", [128, 8])
    w1 = [din(f"w1_{i}", [2, NJ, 128, KC * 256]) for i in range(2)]
    w2 = [din(f"w2_{i}", [2, NQ, 8, 128, JQ * 256]) for i in range(2)]
    a_w_in = din("a_w_in", [D, 2 * D])
    a_w_out = din("a_w_out", [D, D])
    b_w_in = din("b_w_in", [D, 4 * D])
    b_w_out = din("b_w_out", [D, D])
    ple_wp = din("ple_wp", [2, DPLE, D])
    ple_wg = din("ple_wg", [2, D, D])
    a_gb = din("a_gb", [128, 2, D])
    a_wsT = din("a_wsT", [128, 16, 128])
    a_wsS = din("a_wsS", [64, 16, 64])
    a_bs = din("a_bs", [16, 128])
    a_bsS = din("a_bsS", [16, 64])
    sel16 = din("sel16", [128, 16, 128])
    lb_bc = din("lb_bc", [128, 2, D])
    s_in = din("s_in", [16, 128, 128]) if mode == "B" else None

    ag_in = nc.dram_tensor("ag_in", [128, 2048], F32).ap()
    ag_out = nc.dram_tensor("ag_out", [256, 2048], F32).ap()
    yT = dout("yT", [D, NT])
    sp_out = dout("sp_out", [16, 128, 128])
    ss_out = dout("ss_out", [16, 16, 128, 128])
    vs_out = dout("vs_out", [NSM, D])

    def sb(name, shape, dt):
        return es.enter_context(nc.sbuf_tensor(name, list(shape), dt))

    h = sb("h", [128, KC, NT], F32)
    xn = sb("xn", [128, KC, NT], BF16)
    ring = sb("ring", [128, RING, SLOT], BF16)
    vf = sb("vf", [128, NV, KC], F32)
    cm = sb("cm", [128, 8, 128], F32)
    cmb = sb("cmb", [128, 8, 128], BF16)
    onesb = sb("onesb", [128, 128], BF16)
    parS = sb("parS", [128, 8], F32)
    epsS = sb("epsS", [128, 1], F32)
    ARF = 9728
    ARB = 16768
    arF = sb("arF", [128, ARF], F32)
    arB = sb("arB", [128, ARB], BF16)
    ps = es.enter_context(nc.psum_tensor("ps", [128, 8 * 512], F32))

    h_b = [Buf() for _ in range(KC)]
    xn_b = [Buf() for _ in range(KC)]
    ring_b = [Buf() for _ in range(RING)]
    bank_b = [Buf() for _ in range(8)]
    cst_b = Buf()
    st = {"ring": 0, "bank": 0}

    def bank():
        i = st["bank"] % 7
        st["bank"] += 1
        return i, ps[:, i * 512:(i + 1) * 512], bank_b[i]

    def aux_bank():
        return 7, ps[:, 7 * 512:8 * 512], bank_b[7]

    def wslot():
        i = st["ring"] % RING
        st["ring"] += 1
        return i, ring_b[i]

    def wload(srcs, kc, ncol):
        i, rb = wslot()
        view = ring[:, i, 0:kc * ncol].rearrange("p (k n) -> p k n", k=kc)

        def fn(e, sem):
            for (src, off, w) in srcs:
                e.dma_start(out=view[:, :, off:off + w], in_=src).then_inc(sem, 16)
            return len(srcs)
        P.dma("pool", f"ring{i}", None, None, writes=[rb], fn=fn, skip_phase=True)
        return view, rb

    P.dma("sp", "init*", h[:], xT.rearrange("(c p) n -> p c n", p=128), writes=h_b)
    P.dma("sp", "init*", vf[:], vfm, writes=[cst_b])
    P.dma("sp", "init*", cm[:], cmask, writes=[cst_b])
    P.dma("sp", "init*", parS[:], par, writes=[cst_b])
    P.op("dve", lambda e: e.tensor_copy(out=cmb[:], in_=cm[:]), reads=[cst_b], writes=[cst_b])
    P.op("dve", lambda e: e.memset(onesb[:], 1.0), writes=[cst_b])
    P.op("dve", lambda e: e.memset(epsS[:], EPS), writes=[cst_b])
    TRI, RM, TRI4, RM4, OH = 0, 1, 2, 3, 4
    if _DBG.get("earlycc"):
        ncr = _DBG.get("ncores", NCORES)
        groups0 = [[2 * i, 2 * i + 1] for i in range(ncr // 2)]
        ag0_in = nc.dram_tensor("ag0_in", [128, 1024], F32).ap()
        ag0_out = nc.dram_tensor("ag0_out", [256, 1024], F32).ap()
        P.dma("pool", "agi0", ag0_in, cm[:, :, :].rearrange("p a b -> p (a b)"), reads=[cst_b])
        P.chan("agc0")
        P.wait("pool", ("agi0", 16))
        nc.gpsimd.collective_compute("AllGather", ALU.bypass, replica_groups=groups0, ins=[ag0_in.opt()], outs=[ag0_out.opt()]).then_inc(P.sems["agc0"], 1)
        nc.gpsimd.wait_ge(P.sems["agc0"], 1)

    def rmsnorm_to(vidx, dst_fn, tmpF):
        for c in range(KC):
            P.op("act", lambda e, c=c: e.activation(out=xn[:, c, :], in_=h[:, c, :], func=AF.Square),
                 reads=[h_b[c]], writes=[xn_b[c]])
        rstd = tmpF
        rb_ = Buf()
        for (t0, tn) in TBS:
            bi, bap, bb = bank()

            def mm(e, t0=t0, tn=tn, bap=bap):
                ins = None
                for c in range(KC):
                    ins = e.matmul(bap[:, 0:tn], lhsT=onesb[:], rhs=xn[:, c, t0:t0 + tn], start=(c == 0), stop=(c == KC - 1))
                return ins
            P.op("pe", mm, reads=xn_b + [cst_b], writes=[bb])
            P.op("act", lambda e, t0=t0, tn=tn, bap=bap: e.activation(out=rstd[:, t0:t0 + tn], in_=bap[:, 0:tn], func=AF.Sqrt,
                                                                  scale=1.0 / D, bias=epsS[:, 0:1]),
                 reads=[bb, cst_b], writes=[rb_])
        P.op("dve", lambda e: e.reciprocal(out=rstd[:, :], in_=rstd[:, :]), reads=[rb_], writes=[rb_])
        for c in range(KC):
            o_ap, obufs = dst_fn(c)
            P.op("dve", lambda e, c=c, o_ap=o_ap: e.scalar_tensor_tensor(out=o_ap, in0=h[:, c, :], scalar=vf[:, vidx, c:c + 1],
                                                                        in1=rstd[:, :], op0=ALU.mult, op1=ALU.mult),
                 reads=[h_b[c], rb_, cst_b], writes=obufs)

    def norm_xn(vidx, tmpF):
        rmsnorm_to(vidx, lambda c: (xn[:, c, :], [xn_b[c]]), tmpF)

    def ffn(li, which):
        w_in = w1[which]
        w_out = w2[which]
        vidx = (V_NF1 if which == 0 else V_NF2) + li
        rstd = arF[:, 0:NT]
        sg = [arF[:, NT + 512 * i: NT + 512 * (i + 1)] for i in range(4)]
        sg_b = [Buf() for _ in range(4)]
        hid = arB[:, 0:JQ * NT].rearrange("p (j n) -> p j n", j=JQ)
        hid_b = [Buf() for _ in range(JQ)]
        P.phase()
        norm_xn(vidx, rstd)
        k = 0
        for q in range(NQ):
            for jj in range(JQ):
                j = q * JQ + jj
                wv, wb = wload([(w_in[li, j].rearrange("p (k n) -> p k n", k=KC), 0, 256)], KC, 256)
                for (t0, tn) in TBS:
                    _, gap, gb = bank()
                    _, uap, ub = bank()

                    def mm(e, wv=wv, t0=t0, tn=tn, gap=gap, uap=uap):
                        ins = None
                        for (ap_, off) in ((gap, 0), (uap, 128)):
                            for c in range(KC):
                                ins = e.matmul(ap_[:, 0:tn], lhsT=wv[:, c, off:off + 128], rhs=xn[:, c, t0:t0 + tn],
                                               start=(c == 0), stop=(c == KC - 1))
                        return ins
                    P.op("pe", mm, reads=xn_b + [wb], writes=[gb, ub])
                    s_i = k % 4
                    k += 1
                    P.op("act", lambda e, s_i=s_i, tn=tn, gap=gap: e.activation(out=sg[s_i][:, 0:tn], in_=gap[:, 0:tn], func=AF.Silu),
                         reads=[gb], writes=[sg_b[s_i]])
                    P.op("dve", lambda e, s_i=s_i, jj=jj, t0=t0, tn=tn, uap=uap: e.tensor_tensor(
                        out=hid[:, jj, t0:t0 + tn], in0=sg[s_i][:, 0:tn], in1=uap[:, 0:tn], op=ALU.mult),
                        reads=[sg_b[s_i], ub], writes=[hid_b[jj]])
            for fp in range(8):
                wv, wb = wload([(w_out[li, q, fp].rearrange("p (k n) -> p k n", k=JQ), 0, 256)], JQ, 256)
                for f2 in range(2):
                    fo = fp * 2 + f2
                    for (t0, tn) in TBS:
                        _, oap, ob = bank()

                        def mm(e, wv=wv, f2=f2, t0=t0, tn=tn, oap=oap):
                            ins = None
                            for jj in range(JQ):
                                ins = e.matmul(oap[:, 0:tn], lhsT=wv[:, jj, f2 * 128:(f2 + 1) * 128], rhs=hid[:, jj, t0:t0 + tn],
                                               start=(jj == 0), stop=(jj == JQ - 1))
                            return ins
                        P.op("pe", mm, reads=hid_b + [wb], writes=[ob])
                        P.op("dve", lambda e, fo=fo, t0=t0, tn=tn, oap=oap: e.scalar_tensor_tensor(
                            out=h[:, fo, t0:t0 + tn], in0=oap[:, 0:tn], scalar=0.5, in1=h[:, fo, t0:t0 + tn],
                            op0=ALU.mult, op1=ALU.add), reads=[ob, h_b[fo]], writes=[h_b[fo]])

    def ple(li):
        rstd = arF[:, 0:NT]
        gt = [arF[:, NT + 512 * i: NT + 512 * (i + 1)] for i in range(4)]
        gt_b = [Buf() for _ in range(4)]
        pb = arB[:, 0:2 * NT].rearrange("p (k n) -> p k n", k=2)
        pb_b = Buf()
        P.phase()
        norm_xn(V_NPLE + li, rstd)
        P.dma("pool", "pld", pb, pT[li].rearrange("(k p) n -> p k n", p=128), writes=[pb_b])
        k = 0
        for f8 in range(8):
            wv, wb = wload([(ple_wg[li, :, f8 * 256:(f8 + 1) * 256].rearrange("(k p) n -> p k n", p=128), 0, 256)], KC, 256)
            wpv, wpb = wload([(ple_wp[li, :, f8 * 256:(f8 + 1) * 256].rearrange("(k p) n -> p k n", p=128), 0, 256)], 2, 256)
            for f2 in range(2):
                fo = f8 * 2 + f2
                for (t0, tn) in TBS:
                    _, gap, gb = bank()
                    _, pap, pbk = bank()

                    def mm(e, wv=wv, wpv=wpv, f2=f2, t0=t0, tn=tn, gap=gap, pap=pap):
                        ins = None
                        for c in range(KC):
                            ins = e.matmul(gap[:, 0:tn], lhsT=wv[:, c, f2 * 128:(f2 + 1) * 128], rhs=xn[:, c, t0:t0 + tn],
                                           start=(c == 0), stop=(c == KC - 1))
                        for c in range(2):
                            ins = e.matmul(pap[:, 0:tn], lhsT=wpv[:, c, f2 * 128:(f2 + 1) * 128], rhs=pb[:, c, t0:t0 + tn],
                                           start=(c == 0), stop=(c == 1))
                        return ins
                    P.op("pe", mm, reads=xn_b + [wb, wpb, pb_b], writes=[gb, pbk])
                    s_i = k % 4
                    k += 1
                    P.op("act", lambda e, s_i=s_i, tn=tn, gap=gap: e.activation(out=gt[s_i][:, 0:tn], in_=gap[:, 0:tn], func=AF.Sigmoid),
                         reads=[gb], writes=[gt_b[s_i]])
                    P.op("dve", lambda e, s_i=s_i, tn=tn, pap=pap: e.tensor_tensor(
                        out=gt[s_i][:, 0:tn], in0=gt[s_i][:, 0:tn], in1=pap[:, 0:tn], op=ALU.mult),
                        reads=[gt_b[s_i], pbk], writes=[gt_b[s_i]])
                    P.op("dve", lambda e, s_i=s_i, fo=fo, t0=t0, tn=tn: e.tensor_tensor(
                        out=h[:, fo, t0:t0 + tn], in0=h[:, fo, t0:t0 + tn], in1=gt[s_i][:, 0:tn], op=ALU.add),
                        reads=[gt_b[s_i], h_b[fo]], writes=[h_b[fo]])

    def out_accum(wv, wb, act_ap_fn, act_bufs, tbs):
        for fo in range(KC):
            for (t0, tn) in tbs:
                _, oap, ob = bank()
                P.op("pe", lambda e, fo=fo, t0=t0, tn=tn, oap=oap: e.matmul(
                    oap[:, 0:tn], lhsT=wv[:, 0, fo * 128:(fo + 1) * 128], rhs=act_ap_fn(t0, tn), start=True, stop=True),
                    reads=act_bufs + [wb], writes=[ob])
                P.op("dve", lambda e, fo=fo, t0=t0, tn=tn, oap=oap: e.tensor_tensor(
                    out=h[:, fo, t0:t0 + tn], in0=h[:, fo, t0:t0 + tn], in1=oap[:, 0:tn], op=ALU.add),
                    reads=[ob, h_b[fo]], writes=[h_b[fo]])


    def gmlp():
        AX = mybir.AxisListType
        gbc = arF[:, 0:2048]
        bbc = arF[:, 2048:4096]
        vS = arF[:, 4096:6144]
        vfx = [arF[:, 6144 + 512 * i: 6144 + 512 * (i + 1)] for i in range(2)]
        ugx = [arF[:, 7168 + 512 * i: 7168 + 512 * (i + 1)] for i in range(2)]
        stt = arF[:, 8192:8192 + 512]
        sum1 = stt[:, 0:72].rearrange("p (t c) -> p t c", t=NTILE)
        sum2 = stt[:, 72:144].rearrange("p (t c) -> p t c", t=NTILE)
        sm = stt[:, 144:144 + 8 * NTILE].rearrange("p (t c) -> p t c", t=NTILE)
        junk = arF[:, 8704:8704 + 256]
        bsF = arF[:, 9216:9216 + 192]
        vT = arB[:, 0:10240].rearrange("p (t n) -> p t n", t=5)
        WsM = arB[:, 10240:12288].rearrange("p (g t) -> p g t", g=16)
        WsS = arB[:, 12288:13312].rearrange("p (g t) -> p g t", g=16)
        usx = [arB[:, 13312 + 512 * i: 13312 + 512 * (i + 1)] for i in range(2)]
        bsH = arB[:, 14336:14464]
        bsL = arB[:, 14464:14592]
        bsSH = arB[:, 14592:14656]
        bsSL = arB[:, 14656:14720]
        SEL = arB[:, 14720:16768].rearrange("p (g t) -> p g t", g=16)
        gb_b, vS_b, stt_b, ws_b, bs_b = Buf(), Buf(), Buf(), Buf(), Buf()
        vfx_b = [Buf(), Buf()]
        ugx_b = [Buf(), Buf()]
        usx_b = [Buf(), Buf()]
        junk_b = Buf()
        vT_b = [Buf() for _ in range(5)]
        rstd = arF[:, 4096:4096 + NT]
        P.phase()
        norm_xn(V_NMIX + 0, rstd)
        P.dma("sp", "gml*", gbc, a_gb[:, 0, :], writes=[gb_b])
        P.dma("sp", "gml*", bbc, a_gb[:, 1, :], writes=[gb_b])
        P.dma("sp", "gml*", bsF[0:16, 0:128], a_bs, writes=[bs_b])
        P.dma("sp", "gml*", bsF[0:16, 128:192], a_bsS, writes=[bs_b])
        P.dma("pool", "gmlp*", WsM[:, :, :], a_wsT, writes=[ws_b])
        P.dma("pool", "gmlp*", WsS[0:64, :, :], a_wsS, writes=[ws_b])
        P.dma("pool", "gmlp*", SEL[:, :, :], sel16, writes=[ws_b])
        for g in range(16):
            P.op("dve", lambda e, g=g: e.tensor_tensor(out=WsM[:, g, :], in0=WsM[:, g, :], in1=cmb[:, 5, :], op=ALU.mult),
                 reads=[cst_b], writes=[ws_b])
            P.op("dve", lambda e, g=g: e.tensor_tensor(out=WsS[0:64, g, :], in0=WsS[0:64, g, :], in1=cmb[0:64, 2, 0:64], op=ALU.mult),
                 reads=[cst_b], writes=[ws_b])
        P.op("dve", lambda e: e.memset(arB[:, 14336:14720], 0.0), writes=[ws_b])
        P.op("dve", lambda e: e.tensor_copy(out=bsH[0:16, :], in_=bsF[0:16, 0:128]), reads=[bs_b], writes=[ws_b])
        P.op("dve", lambda e: e.tensor_tensor(out=bsL[0:16, :], in0=bsF[0:16, 0:128], in1=bsH[0:16, :], op=ALU.subtract), reads=[bs_b], writes=[ws_b])
        P.op("dve", lambda e: e.tensor_copy(out=bsSH[0:16, :], in_=bsF[0:16, 128:192]), reads=[bs_b], writes=[ws_b])
        P.op("dve", lambda e: e.tensor_tensor(out=bsSL[0:16, :], in0=bsF[0:16, 128:192], in1=bsSH[0:16, :], op=ALU.subtract), reads=[bs_b], writes=[ws_b])

        for (tiles, tbs) in (([0, 1, 2, 3], [(0, 512)]), ([4, 5, 6, 7, 8], [(512, 512), (1024, 64)])):
            k = 0
            for cb in range(8):
                wv, wb = wload([(a_w_in[:, D + cb * 256: D + (cb + 1) * 256].rearrange("(k p) n -> p k n", p=128), 0, 256)], KC, 256)
                for lt, tt in enumerate(tiles):
                    rows = 64 if tt == 8 else 128
                    _, bap, bb = bank()

                    def mm(e, wv=wv, tt=tt, rows=rows, bap=bap):
                        ins = None
                        for c in range(KC):
                            ins = e.matmul(bap[0:rows, 0:256], lhsT=xn[:, c, tt * 128: tt * 128 + rows], rhs=wv[:, c, 0:256],
                                           start=(c == 0), stop=(c == KC - 1))
                        return ins
                    P.op("pe", mm, reads=xn_b + [wb], writes=[bb])
                    i = k % 2
                    k += 1
                    P.op("act", lambda e, i=i, rows=rows, bap=bap, tt=tt, cb=cb: e.activation(
                        out=vfx[i][0:rows, 0:256], in_=bap[0:rows, 0:256], func=AF.Gelu, accum_out=sum1[0:rows, tt, cb:cb + 1]),
                        reads=[bb], writes=[vfx_b[i], stt_b])
                    P.op("act", lambda e, i=i, rows=rows, tt=tt, cb=cb: e.activation(
                        out=junk[0:rows, 0:256], in_=vfx[i][0:rows, 0:256], func=AF.Square, accum_out=sum2[0:rows, tt, cb:cb + 1]),
                        reads=[vfx_b[i]], writes=[junk_b, stt_b])
                    if tt == 8:
                        P.op("dve", lambda e, i=i, cb=cb: e.tensor_copy(out=vS[0:64, cb * 256:(cb + 1) * 256], in_=vfx[i][0:64, 0:256]),
                             reads=[vfx_b[i]], writes=[vS_b])
                    else:
                        P.op("dve", lambda e, i=i, lt=lt, cb=cb: e.tensor_copy(out=vT[:, lt, cb * 256:(cb + 1) * 256], in_=vfx[i][:, 0:256]),
                             reads=[vfx_b[i]], writes=[vT_b[lt]])
            for lt, tt in sorted(enumerate(tiles), key=lambda x: -x[1]):
                rows = 64 if tt == 8 else 128
                S = lambda j, tt=tt, rows=rows: sm[0:rows, tt, j:j + 1]
                P.op("dve", lambda e, tt=tt, rows=rows, S=S: e.tensor_reduce(out=S(0), in_=sum1[0:rows, tt, :], axis=AX.X, op=ALU.add),
                     reads=[stt_b], writes=[stt_b])
                P.op("dve", lambda e, tt=tt, rows=rows, S=S: e.tensor_reduce(out=S(1), in_=sum2[0:rows, tt, :], axis=AX.X, op=ALU.add),
                     reads=[stt_b], writes=[stt_b])
                P.op("dve", lambda e, S=S: e.tensor_scalar(out=S(2), in0=S(0), scalar1=1.0 / D, scalar2=None, op0=ALU.mult), reads=[stt_b], writes=[stt_b])
                P.op("dve", lambda e, S=S: e.tensor_tensor(out=S(3), in0=S(2), in1=S(2), op=ALU.mult), reads=[stt_b], writes=[stt_b])
                P.op("dve", lambda e, S=S: e.scalar_tensor_tensor(out=S(4), in0=S(1), scalar=1.0 / D, in1=S(3), op0=ALU.mult, op1=ALU.subtract),
                     reads=[stt_b], writes=[stt_b])
                P.op("act", lambda e, S=S, rows=rows: e.activation(out=S(5), in_=S(4), func=AF.Sqrt, bias=epsS[0:rows, 0:1]),
                     reads=[stt_b, cst_b], writes=[stt_b])
                P.op("dve", lambda e, S=S: e.reciprocal(out=S(5), in_=S(5)), reads=[stt_b], writes=[stt_b])
                P.op("dve", lambda e, S=S: e.scalar_tensor_tensor(out=S(6), in0=S(2), scalar=-1.0, in1=S(5), op0=ALU.mult, op1=ALU.mult),
                     reads=[stt_b], writes=[stt_b])
                if tt == 8:
                    P.op("act", lambda e, S=S: e.activation(out=vS[0:64, :], in_=vS[0:64, :], func=AF.Identity, scale=S(5), bias=S(6)),
                         reads=[stt_b], writes=[vS_b])
                    P.op("dve", lambda e: e.tensor_tensor(out=vS[0:64, :], in0=vS[0:64, :], in1=gbc[0:64, :], op=ALU.mult), reads=[gb_b], writes=[vS_b])
                    P.op("dve", lambda e: e.tensor_tensor(out=vS[0:64, :], in0=vS[0:64, :], in1=bbc[0:64, :], op=ALU.add), reads=[gb_b], writes=[vS_b])
                    P.op("dve", lambda e, lt=lt: e.tensor_copy(out=vT[0:64, lt, :], in_=vS[0:64, :]), reads=[vS_b], writes=[vT_b[lt]])
                    P.dma("sp", "o_vs", vs_out, vS[0:64, :], reads=[vS_b])
                else:
                    P.op("act", lambda e, S=S, lt=lt: e.activation(out=vS[:, :], in_=vT[:, lt, :], func=AF.Identity, scale=S(5), bias=S(6)),
                         reads=[stt_b, vT_b[lt]], writes=[vS_b])
                    P.op("dve", lambda e: e.tensor_tensor(out=vS[:, :], in0=vS[:, :], in1=gbc, op=ALU.mult), reads=[gb_b], writes=[vS_b])
                    P.op("dve", lambda e, lt=lt: e.tensor_tensor(out=vT[:, lt, :], in0=vS[:, :], in1=bbc, op=ALU.add), reads=[gb_b, vS_b], writes=[vT_b[lt]])
            k = 0
            for gp in range(8):
                wv, wb = wload([(a_w_in[:, gp * 256:(gp + 1) * 256].rearrange("(k p) n -> p k n", p=128), 0, 256)], KC, 256)
                for g2 in range(2):
                    g = gp * 2 + g2
                    wo, wob = wload([(a_w_out[g * 128:(g + 1) * 128, :].rearrange("(k p) n -> p k n", p=128), 0, D)], 1, D)
                    for (t0, tn) in tbs:
                        _, uap, ub = bank()
                        _, sap, sbk = bank()

                        def mm(e, wv=wv, g2=g2, g=g, t0=t0, tn=tn, uap=uap, sap=sap, tiles=tiles):
                            ins = None
                            for c in range(KC):
                                ins = e.matmul(uap[:, 0:tn], lhsT=wv[:, c, g2 * 128:(g2 + 1) * 128], rhs=xn[:, c, t0:t0 + tn],
                                               start=(c == 0), stop=(c == KC - 1))
                            if tn == 64:
                                lt = tiles.index(8)
                                ins = e.matmul(sap[:, 0:64], lhsT=vT[0:64, lt, g * 128:(g + 1) * 128], rhs=WsS[0:64, g, :], start=True, stop=_DBG.get("nobias", False))
                                if not _DBG.get("nobias"):
                                    e.matmul(sap[:, 0:64], lhsT=SEL[:, g, :], rhs=bsSH[:, :], start=False, stop=False)
                                    ins = e.matmul(sap[:, 0:64], lhsT=SEL[:, g, :], rhs=bsSL[:, :], start=False, stop=True)
                            else:
                                for tt in range(t0 // 128, (t0 + tn) // 128):
                                    lt = tiles.index(tt)
                                    o = tt * 128 - t0
                                    ins = e.matmul(sap[:, o:o + 128], lhsT=vT[:, lt, g * 128:(g + 1) * 128], rhs=WsM[:, g, :], start=True, stop=_DBG.get("nobias", False))
                                    if not _DBG.get("nobias"):
                                        e.matmul(sap[:, o:o + 128], lhsT=SEL[:, g, :], rhs=bsH[:, :], start=False, stop=False)
                                        ins = e.matmul(sap[:, o:o + 128], lhsT=SEL[:, g, :], rhs=bsL[:, :], start=False, stop=True)
                            return ins
                        P.op("pe", mm, reads=xn_b + vT_b + [wb, ws_b], writes=[ub, sbk])
                        i = k % 2
                        k += 1
                        P.op("act", lambda e, i=i, tn=tn, uap=uap: e.activation(out=ugx[i][:, 0:tn], in_=uap[:, 0:tn], func=AF.Gelu),
                             reads=[ub], writes=[ugx_b[i]])
                        P.op("dve", lambda e, i=i, tn=tn, sap=sap: e.tensor_tensor(out=usx[i][:, 0:tn], in0=ugx[i][:, 0:tn], in1=sap[:, 0:tn], op=ALU.mult),
                             reads=[ugx_b[i], sbk], writes=[usx_b[i]])
                        out_accum(wo, wob, lambda a, n, i=i: usx[i][:, 0:n], [usx_b[i]], [(t0, tn)])


    def hgrn():
        Sst = arF[:, 0:2048].rearrange("p (h v) -> p h v", h=16)
        S0 = arF[:, 2048:4096].rearrange("p (s v) -> p s v", s=16)
        qs = arF[:, 4096:4608]
        kF = arF[:, 4608:5120]
        Ef = arF[:, 5120:5632]
        tF = arF[:, 5632:6144]
        rst = arF[:, 6144:6656]
        logfT = arF[:, 6656:7808].rearrange("p (t k) -> p t k", t=NTILE)
        t1 = arF[:, 7808:7936]
        kT = arF[:, 8064:8192]
        edT = arF[:, 8192:8320]
        lbh = arF[:, 8320:8448]
        omlh = arF[:, 8448:8576]
        l01 = arF[:, 8576:8832]
        Dl = arF[:, 8832:8864]
        omlFM = arF[:, 8864:8880]
        qt = arB[:, 0:1088]
        kt = arB[:, 1088:2176]
        sg = arB[:, 2176:3264]
        osq = arB[:, 3264:4352]
        ot = arB[:, 4352:5440]
        kdec = arB[:, 5440:6592].rearrange("p (t k) -> p t k", t=NTILE)
        kdecB = arB[:, 13248:14400].rearrange("p (t k) -> p t k", t=NTILE)
        vT = arB[:, 6592:7744].rearrange("p (t k) -> p t k", t=NTILE)
        attm = arB[:, 7744:8896].rearrange("p (t k) -> p t k", t=NTILE)
        Sbf = arB[:, 8896:10944].rearrange("p (c v) -> p c v", c=16)
        S0bf = arB[:, 10944:12992].rearrange("p (s v) -> p s v", s=16)
        vmk = [arB[:, 12992 + 128 * i: 12992 + 128 * (i + 1)] for i in range(2)]
        B = {k: Buf() for k in ("Sst", "S0", "qs", "kF", "Ef", "tF", "rst", "logf", "t1", "kT", "edT", "lb", "l01", "Dl", "oml",
                                "qt", "kt", "sg", "osq", "ot", "kdec", "vT", "attm", "Sbf", "S0bf", "vmk0", "vmk1")}
        rstd = arF[:, 4096:4096 + NT]
        P.phase()
        norm_xn(V_NMIX + 1, rstd)
        P.op("dve", lambda e: e.tensor_tensor(out=omlFM, in0=vf[:, V_L0, :], in1=vf[:, V_L1, :], op=ALU.subtract), reads=[cst_b], writes=[B["oml"]])
        P.op("act", lambda e: e.activation(out=omlFM, in_=omlFM, func=AF.Sigmoid), writes=[B["oml"]])
        P.op("dve", lambda e: e.memset(arF[:, 0:2048], 0.0), writes=[B["Sst"]])
        P.op("dve", lambda e: e.memset(arB[:, 0:14400], 0.0), writes=[B[k_] for k_ in ("kdec", "vT", "attm", "vmk0", "vmk1", "Sbf", "S0bf", "qt", "kt")])

        def head(hh, full):
            c0 = hh * 128
            if full:
                wA, wAb = wload([(b_w_in[:, c0:c0 + 128].rearrange("(k p) n -> p k n", p=128), 0, 128),
                                 (b_w_in[:, 3 * D + c0:3 * D + c0 + 128].rearrange("(k p) n -> p k n", p=128), 128, 128)], KC, 256)
            wB, wBb = wload([(b_w_in[:, D + c0:D + c0 + 128].rearrange("(k p) n -> p k n", p=128), 0, 128),
                             (b_w_in[:, 2 * D + c0:2 * D + c0 + 128].rearrange("(k p) n -> p k n", p=128), 128, 128)], KC, 256)
            if full:
                wO, wOb = wload([(b_w_out[c0:c0 + 128, :].rearrange("(k p) n -> p k n", p=128), 0, D)], 1, D)
            P.dma("sp", "lbA", l01[:, 0:128], lb_bc[:, 0, c0:c0 + 128], writes=[B["l01"]])
            P.dma("sp", "lbB", l01[:, 128:256], lb_bc[:, 1, c0:c0 + 128], writes=[B["l01"]])
            P.op("dve", lambda e: e.tensor_tensor(out=lbh, in0=l01[:, 128:256], in1=l01[:, 0:128], op=ALU.subtract), reads=[B["l01"]], writes=[B["lb"]])
            P.op("act", lambda e: e.activation(out=lbh, in_=lbh, func=AF.Sigmoid), writes=[B["lb"]])
            P.op("dve", lambda e: e.tensor_scalar(out=omlh, in0=lbh, scalar1=-1.0, scalar2=1.0, op0=ALU.mult, op1=ALU.add), reads=[B["lb"]], writes=[B["lb"]])
            tiles = list(range(NTILE)) if full else list(range(8))
            _, blap, blb = aux_bank()
            for tt in tiles:
                rows = 64 if tt == 8 else 128
                _, bap, bb = bank()

                def mm(e, tt=tt, rows=rows, bap=bap):
                    ins = None
                    for c in range(KC):
                        ins = e.matmul(bap[0:rows, 0:256], lhsT=xn[:, c, tt * 128: tt * 128 + rows], rhs=wB[:, c, 0:256],
                                       start=(c == 0), stop=(c == KC - 1))
                    return ins
                P.op("pe", mm, reads=xn_b + [wBb], writes=[bb])
                P.op("act", lambda e, rows=rows, bap=bap: e.activation(out=t1[0:rows, :], in_=bap[0:rows, 0:128], func=AF.Sigmoid), reads=[bb], writes=[B["t1"]])
                P.op("act", lambda e, rows=rows, bap=bap, tt=tt: e.activation(out=vT[0:rows, tt, :], in_=bap[0:rows, 128:256], func=AF.Copy), reads=[bb], writes=[B["vT"]])
                P.op("dve", lambda e, rows=rows: e.tensor_tensor(out=t1[0:rows, :], in0=t1[0:rows, :], in1=omlh[0:rows, :], op=ALU.mult), reads=[B["lb"]], writes=[B["t1"]])
                P.op("dve", lambda e, rows=rows: e.tensor_tensor(out=t1[0:rows, :], in0=t1[0:rows, :], in1=lbh[0:rows, :], op=ALU.add), reads=[B["lb"]], writes=[B["t1"]])
                P.op("act", lambda e, rows=rows, tt=tt: e.activation(out=logfT[0:rows, tt, :], in_=t1[0:rows, :], func=AF.Ln), reads=[B["t1"]], writes=[B["logf"]])
                P.op("dve", lambda e, rows=rows: e.tensor_scalar(out=kT[0:rows, :], in0=t1[0:rows, :], scalar1=-1.0, scalar2=1.0, op0=ALU.mult, op1=ALU.add),
                     reads=[B["t1"]], writes=[B["kT"]])
                _, dap, db = bank()
                mk = RM4 if tt == 8 else RM
                P.op("pe", lambda e, rows=rows, tt=tt, dap=dap, mk=mk: e.matmul(dap[0:rows, 0:128], lhsT=cm[0:rows, mk, 0:rows], rhs=logfT[0:rows, tt, :],
                                                                          start=True, stop=True), reads=[B["logf"], cst_b], writes=[db])
                P.op("act", lambda e, rows=rows, dap=dap: e.activation(out=edT[0:rows, :], in_=dap[0:rows, 0:128], func=AF.Exp), reads=[db], writes=[B["edT"]])
                P.op("dve", lambda e, rows=rows, tt=tt: e.scalar_tensor_tensor(out=kdec[0:rows, tt, :], in0=kT[0:rows, :], scalar=cm[0:rows, 6, 0:1], in1=edT[0:rows, :],
                                                                           op0=ALU.mult, op1=ALU.mult), reads=[B["kT"], B["edT"], cst_b], writes=[B["kdec"]])
                if tt < 8:
                    P.op("dve", lambda e, tt=tt: e.scalar_tensor_tensor(out=kdecB[:, tt, :], in0=kT[:, :], scalar=cm[:, 6, 1:2], in1=edT[:, :],
                                                                   op0=ALU.mult, op1=ALU.mult), reads=[B["kT"], B["edT"], cst_b], writes=[B["kdec"]])
                if tt == 8:
                    P.op("pe", lambda e, blap=blap: e.matmul(blap[:, 16:32], lhsT=logfT[0:64, 8, :], rhs=cm[0:64, TRI4, 3:64:4], start=True, stop=True),
                         reads=[B["logf"], cst_b], writes=[blb])
                else:
                    P.op("pe", lambda e, tt=tt, blap=blap: e.matmul(blap[:, 2 * tt:2 * tt + 2], lhsT=logfT[:, tt, :], rhs=cm[:, TRI, 63:128:64], start=True, stop=True),
                         reads=[B["logf"], cst_b], writes=[blb])
            nb = 32 if full else 16
            P.op("act", lambda e, blap=blap, nb=nb: e.activation(out=Dl[:, 0:nb], in_=blap[:, 0:nb], func=AF.Exp), reads=[blb], writes=[B["Dl"]])
            for c4 in range(4):
                _, uap, ub = bank()

                def mmu(e, c4=c4, uap=uap):
                    ins = None
                    for ci in range(4):
                        c = c4 * 4 + ci
                        tt, hf = c // 2, c % 2
                        ins = e.matmul(uap[:, ci * 128:(ci + 1) * 128], lhsT=(kdecB if hf else kdec)[:, tt, :], rhs=vT[:, tt, :],
                                       start=True, stop=True)
                    return ins
                P.op("pe", mmu, reads=[B["kdec"], B["vT"]], writes=[ub])
                for ci in range(4):
                    c = c4 * 4 + ci
                    if full:
                        P.op("dve", lambda e, c=c: e.tensor_copy(out=Sbf[:, c, :], in_=Sst[:, hh, :]), reads=[B["Sst"]], writes=[B["Sbf"]])
                    P.op("dve", lambda e, c=c, ci=ci, uap=uap: e.scalar_tensor_tensor(out=Sst[:, hh, :], in0=Sst[:, hh, :], scalar=Dl[:, c:c + 1],
                                                                                  in1=uap[:, ci * 128:(ci + 1) * 128], op0=ALU.mult, op1=ALU.add),
                         reads=[B["Dl"], ub], writes=[B["Sst"]])
            if not full:
                return
            P.dma("sp", "s0ld", S0[:, :, :], s0[:, hh].rearrange("s k v -> k s v"), writes=[B["S0"]])
            P.op("act", lambda e: e.activation(out=S0bf[:, :, :], in_=S0[:, :, :], func=AF.Copy), reads=[B["S0"]], writes=[B["S0bf"]])
            for j4 in range(4):
                _, uap, ub = bank()
                for ji in range(4):
                    j = j4 * 4 + ji
                    i = j % 2
                    P.op("dve", lambda e, i=i, j=j: e.tensor_scalar(out=vmk[i][0:64, :], in0=vT[0:64, 8, :], scalar1=cm[0:64, OH, j:j + 1], scalar2=None, op0=ALU.mult),
                         reads=[B["vT"], cst_b], writes=[B[f"vmk{i}"]])
                    P.op("pe", lambda e, i=i, ji=ji, uap=uap: e.matmul(uap[:, ji * 128:(ji + 1) * 128], lhsT=kdec[:, 8, :], rhs=vmk[i][:, :], start=True, stop=True),
                         reads=[B["kdec"], B[f"vmk{i}"]], writes=[ub])
                for ji in range(4):
                    j = j4 * 4 + ji
                    P.op("dve", lambda e, j=j, ji=ji, uap=uap: e.scalar_tensor_tensor(out=S0[:, j, :], in0=S0[:, j, :], scalar=Dl[:, 16 + j:17 + j],
                                                                                  in1=uap[:, ji * 128:(ji + 1) * 128], op0=ALU.mult, op1=ALU.add),
                         reads=[B["Dl"], ub, B["S0bf"]], writes=[B["S0"]])
            P.dma("sp", "o_ss", ss_out[:, hh].rearrange("s k v -> k s v"), S0[:, :, :], reads=[B["S0"]])
            for (t0, tn) in TBS:
                _, qap, qb = bank()
                _, fap, fb = bank()
                _, gap, gb = bank()

                def mmf(e, t0=t0, tn=tn, qap=qap, fap=fap, gap=gap):
                    ins = None
                    for (ap_, wv_, off) in ((qap, wA, 0), (fap, wB, 0), (gap, wA, 128)):
                        for c in range(KC):
                            ins = e.matmul(ap_[:, 0:tn], lhsT=wv_[:, c, off:off + 128], rhs=xn[:, c, t0:t0 + tn], start=(c == 0), stop=(c == KC - 1))
                    return ins
                P.op("pe", mmf, reads=xn_b + [wAb, wBb], writes=[qb, fb, gb])
                P.op("act", lambda e, tn=tn, qap=qap: e.activation(out=qs[:, 0:tn], in_=qap[:, 0:tn], func=AF.Silu), reads=[qb], writes=[B["qs"]])
                P.op("act", lambda e, t0=t0, tn=tn, gap=gap: e.activation(out=sg[:, t0:t0 + tn], in_=gap[:, 0:tn], func=AF.Silu), reads=[gb], writes=[B["sg"]])
                P.op("act", lambda e, tn=tn, fap=fap: e.activation(out=kF[:, 0:tn], in_=fap[:, 0:tn], func=AF.Sigmoid, scale=-1.0), reads=[fb], writes=[B["kF"]])
                P.op("dve", lambda e, tn=tn: e.tensor_scalar(out=kF[:, 0:tn], in0=kF[:, 0:tn], scalar1=omlFM[:, hh:hh + 1], scalar2=None, op0=ALU.mult),
                     reads=[B["oml"]], writes=[B["kF"]])
                _, bap, bb = bank()

                def mmb(e, t0=t0, tn=tn, bap=bap):
                    ins = None
                    if tn == 64:
                        ins = e.matmul(bap[:, 0:64], lhsT=logfT[0:64, 8, :], rhs=cm[0:64, TRI4, 0:64], start=True, stop=True)
                    else:
                        for tt in range(t0 // 128, (t0 + tn) // 128):
                            o = tt * 128 - t0
                            ins = e.matmul(bap[:, o:o + 128], lhsT=logfT[:, tt, :], rhs=cm[:, TRI, :], start=True, stop=True)
                    return ins
                P.op("pe", mmb, reads=[B["logf"], cst_b], writes=[bb])
                P.op("act", lambda e, tn=tn, bap=bap: e.activation(out=Ef[:, 0:tn], in_=bap[:, 0:tn], func=AF.Exp), reads=[bb], writes=[B["Ef"]])
                P.op("act", lambda e, tn=tn, bap=bap: e.activation(out=tF[:, 0:tn], in_=bap[:, 0:tn], func=AF.Exp, scale=-1.0), reads=[bb], writes=[B["tF"]])
                P.op("dve", lambda e, t0=t0, tn=tn: e.tensor_tensor(out=qt[:, t0:t0 + tn], in0=qs[:, 0:tn], in1=Ef[:, 0:tn], op=ALU.mult),
                     reads=[B["qs"], B["Ef"]], writes=[B["qt"]])
                P.op("dve", lambda e, t0=t0, tn=tn: e.tensor_tensor(out=kt[:, t0:t0 + tn], in0=kF[:, 0:tn], in1=tF[:, 0:tn], op=ALU.mult),
                     reads=[B["kF"], B["tF"]], writes=[B["kt"]])
                _, aap, ab = bank()

                def mma(e, t0=t0, tn=tn, aap=aap):
                    ins = None
                    if tn == 64:
                        ins = e.matmul(aap[0:64, 0:64], lhsT=kt[:, 1024:1088], rhs=qt[:, 1024:1088], start=True, stop=True)
                    else:
                        for tt in range(t0 // 128, (t0 + tn) // 128):
                            o = tt * 128 - t0
                            ins = e.matmul(aap[:, o:o + 128], lhsT=kt[:, tt * 128:(tt + 1) * 128], rhs=qt[:, tt * 128:(tt + 1) * 128], start=True, stop=True)
                    return ins
                P.op("pe", mma, reads=[B["kt"], B["qt"]], writes=[ab])
                if tn == 64:
                    P.op("dve", lambda e, aap=aap: e.tensor_tensor(out=attm[0:64, 8, 0:64], in0=aap[0:64, 0:64], in1=cm[0:64, TRI4, 0:64], op=ALU.mult),
                         reads=[ab, cst_b], writes=[B["attm"]])
                else:
                    for tt in range(t0 // 128, (t0 + tn) // 128):
                        o = tt * 128 - t0
                        P.op("dve", lambda e, tt=tt, o=o, aap=aap: e.tensor_tensor(out=attm[:, tt, :], in0=aap[:, o:o + 128], in1=cm[:, TRI, :], op=ALU.mult),
                             reads=[ab, cst_b], writes=[B["attm"]])
                _, oap, ob = bank()

                def mmo(e, t0=t0, tn=tn, oap=oap):
                    ins = None
                    if tn == 64:
                        e.matmul(oap[:, 0:64], lhsT=vT[:, 8, :], rhs=attm[:, 8, 0:64], start=True, stop=False)
                        for j in range(16):
                            ins = e.matmul(oap[:, 4 * j:4 * j + 4], lhsT=S0bf[:, j, :], rhs=qt[:, 1024 + 4 * j:1028 + 4 * j], start=False, stop=(j == 15))
                    else:
                        for tt in range(t0 // 128, (t0 + tn) // 128):
                            o = tt * 128 - t0
                            e.matmul(oap[:, o:o + 128], lhsT=vT[:, tt, :], rhs=attm[:, tt, :], start=True, stop=False)
                            for hf in range(2):
                                c = tt * 2 + hf
                                ins = e.matmul(oap[:, o + hf * 64:o + hf * 64 + 64], lhsT=Sbf[:, c, :], rhs=qt[:, tt * 128 + hf * 64:tt * 128 + hf * 64 + 64],
                                               start=False, stop=(hf == 1))
                    return ins
                P.op("pe", mmo, reads=[B["vT"], B["attm"], B["Sbf"], B["S0bf"], B["qt"]], writes=[ob])
                P.op("act", lambda e, t0=t0, tn=tn, oap=oap: e.activation(out=osq[:, t0:t0 + tn], in_=oap[:, 0:tn], func=AF.Square), reads=[ob], writes=[B["osq"]])
                _, sap, sbk = bank()
                P.op("pe", lambda e, t0=t0, tn=tn, sap=sap: e.matmul(sap[:, 0:tn], lhsT=onesb[:], rhs=osq[:, t0:t0 + tn], start=True, stop=True),
                     reads=[B["osq"], cst_b], writes=[sbk])
                P.op("act", lambda e, tn=tn, sap=sap: e.activation(out=rst[:, 0:tn], in_=sap[:, 0:tn], func=AF.Sqrt, scale=1.0 / 128, bias=epsS[:, 0:1]),
                     reads=[sbk, cst_b], writes=[B["rst"]])
                P.op("dve", lambda e, tn=tn: e.reciprocal(out=rst[:, 0:tn], in_=rst[:, 0:tn]), writes=[B["rst"]])
                P.op("dve", lambda e, tn=tn, oap=oap: e.scalar_tensor_tensor(out=tF[:, 0:tn], in0=oap[:, 0:tn], scalar=vf[:, V_BNG, hh:hh + 1], in1=rst[:, 0:tn],
                                                                          op0=ALU.mult, op1=ALU.mult), reads=[ob, B["rst"], cst_b], writes=[B["tF"]])
                P.op("dve", lambda e, t0=t0, tn=tn: e.tensor_tensor(out=ot[:, t0:t0 + tn], in0=tF[:, 0:tn], in1=sg[:, t0:t0 + tn], op=ALU.mult),
                     reads=[B["tF"], B["sg"]], writes=[B["ot"]])
                out_accum(wO, wOb, lambda a, n: ot[:, a:a + n], [B["ot"]], [(t0, tn)])

        groups = None
        if mode == "B":
            P.dma("sp", "sin", Sst[:, :, :], s_in.rearrange("h k v -> k h v"), writes=[B["Sst"]])
        if mode != "B" and _DBG.get("xcore", True):
            for hh in range(16):
                head(hh, False)
            ncr = _DBG.get("ncores", NCORES)
            groups = [[2 * i, 2 * i + 1] for i in range(ncr // 2)]
            if _DBG.get("nocc"):
                P.op("dve", lambda e: e.memset(arF[:, 0:2048], 0.0), writes=[B["Sst"]])
                groups = None
            if mode == "A":
                P.dma("sp", "o_sp", sp_out.rearrange("h k v -> k h v"), Sst[:, :, :], reads=[B["Sst"]])
                return
        if mode == "fused" and _DBG.get("xcore", True) and groups is not None:
            P.dma("pool", "agi", ag_in, arF[:, 0:2048], reads=[B["Sst"]])
            ag_b = Buf()
            t_in = B["Sst"].r["agi"]
            P.chan("agc")
            P.wait("pool", t_in)
            nc.gpsimd.collective_compute("AllGather", ALU.bypass, replica_groups=groups, ins=[ag_in.opt()], outs=[ag_out.opt()]).then_inc(P.sems["agc"], 1)
            ag_b.w = ("agc", 1)
            if _DBG.get("cconly"):
                P.wait("pool", ("agc", 1))
                P.wait("dve", ("agc", 1))
                P.op("dve", lambda e: e.memset(arF[:, 0:2048], 0.0), writes=[B["Sst"]])
            else:
                P.dma("pool", "ago", arF[:, 0:2048], ag_out[0:128, :], reads=[ag_b], writes=[B["Sst"]])
            if _DBG.get("agoonly"):
                P.op("dve", lambda e: e.memset(arF[:, 0:2048], 0.0), writes=[B["Sst"]])
            elif not _DBG.get("cconly"):
                P.op("dve", lambda e: e.tensor_scalar(out=arF[:, 0:2048], in0=arF[:, 0:2048], scalar1=parS[:, 0:1], scalar2=None, op0=ALU.mult),
                     reads=[cst_b], writes=[B["Sst"]])
        for hh in range(16):
            head(hh, True)
        P.dma("sp", "o_sp", sp_out.rearrange("h k v -> k h v"), Sst[:, :, :], reads=[B["Sst"]])

    mixers = _DBG.get("mixers", True)
    if mode == "A":
        stop = "A"
    for li in range(2):
        if mode == "B" and li == 0:
            continue
        if mode != "B":
            ffn(li, 0)
        if stop == ("ffn1", li):
            break
        if mixers:
            if li == 0:
                gmlp()
            else:
                hgrn()
        if stop == ("mix", li) or (mode == "A" and li == 1):
            break
        ffn(li, 1)
        if stop == ("ffn2", li):
            break
        ple(li)
        if stop == ("ple", li):
            break

    P.phase()
    if stop is None:
        rmsnorm_to(V_FIN, lambda c: (h[:, c, :], [h_b[c]]), arF[:, 0:NT])
    P.dma("sp", "fin", yT.rearrange("(c p) n -> p c n", p=128), h[:], reads=h_b)
    for ch in [c for c in P.cnt if c.startswith("fin") or c.startswith("o_")]:
        if P.cnt[ch] > 0:
            nc.sync.wait_ge(P.sems[ch], P.cnt[ch])


def _masks():
    m = np.zeros((128, 8, 128), np.float32)
    s = np.arange(128)[:, None]
    t = np.arange(128)[None, :]
    same64 = (s // 64) == (t // 64)
    m[:, 0, :] = (same64 & (s <= t))
    m[:, 1, :] = (same64 & (s > t))
    same4 = (s // 4) == (t // 4)
    m[:, 2, :] = (same4 & (s <= t))
    m[:, 3, :] = (same4 & (s > t))
    m[:, 4, 0:16] = (s // 4 == np.arange(16)[None, :]) & (s < 64)
    m[:, 5, :] = (s <= t)
    m[:, 6, 0] = (np.arange(128) < 64)
    m[:, 6, 1] = (np.arange(128) >= 64)
    return m


def _prep_shared(inp):
    sh = {}
    for which, (kin, kout) in enumerate((("ffn1_w_in", "ffn1_w_out"), ("ffn2_w_in", "ffn2_w_out"))):
        wi = np.asarray(inp[kin])
        g = wi[:, :, :DFF].reshape(2, KC, 128, NJ, 128)
        u = wi[:, :, DFF:].reshape(2, KC, 128, NJ, 128)
        blk = np.stack([g, u], axis=4)
        blk = blk.transpose(0, 3, 2, 1, 4, 5)
        sh[f"w1_{which}"] = np.ascontiguousarray(blk).reshape(2, NJ, 128, KC * 256)
        wo = np.asarray(inp[kout])
        b = wo.reshape(2, NQ, JQ, 128, 8, 256).transpose(0, 1, 4, 3, 2, 5)
        sh[f"w2_{which}"] = np.ascontiguousarray(b).reshape(2, NQ, 8, 128, JQ * 256)
    sh["a_w_in"] = np.ascontiguousarray(inp["a_w_in"][0])
    sh["a_w_out"] = np.ascontiguousarray(inp["a_w_out"][0])
    sh["b_w_in"] = np.ascontiguousarray(inp["b_w_in"][0])
    sh["b_w_out"] = np.ascontiguousarray(inp["b_w_out"][0])
    sh["ple_wp"] = np.ascontiguousarray(inp["ple_w_proj"])
    sh["ple_wg"] = np.ascontiguousarray(inp["ple_w_gate"])
    vecs = [inp["norm_ffn1"][0], inp["norm_ffn1"][1], inp["norm_mix"][0], inp["norm_mix"][1],
            inp["norm_ffn2"][0], inp["norm_ffn2"][1], inp["norm_ple"][0], inp["norm_ple"][1],
            inp["final_norm"], inp["b_norm_g"][0], inp["b_lb_logits"][0], inp["b_lb_logits"][1]]
    v = np.stack([np.asarray(x, np.float32).reshape(KC, 128).T for x in vecs], axis=1)
    sh["vfm"] = np.ascontiguousarray(v)
    sh["cmask"] = _masks()
    gb = np.stack([np.asarray(inp["a_ln_g"][0]), np.asarray(inp["a_ln_b"][0])], 0)
    sh["a_gb"] = np.ascontiguousarray(np.broadcast_to(gb[None], (128, 2, D))).astype(np.float32)
    ws = np.asarray(inp["a_w_s"][0])
    sh["a_wsT"] = np.ascontiguousarray(ws.transpose(2, 0, 1))
    i4 = np.arange(64) % 4
    sh["a_wsS"] = np.ascontiguousarray(ws[:, i4[None, :], i4[:, None]].transpose(1, 0, 2))
    bs = np.asarray(inp["a_b_s"][0])
    sh["a_bs"] = np.ascontiguousarray(bs)
    sh["a_bsS"] = np.ascontiguousarray(bs[:, i4])
    sel = np.zeros((128, 16, 128), np.float32)
    for g_ in range(16):
        sel[g_, g_, :] = 1.0
    sh["sel16"] = sel
    lb = np.asarray(inp["b_lb_logits"])
    sh["lb_bc"] = np.ascontiguousarray(np.broadcast_to(lb[None], (128, 2, D))).astype(np.float32)
    return sh


def _prep_core(inp, c):
    s, half = c // 2, c % 2
    xp = np.asarray(inp["x_prompt"])[s, half * NPR:(half + 1) * NPR]
    xs = np.asarray(inp["x_sample"])[16 * c:16 * c + 16].reshape(NSM, D)
    m = {"xT": np.ascontiguousarray(np.concatenate([xp, xs], 0).T)}
    pp = np.asarray(inp["p_prompt"])[:, s, half * NPR:(half + 1) * NPR]
    psm = np.asarray(inp["p_sample"])[:, 16 * c:16 * c + 16].reshape(2, NSM, DPLE)
    m["pT"] = np.ascontiguousarray(np.concatenate([pp, psm], 1).transpose(0, 2, 1))
    m["s0"] = np.ascontiguousarray(np.asarray(inp["state_hgrn"])[0, 16 * c:16 * c + 16])
    m["par"] = np.full((128, 8), float(half), np.float32)
    return m


_CACHE = {}


def kernel(**inputs):
    if "nc" not in _CACHE:
        if TWO_LAUNCH:
            _CACHE["two"] = True
            _CACHE["nc"] = build_program("A")
            _CACHE["ncB"] = build_program("B")
        else:
            _CACHE["nc"] = build_program()
    nc = _CACHE["nc"]
    sh = _prep_shared(inputs)
    names = set()
    in_maps = []
    for c in range(NCORES):
        m = dict(sh)
        m.update(_prep_core(inputs, c))
        in_maps.append(m)
    res = run_bass_kernel_spmd(nc, in_maps, core_ids=list(range(NCORES)))
    R = res.results
    if _CACHE.get("two"):
        RA = R
        ncB = _CACHE["ncB"]
        for c in range(NCORES):
            in_maps[c]["xT"] = np.ascontiguousarray(RA[c]["yT"])
            in_maps[c]["s_in"] = np.ascontiguousarray(RA[c - 1]["sp_out"]) if c % 2 == 1 else np.zeros((16, 128, 128), np.float32)
        R = run_bass_kernel_spmd(ncB, in_maps, core_ids=list(range(NCORES))).results
        for c in range(NCORES):
            R[c]["vs_out"] = RA[c]["vs_out"]
    y_prompt = np.zeros((4, 2048, D), np.float32)
    y_sample = np.zeros((128, 4, D), np.float32)
    sp = np.zeros((1, 4, 16, 128, 128), np.float32)
    ss = np.zeros((1, 128, 16, 128, 128), np.float32)
    vs = np.zeros((1, 128, 4, D), np.float32)
    for c in range(NCORES):
        s, half = c // 2, c % 2
        y = R[c]["yT"].T
        y_prompt[s, half * NPR:(half + 1) * NPR] = y[:NPR]
        y_sample[16 * c:16 * c + 16] = y[NPR:].reshape(16, 4, D)
        if half == 1:
            sp[0, s] = R[c]["sp_out"]
        ss[0, 16 * c:16 * c + 16] = R[c]["ss_out"]
        vs[0, 16 * c:16 * c + 16] = R[c]["vs_out"].reshape(16, 4, D)
    return (y_prompt, y_sample, sp, ss, vs)
```

```python
import numpy as np
from contextlib import ExitStack
import ml_dtypes
import concourse.bass as bass
import concourse.mybir as mybir
from concourse.bass_utils import run_bass_kernel_spmd

F32 = mybir.dt.float32
BF16 = mybir.dt.bfloat16
AF = mybir.ActivationFunctionType
ALU = mybir.AluOpType

NCORES = 8
D = 2048
KC = 16
NT = 1088
NPR = 1024
NSM = 64
DFF = 5632
NJ = 44
NQ = 4
JQ = 11
DPLE = 256
EPS = 1e-6
TBS = [(0, 512), (512, 512), (1024, 64)]
NTILE = 9
RING = 3
SLOT = 4096

V_NF1, V_NMIX, V_NF2, V_NPLE = 0, 2, 4, 6
V_FIN, V_BNG, V_L0, V_L1 = 8, 9, 10, 11
NV = 12

_DBG = {"stop": None}
TWO_LAUNCH = False


class Buf:
    __slots__ = ("w", "r")

    def __init__(self):
        self.w = None
        self.r = {}


class Prog:
    def __init__(self, nc, es):
        self.nc = nc
        self.es = es
        self.eng = {"pe": nc.tensor, "act": nc.scalar, "dve": nc.vector, "pool": nc.gpsimd, "sp": nc.sync}
        self.sems = {}
        self.cnt = {}
        self.seen = {}
        self.nsig = 0
        self.pending = {}
        self.T = []
        for e in self.eng:
            self._mk(e)

    def phase(self):
        self.T = [(k, v) for k, v in self.cnt.items() if v > 0]
        self.pending = {e: True for e in self.eng}

    def _mk(self, name):
        self.sems[name] = self.es.enter_context(self.nc.semaphore("s_" + name))
        self.cnt[name] = 0

    def chan(self, name):
        if name not in self.sems:
            self._mk(name)
        return name

    def wait(self, eng, t):
        if t is None:
            return
        p, n = t
        if p == eng and eng == "pe":
            return
        k = (eng, p)
        if self.seen.get(k, 0) >= n:
            return
        self.seen[k] = n
        self.eng[eng].wait_ge(self.sems[p], n)

    def _deps(self, eng, reads, writes, extra, skip_phase=False):
        if self.pending.get(eng) and not skip_phase:
            self.pending[eng] = False
            for t in self.T:
                self.wait(eng, t)
        for t in extra:
            self.wait(eng, t)
        for b in reads:
            self.wait(eng, b.w)
        for b in writes:
            self.wait(eng, b.w)
            for t in list(b.r.values()):
                self.wait(eng, t)

    def _commit(self, t, reads, writes):
        for b in reads:
            b.r[t[0]] = t
        for b in writes:
            b.w = t
            b.r = {}

    def op(self, eng, fn, reads=(), writes=(), extra=()):
        self._deps(eng, reads, writes, extra)
        ins = fn(self.eng[eng])
        self.cnt[eng] += 1
        ins.then_inc(self.sems[eng], 1)
        t = (eng, self.cnt[eng])
        self._commit(t, reads, writes)
        return t

    def dma(self, q, ch, out, in_, reads=(), writes=(), extra=(), n=1, fn=None, skip_phase=False):
        if ch.endswith("*"):
            self.nsig += 1
            ch = ch[:-1] + str(self.nsig)
        self.chan(ch)
        self._deps(q, reads, writes, extra, skip_phase)
        e = self.eng[q]
        if fn is None:
            e.dma_start(out=out, in_=in_).then_inc(self.sems[ch], 16)
        else:
            n = fn(e, self.sems[ch])
        self.cnt[ch] += 16 * n
        t = (ch, self.cnt[ch])
        self._commit(t, reads, writes)
        return t


def build_program(mode="fused"):
    nc = bass.Bass("TRN2", target_bir_lowering=False)
    es = ExitStack()
    with es:
        _build(nc, es, mode)
    return nc


def _build(nc, es, mode="fused"):
    P = Prog(nc, es)
    stop = _DBG["stop"]

    def din(name, shape, dt=F32):
        return nc.dram_tensor(name, list(shape), dt, kind="ExternalInput").ap()

    def dout(name, shape, dt=F32):
        return nc.dram_tensor(name, list(shape), dt, kind="ExternalOutput").ap()

    xT = din("xT", [D, NT])
    pT = din("pT", [2, DPLE, NT])
    s0 = din("s0", [16, 16, 128, 128])
    vfm = din("vfm", [128, NV, KC])
    cmask = din("cmask", [128, 8, 128])
    par = din("par", [128, 8])
    w1 = [din(f"w1_{i}", [2, NJ, 128, KC * 256]) for i in range(2)]
    w2 = [din(f"w2_{i}", [2, NQ, 8, 128, JQ * 256]) for i in range(2)]
    a_w_in = din("a_w_in", [D, 2 * D])
    a_w_out = din("a_w_out", [D, D])
    b_w_in = din("b_w_in", [D, 4 * D])
    b_w_out = din("b_w_out", [D, D])
    ple_wp = din("ple_wp", [2, DPLE, D])
    ple_wg = din("ple_wg", [2, D, D])
    a_gb = din("a_gb", [128, 2, D])
    a_wsT = din("a_wsT", [128, 16, 128])
    a_wsS = din("a_wsS", [64, 16, 64])
    a_bs = din("a_bs", [16, 128])
    a_bsS = din("a_bsS", [16, 64])
    sel16 = din("sel16", [128, 16, 128])
    lb_bc = din("lb_bc", [128, 2, D])
    s_in = din("s_in", [16, 128, 128]) if mode == "B" else None

    ag_in = nc.dram_tensor("ag_in", [128, 2048], F32).ap()
    ag_out = nc.dram_tensor("ag_out", [256, 2048], F32).ap()
    yT = dout("yT", [D, NT])
    sp_out = dout("sp_out", [16, 128, 128])
    ss_out = dout("ss_out", [16, 16, 128, 128])
    vs_out = dout("vs_out", [NSM, D])

    def sb(name, shape, dt):
        return es.enter_context(nc.sbuf_tensor(name, list(shape), dt))

    h = sb("h", [128, KC, NT], F32)
    xn = sb("xn", [128, KC, NT], BF16)
    ring = sb("ring", [128, RING, SLOT], BF16)
    vf = sb("vf", [128, NV, KC], F32)
    cm = sb("cm", [128, 8, 128], F32)
    cmb = sb("cmb", [128, 8, 128], BF16)
    onesb = sb("onesb", [128, 128], BF16)
    parS = sb("parS", [128, 8], F32)
    epsS = sb("epsS", [128, 1], F32)
    ARF = 10432
    ARB = 16768
    arF = sb("arF", [128, ARF], F32)
    arB = sb("arB", [128, ARB], BF16)
    ps = es.enter_context(nc.psum_tensor("ps", [128, 8 * 512], F32))

    h_b = [Buf() for _ in range(KC)]
    xn_b = [Buf() for _ in range(KC)]
    ring_b = [Buf() for _ in range(RING)]
    bank_b = [Buf() for _ in range(8)]
    cst_b = Buf()
    st = {"ring": 0, "bank": 0}

    def bank():
        i = st["bank"] % 7
        st["bank"] += 1
        return i, ps[:, i * 512:(i + 1) * 512], bank_b[i]

    def aux_bank():
        return 7, ps[:, 7 * 512:8 * 512], bank_b[7]

    def wslot():
        i = st["ring"] % RING
        st["ring"] += 1
        return i, ring_b[i]

    def wload(srcs, kc, ncol):
        i, rb = wslot()
        view = ring[:, i, 0:kc * ncol].rearrange("p (k n) -> p k n", k=kc)

        def fn(e, sem):
            for (src, off, w) in srcs:
                e.dma_start(out=view[:, :, off:off + w], in_=src).then_inc(sem, 16)
            return len(srcs)
        P.dma("pool", f"ring{i}", None, None, writes=[rb], fn=fn, skip_phase=True)
        return view, rb

    P.dma("sp", "init*", h[:], xT.rearrange("(c p) n -> p c n", p=128), writes=h_b)
    P.dma("sp", "init*", vf[:], vfm, writes=[cst_b])
    P.dma("sp", "init*", cm[:], cmask, writes=[cst_b])
    P.dma("sp", "init*", parS[:], par, writes=[cst_b])
    P.op("dve", lambda e: e.tensor_copy(out=cmb[:], in_=cm[:]), reads=[cst_b], writes=[cst_b])
    P.op("dve", lambda e: e.memset(onesb[:], 1.0), writes=[cst_b])
    P.op("dve", lambda e: e.memset(epsS[:], EPS), writes=[cst_b])
    TRI, RM, TRI4, RM4, OH = 0, 1, 2, 3, 4
    if _DBG.get("earlycc"):
        ncr = _DBG.get("ncores", NCORES)
        groups0 = [[2 * i, 2 * i + 1] for i in range(ncr // 2)]
        ag0_in = nc.dram_tensor("ag0_in", [128, 1024], F32).ap()
        ag0_out = nc.dram_tensor("ag0_out", [256, 1024], F32).ap()
        P.dma("pool", "agi0", ag0_in, cm[:, :, :].rearrange("p a b -> p (a b)"), reads=[cst_b])
        P.chan("agc0")
        P.wait("pool", ("agi0", 16))
        nc.gpsimd.collective_compute("AllGather", ALU.bypass, replica_groups=groups0, ins=[ag0_in.opt()], outs=[ag0_out.opt()]).then_inc(P.sems["agc0"], 1)
        nc.gpsimd.wait_ge(P.sems["agc0"], 1)

    def rmsnorm_to(vidx, dst_fn, tmpF):
        for c in range(KC):
            P.op("act", lambda e, c=c: e.activation(out=xn[:, c, :], in_=h[:, c, :], func=AF.Square),
                 reads=[h_b[c]], writes=[xn_b[c]])
        rstd = tmpF
        rb_ = Buf()
        for (t0, tn) in TBS:
            bi, bap, bb = bank()

            def mm(e, t0=t0, tn=tn, bap=bap):
                ins = None
                for c in range(KC):
                    ins = e.matmul(bap[:, 0:tn], lhsT=onesb[:], rhs=xn[:, c, t0:t0 + tn], start=(c == 0), stop=(c == KC - 1))
                return ins
            P.op("pe", mm, reads=xn_b + [cst_b], writes=[bb])
            P.op("act", lambda e, t0=t0, tn=tn, bap=bap: e.activation(out=rstd[:, t0:t0 + tn], in_=bap[:, 0:tn], func=AF.Sqrt,
                                                                  scale=1.0 / D, bias=epsS[:, 0:1]),
                 reads=[bb, cst_b], writes=[rb_])
        P.op("dve", lambda e: e.reciprocal(out=rstd[:, :], in_=rstd[:, :]), reads=[rb_], writes=[rb_])
        for c in range(KC):
            o_ap, obufs = dst_fn(c)
            P.op("dve", lambda e, c=c, o_ap=o_ap: e.scalar_tensor_tensor(out=o_ap, in0=h[:, c, :], scalar=vf[:, vidx, c:c + 1],
                                                                        in1=rstd[:, :], op0=ALU.mult, op1=ALU.mult),
                 reads=[h_b[c], rb_, cst_b], writes=obufs)

    def norm_xn(vidx, tmpF):
        rmsnorm_to(vidx, lambda c: (xn[:, c, :], [xn_b[c]]), tmpF)

    def ffn(li, which):
        w_in = w1[which]
        w_out = w2[which]
        vidx = (V_NF1 if which == 0 else V_NF2) + li
        rstd = arF[:, 0:NT]
        sg = [arF[:, NT + 512 * i: NT + 512 * (i + 1)] for i in range(4)]
        sg_b = [Buf() for _ in range(4)]
        hid = arB[:, 0:JQ * NT].rearrange("p (j n) -> p j n", j=JQ)
        hid_b = [Buf() for _ in range(JQ)]
        P.phase()
        norm_xn(vidx, rstd)
        k = 0
        for q in range(NQ):
            for jj in range(JQ):
                j = q * JQ + jj
                wv, wb = wload([(w_in[li, j].rearrange("p (k n) -> p k n", k=KC), 0, 256)], KC, 256)
                for (t0, tn) in TBS:
                    _, gap, gb = bank()
                    _, uap, ub = bank()

                    def mm(e, wv=wv, t0=t0, tn=tn, gap=gap, uap=uap):
                        ins = None
                        for (ap_, off) in ((gap, 0), (uap, 128)):
                            for c in range(KC):
                                ins = e.matmul(ap_[:, 0:tn], lhsT=wv[:, c, off:off + 128], rhs=xn[:, c, t0:t0 + tn],
                                               start=(c == 0), stop=(c == KC - 1))
                        return ins
                    P.op("pe", mm, reads=xn_b + [wb], writes=[gb, ub])
                    s_i = k % 4
                    k += 1
                    P.op("act", lambda e, s_i=s_i, tn=tn, gap=gap: e.activation(out=sg[s_i][:, 0:tn], in_=gap[:, 0:tn], func=AF.Silu),
                         reads=[gb], writes=[sg_b[s_i]])
                    P.op("dve", lambda e, s_i=s_i, jj=jj, t0=t0, tn=tn, uap=uap: e.tensor_tensor(
                        out=hid[:, jj, t0:t0 + tn], in0=sg[s_i][:, 0:tn], in1=uap[:, 0:tn], op=ALU.mult),
                        reads=[sg_b[s_i], ub], writes=[hid_b[jj]])
            for fp in range(8):
                wv, wb = wload([(w_out[li, q, fp].rearrange("p (k n) -> p k n", k=JQ), 0, 256)], JQ, 256)
                for f2 in range(2):
                    fo = fp * 2 + f2
                    for (t0, tn) in TBS:
                        _, oap, ob = bank()

                        def mm(e, wv=wv, f2=f2, t0=t0, tn=tn, oap=oap):
                            ins = None
                            for jj in range(JQ):
                                ins = e.matmul(oap[:, 0:tn], lhsT=wv[:, jj, f2 * 128:(f2 + 1) * 128], rhs=hid[:, jj, t0:t0 + tn],
                                               start=(jj == 0), stop=(jj == JQ - 1))
                            return ins
                        P.op("pe", mm, reads=hid_b + [wb], writes=[ob])
                        P.op("dve", lambda e, fo=fo, t0=t0, tn=tn, oap=oap: e.scalar_tensor_tensor(
                            out=h[:, fo, t0:t0 + tn], in0=oap[:, 0:tn], scalar=0.5, in1=h[:, fo, t0:t0 + tn],
                            op0=ALU.mult, op1=ALU.add), reads=[ob, h_b[fo]], writes=[h_b[fo]])

    def ple(li):
        rstd = arF[:, 0:NT]
        gt = [arF[:, NT + 512 * i: NT + 512 * (i + 1)] for i in range(4)]
        gt_b = [Buf() for _ in range(4)]
        pb = arB[:, 0:2 * NT].rearrange("p (k n) -> p k n", k=2)
        pb_b = Buf()
        P.phase()
        norm_xn(V_NPLE + li, rstd)
        P.dma("pool", "pld", pb, pT[li].rearrange("(k p) n -> p k n", p=128), writes=[pb_b])
        k = 0
        for f8 in range(8):
            wv, wb = wload([(ple_wg[li, :, f8 * 256:(f8 + 1) * 256].rearrange("(k p) n -> p k n", p=128), 0, 256)], KC, 256)
            wpv, wpb = wload([(ple_wp[li, :, f8 * 256:(f8 + 1) * 256].rearrange("(k p) n -> p k n", p=128), 0, 256)], 2, 256)
            for f2 in range(2):
                fo = f8 * 2 + f2
                for (t0, tn) in TBS:
                    _, gap, gb = bank()
                    _, pap, pbk = bank()

                    def mm(e, wv=wv, wpv=wpv, f2=f2, t0=t0, tn=tn, gap=gap, pap=pap):
                        ins = None
                        for c in range(KC):
                            ins = e.matmul(gap[:, 0:tn], lhsT=wv[:, c, f2 * 128:(f2 + 1) * 128], rhs=xn[:, c, t0:t0 + tn],
                                           start=(c == 0), stop=(c == KC - 1))
                        for c in range(2):
                            ins = e.matmul(pap[:, 0:tn], lhsT=wpv[:, c, f2 * 128:(f2 + 1) * 128], rhs=pb[:, c, t0:t0 + tn],
                                           start=(c == 0), stop=(c == 1))
                        return ins
                    P.op("pe", mm, reads=xn_b + [wb, wpb, pb_b], writes=[gb, pbk])
                    s_i = k % 4
                    k += 1
                    P.op("act", lambda e, s_i=s_i, tn=tn, gap=gap: e.activation(out=gt[s_i][:, 0:tn], in_=gap[:, 0:tn], func=AF.Sigmoid),
                         reads=[gb], writes=[gt_b[s_i]])
                    P.op("dve", lambda e, s_i=s_i, tn=tn, pap=pap: e.tensor_tensor(
                        out=gt[s_i][:, 0:tn], in0=gt[s_i][:, 0:tn], in1=pap[:, 0:tn], op=ALU.mult),
                        reads=[gt_b[s_i], pbk], writes=[gt_b[s_i]])
                    P.op("dve", lambda e, s_i=s_i, fo=fo, t0=t0, tn=tn: e.tensor_tensor(
                        out=h[:, fo, t0:t0 + tn], in0=h[:, fo, t0:t0 + tn], in1=gt[s_i][:, 0:tn], op=ALU.add),
                        reads=[gt_b[s_i], h_b[fo]], writes=[h_b[fo]])

    def out_accum(wv, wb, act_ap_fn, act_bufs, tbs):
        for fo in range(KC):
            for (t0, tn) in tbs:
                _, oap, ob = bank()
                P.op("pe", lambda e, fo=fo, t0=t0, tn=tn, oap=oap: e.matmul(
                    oap[:, 0:tn], lhsT=wv[:, 0, fo * 128:(fo + 1) * 128], rhs=act_ap_fn(t0, tn), start=True, stop=True),
                    reads=act_bufs + [wb], writes=[ob])
                P.op("dve", lambda e, fo=fo, t0=t0, tn=tn, oap=oap: e.tensor_tensor(
                    out=h[:, fo, t0:t0 + tn], in0=h[:, fo, t0:t0 + tn], in1=oap[:, 0:tn], op=ALU.add),
                    reads=[ob, h_b[fo]], writes=[h_b[fo]])


    def gmlp():
        AX = mybir.AxisListType
        gbc = arF[:, 0:2048]
        bbc = arF[:, 2048:4096]
        vS = arF[:, 4096:6144]
        vfx = [arF[:, 6144 + 512 * i: 6144 + 512 * (i + 1)] for i in range(2)]
        ugx = [arF[:, 7168 + 512 * i: 7168 + 512 * (i + 1)] for i in range(2)]
        stt = arF[:, 8192:8192 + 512]
        sum1 = stt[:, 0:72].rearrange("p (t c) -> p t c", t=NTILE)
        sum2 = stt[:, 72:144].rearrange("p (t c) -> p t c", t=NTILE)
        sm = stt[:, 144:144 + 8 * NTILE].rearrange("p (t c) -> p t c", t=NTILE)
        junk = arF[:, 8704:8704 + 256]
        bsF = arF[:, 9216:9216 + 192]
        vT = arB[:, 0:10240].rearrange("p (t n) -> p t n", t=5)
        WsM = arB[:, 10240:12288].rearrange("p (g t) -> p g t", g=16)
        WsS = arB[:, 12288:13312].rearrange("p (g t) -> p g t", g=16)
        usx = [arB[:, 13312 + 512 * i: 13312 + 512 * (i + 1)] for i in range(2)]
        bsH = arB[:, 14336:14464]
        bsL = arB[:, 14464:14592]
        bsSH = arB[:, 14592:14656]
        bsSL = arB[:, 14656:14720]
        SEL = arB[:, 14720:16768].rearrange("p (g t) -> p g t", g=16)
        gb_b, vS_b, stt_b, ws_b, bs_b = Buf(), Buf(), Buf(), Buf(), Buf()
        vfx_b = [Buf(), Buf()]
        ugx_b = [Buf(), Buf()]
        usx_b = [Buf(), Buf()]
        junk_b = Buf()
        vT_b = [Buf() for _ in range(5)]
        rstd = arF[:, 4096:4096 + NT]
        P.phase()
        norm_xn(V_NMIX + 0, rstd)
        P.dma("sp", "gml*", gbc, a_gb[:, 0, :], writes=[gb_b])
        P.dma("sp", "gml*", bbc, a_gb[:, 1, :], writes=[gb_b])
        P.dma("sp", "gml*", bsF[0:16, 0:128], a_bs, writes=[bs_b])
        P.dma("sp", "gml*", bsF[0:16, 128:192], a_bsS, writes=[bs_b])
        P.dma("pool", "gmlp*", WsM[:, :, :], a_wsT, writes=[ws_b])
        P.dma("pool", "gmlp*", WsS[0:64, :, :], a_wsS, writes=[ws_b])
        P.dma("pool", "gmlp*", SEL[:, :, :], sel16, writes=[ws_b])
        for g in range(16):
            P.op("dve", lambda e, g=g: e.tensor_tensor(out=WsM[:, g, :], in0=WsM[:, g, :], in1=cmb[:, 5, :], op=ALU.mult),
                 reads=[cst_b], writes=[ws_b])
            P.op("dve", lambda e, g=g: e.tensor_tensor(out=WsS[0:64, g, :], in0=WsS[0:64, g, :], in1=cmb[0:64, 2, 0:64], op=ALU.mult),
                 reads=[cst_b], writes=[ws_b])
        P.op("dve", lambda e: e.memset(arB[:, 14336:14720], 0.0), writes=[ws_b])
        P.op("dve", lambda e: e.tensor_copy(out=bsH[0:16, :], in_=bsF[0:16, 0:128]), reads=[bs_b], writes=[ws_b])
        P.op("dve", lambda e: e.tensor_tensor(out=bsL[0:16, :], in0=bsF[0:16, 0:128], in1=bsH[0:16, :], op=ALU.subtract), reads=[bs_b], writes=[ws_b])
        P.op("dve", lambda e: e.tensor_copy(out=bsSH[0:16, :], in_=bsF[0:16, 128:192]), reads=[bs_b], writes=[ws_b])
        P.op("dve", lambda e: e.tensor_tensor(out=bsSL[0:16, :], in0=bsF[0:16, 128:192], in1=bsSH[0:16, :], op=ALU.subtract), reads=[bs_b], writes=[ws_b])

        for (tiles, tbs) in (([0, 1, 2, 3], [(0, 512)]), ([4, 5, 6, 7, 8], [(512, 512), (1024, 64)])):
            k = 0
            for cb in range(8):
                wv, wb = wload([(a_w_in[:, D + cb * 256: D + (cb + 1) * 256].rearrange("(k p) n -> p k n", p=128), 0, 256)], KC, 256)
                for lt, tt in enumerate(tiles):
                    rows = 64 if tt == 8 else 128
                    _, bap, bb = bank()

                    def mm(e, wv=wv, tt=tt, rows=rows, bap=bap):
                        ins = None
                        for c in range(KC):
                            ins = e.matmul(bap[0:rows, 0:256], lhsT=xn[:, c, tt * 128: tt * 128 + rows], rhs=wv[:, c, 0:256],
                                           start=(c == 0), stop=(c == KC - 1))
                        return ins
                    P.op("pe", mm, reads=xn_b + [wb], writes=[bb])
                    i = k % 2
                    k += 1
                    P.op("act", lambda e, i=i, rows=rows, bap=bap, tt=tt, cb=cb: e.activation(
                        out=vfx[i][0:rows, 0:256], in_=bap[0:rows, 0:256], func=AF.Gelu, accum_out=sum1[0:rows, tt, cb:cb + 1]),
                        reads=[bb], writes=[vfx_b[i], stt_b])
                    P.op("dve", lambda e, i=i, rows=rows, tt=tt, cb=cb: e.scalar_tensor_tensor(
                        out=junk[0:rows, 0:256], in0=vfx[i][0:rows, 0:256], scalar=1.0, in1=vfx[i][0:rows, 0:256], op0=ALU.mult, op1=ALU.mult,
                        accum_out=sum2[0:rows, tt, cb:cb + 1]), reads=[vfx_b[i]], writes=[junk_b, stt_b])
                    if tt == 8:
                        P.op("dve", lambda e, i=i, cb=cb: e.tensor_copy(out=vS[0:64, cb * 256:(cb + 1) * 256], in_=vfx[i][0:64, 0:256]),
                             reads=[vfx_b[i]], writes=[vS_b])
                    else:
                        P.op("dve", lambda e, i=i, lt=lt, cb=cb: e.tensor_copy(out=vT[:, lt, cb * 256:(cb + 1) * 256], in_=vfx[i][:, 0:256]),
                             reads=[vfx_b[i]], writes=[vT_b[lt]])
            for lt, tt in sorted(enumerate(tiles), key=lambda x: -x[1]):
                rows = 64 if tt == 8 else 128
                S = lambda j, tt=tt, rows=rows: sm[0:rows, tt, j:j + 1]
                P.op("dve", lambda e, tt=tt, rows=rows, S=S: e.tensor_reduce(out=S(0), in_=sum1[0:rows, tt, :], axis=AX.X, op=ALU.add),
                     reads=[stt_b], writes=[stt_b])
                P.op("dve", lambda e, tt=tt, rows=rows, S=S: e.tensor_reduce(out=S(1), in_=sum2[0:rows, tt, :], axis=AX.X, op=ALU.add),
                     reads=[stt_b], writes=[stt_b])
                P.op("dve", lambda e, S=S: e.tensor_scalar(out=S(2), in0=S(0), scalar1=1.0 / D, scalar2=None, op0=ALU.mult), reads=[stt_b], writes=[stt_b])
                P.op("dve", lambda e, S=S: e.tensor_tensor(out=S(3), in0=S(2), in1=S(2), op=ALU.mult), reads=[stt_b], writes=[stt_b])
                P.op("dve", lambda e, S=S: e.scalar_tensor_tensor(out=S(4), in0=S(1), scalar=1.0 / D, in1=S(3), op0=ALU.mult, op1=ALU.subtract),
                     reads=[stt_b], writes=[stt_b])
                P.op("act", lambda e, S=S, rows=rows: e.activation(out=S(5), in_=S(4), func=AF.Sqrt, bias=epsS[0:rows, 0:1]),
                     reads=[stt_b, cst_b], writes=[stt_b])
                P.op("dve", lambda e, S=S: e.reciprocal(out=S(5), in_=S(5)), reads=[stt_b], writes=[stt_b])
                P.op("dve", lambda e, S=S: e.scalar_tensor_tensor(out=S(6), in0=S(2), scalar=-1.0, in1=S(5), op0=ALU.mult, op1=ALU.mult),
                     reads=[stt_b], writes=[stt_b])
                if tt == 8:
                    P.op("act", lambda e, S=S: e.activation(out=vS[0:64, :], in_=vS[0:64, :], func=AF.Identity, scale=S(5), bias=S(6)),
                         reads=[stt_b], writes=[vS_b])
                    P.op("dve", lambda e: e.tensor_tensor(out=vS[0:64, :], in0=vS[0:64, :], in1=gbc[0:64, :], op=ALU.mult), reads=[gb_b], writes=[vS_b])
                    P.op("dve", lambda e: e.tensor_tensor(out=vS[0:64, :], in0=vS[0:64, :], in1=bbc[0:64, :], op=ALU.add), reads=[gb_b], writes=[vS_b])
                    P.op("dve", lambda e, lt=lt: e.tensor_copy(out=vT[0:64, lt, :], in_=vS[0:64, :]), reads=[vS_b], writes=[vT_b[lt]])
                    P.dma("sp", "o_vs", vs_out, vS[0:64, :], reads=[vS_b])
                else:
                    P.op("act", lambda e, S=S, lt=lt: e.activation(out=vS[:, :], in_=vT[:, lt, :], func=AF.Identity, scale=S(5), bias=S(6)),
                         reads=[stt_b, vT_b[lt]], writes=[vS_b])
                    P.op("dve", lambda e: e.tensor_tensor(out=vS[:, :], in0=vS[:, :], in1=gbc, op=ALU.mult), reads=[gb_b], writes=[vS_b])
                    P.op("dve", lambda e, lt=lt: e.tensor_tensor(out=vT[:, lt, :], in0=vS[:, :], in1=bbc, op=ALU.add), reads=[gb_b, vS_b], writes=[vT_b[lt]])
            k = 0
            for gp in range(8):
                wv, wb = wload([(a_w_in[:, gp * 256:(gp + 1) * 256].rearrange("(k p) n -> p k n", p=128), 0, 256)], KC, 256)
                for g2 in range(2):
                    g = gp * 2 + g2
                    wo, wob = wload([(a_w_out[g * 128:(g + 1) * 128, :].rearrange("(k p) n -> p k n", p=128), 0, D)], 1, D)
                    for (t0, tn) in tbs:
                        _, uap, ub = bank()
                        _, sap, sbk = bank()

                        def mm(e, wv=wv, g2=g2, g=g, t0=t0, tn=tn, uap=uap, sap=sap, tiles=tiles):
                            ins = None
                            for c in range(KC):
                                ins = e.matmul(uap[:, 0:tn], lhsT=wv[:, c, g2 * 128:(g2 + 1) * 128], rhs=xn[:, c, t0:t0 + tn],
                                               start=(c == 0), stop=(c == KC - 1))
                            if tn == 64:
                                lt = tiles.index(8)
                                ins = e.matmul(sap[:, 0:64], lhsT=vT[0:64, lt, g * 128:(g + 1) * 128], rhs=WsS[0:64, g, :], start=True, stop=_DBG.get("nobias", False))
                                if not _DBG.get("nobias"):
                                    e.matmul(sap[:, 0:64], lhsT=SEL[:, g, :], rhs=bsSH[:, :], start=False, stop=False)
                                    ins = e.matmul(sap[:, 0:64], lhsT=SEL[:, g, :], rhs=bsSL[:, :], start=False, stop=True)
                            else:
                                for tt in range(t0 // 128, (t0 + tn) // 128):
                                    lt = tiles.index(tt)
                                    o = tt * 128 - t0
                                    ins = e.matmul(sap[:, o:o + 128], lhsT=vT[:, lt, g * 128:(g + 1) * 128], rhs=WsM[:, g, :], start=True, stop=_DBG.get("nobias", False))
                                    if not _DBG.get("nobias"):
                                        e.matmul(sap[:, o:o + 128], lhsT=SEL[:, g, :], rhs=bsH[:, :], start=False, stop=False)
                                        ins = e.matmul(sap[:, o:o + 128], lhsT=SEL[:, g, :], rhs=bsL[:, :], start=False, stop=True)
                            return ins
                        P.op("pe", mm, reads=xn_b + vT_b + [wb, ws_b], writes=[ub, sbk])
                        i = k % 2
                        k += 1
                        P.op("act", lambda e, i=i, tn=tn, uap=uap: e.activation(out=ugx[i][:, 0:tn], in_=uap[:, 0:tn], func=AF.Gelu),
                             reads=[ub], writes=[ugx_b[i]])
                        P.op("dve", lambda e, i=i, tn=tn, sap=sap: e.tensor_tensor(out=usx[i][:, 0:tn], in0=ugx[i][:, 0:tn], in1=sap[:, 0:tn], op=ALU.mult),
                             reads=[ugx_b[i], sbk], writes=[usx_b[i]])
                        out_accum(wo, wob, lambda a, n, i=i: usx[i][:, 0:n], [usx_b[i]], [(t0, tn)])


    def hgrn():
        Sst = arF[:, 0:2048].rearrange("p (h v) -> p h v", h=16)
        S0 = arF[:, 2048:4096].rearrange("p (s v) -> p s v", s=16)
        qs = arF[:, 4096:4608]
        kF = arF[:, 4608:5120]
        Ef = arF[:, 5120:5632]
        tF = arF[:, 5632:6144]
        rst = arF[:, 6144:6656]
        logfT = arF[:, 6656:7808].rearrange("p (t k) -> p t k", t=NTILE)
        t1x = [arF[:, 7808:8064], arF[:, 8064:8320]]
        kTx = [arF[:, 8320:8576], arF[:, 8576:8832]]
        edTx = [arF[:, 8832:9088], arF[:, 9088:9344]]
        lbh = arF[:, 9344:9600]
        omlh = arF[:, 9600:9856]
        l01 = arF[:, 9856:10368]
        Dl = arF[:, 10368:10400]
        omlFM = arF[:, 10400:10416]
        qt = arB[:, 0:1088]
        kt = arB[:, 1088:2176]
        sg = arB[:, 2176:3264]
        osq = arB[:, 3264:4352]
        ot = arB[:, 4352:5440]
        kdec = arB[:, 5440:6592].rearrange("p (t k) -> p t k", t=NTILE)
        kdecB = arB[:, 13248:14400].rearrange("p (t k) -> p t k", t=NTILE)
        vT = arB[:, 6592:7744].rearrange("p (t k) -> p t k", t=NTILE)
        attm = arB[:, 7744:8896].rearrange("p (t k) -> p t k", t=NTILE)
        Sbf = arB[:, 8896:10944].rearrange("p (c v) -> p c v", c=16)
        S0bf = arB[:, 10944:12992].rearrange("p (s v) -> p s v", s=16)
        vmk = [arB[:, 12992 + 128 * i: 12992 + 128 * (i + 1)] for i in range(2)]
        B = {k: Buf() for k in ("Sst", "S0", "qs", "kF", "Ef", "tF", "rst", "logf", "t1", "kT", "edT", "lb", "l01", "Dl", "oml",
                                "qt", "kt", "sg", "osq", "ot", "kdec", "vT", "attm", "Sbf", "S0bf", "vmk0", "vmk1",
                                "t1_0", "t1_1", "kT_0", "kT_1", "edT_0", "edT_1")}
        rstd = arF[:, 4096:4096 + NT]
        P.phase()
        norm_xn(V_NMIX + 1, rstd)
        P.op("dve", lambda e: e.tensor_tensor(out=omlFM, in0=vf[:, V_L0, :], in1=vf[:, V_L1, :], op=ALU.subtract), reads=[cst_b], writes=[B["oml"]])
        P.op("act", lambda e: e.activation(out=omlFM, in_=omlFM, func=AF.Sigmoid), writes=[B["oml"]])
        P.op("dve", lambda e: e.memset(arF[:, 0:2048], 0.0), writes=[B["Sst"]])
        P.op("dve", lambda e: e.memset(arB[:, 0:14400], 0.0), writes=[B[k_] for k_ in ("kdec", "vT", "attm", "vmk0", "vmk1", "Sbf", "S0bf", "qt", "kt")])

        def head(hh, full):
            c0 = hh * 128
            if full:
                wA, wAb = wload([(b_w_in[:, c0:c0 + 128].rearrange("(k p) n -> p k n", p=128), 0, 128),
                                 (b_w_in[:, 3 * D + c0:3 * D + c0 + 128].rearrange("(k p) n -> p k n", p=128), 128, 128)], KC, 256)
            wB, wBb = wload([(b_w_in[:, D + c0:D + c0 + 128].rearrange("(k p) n -> p k n", p=128), 0, 128),
                             (b_w_in[:, 2 * D + c0:2 * D + c0 + 128].rearrange("(k p) n -> p k n", p=128), 128, 128)], KC, 256)
            if full:
                wO, wOb = wload([(b_w_out[c0:c0 + 128, :].rearrange("(k p) n -> p k n", p=128), 0, D)], 1, D)
            P.dma("sp", "lbA", l01[:, 0:128], lb_bc[:, 0, c0:c0 + 128], writes=[B["l01"]])
            P.dma("sp", "lbB", l01[:, 128:256], lb_bc[:, 0, c0:c0 + 128], writes=[B["l01"]])
            P.dma("sp", "lbC", l01[:, 256:384], lb_bc[:, 1, c0:c0 + 128], writes=[B["l01"]])
            P.dma("sp", "lbD", l01[:, 384:512], lb_bc[:, 1, c0:c0 + 128], writes=[B["l01"]])
            for ch_ in ("lbA", "lbB", "lbC", "lbD"):
                P.wait("dve", (ch_, P.cnt[ch_]))
            P.op("dve", lambda e: e.tensor_tensor(out=lbh, in0=l01[:, 0:256], in1=l01[:, 256:512], op=ALU.subtract), reads=[B["l01"]], writes=[B["lb"]])
            P.op("act", lambda e: e.activation(out=lbh, in_=lbh, func=AF.Exp), writes=[B["lb"]])
            P.op("dve", lambda e: e.tensor_scalar(out=lbh, in0=lbh, scalar1=1.0, scalar2=None, op0=ALU.add), writes=[B["lb"]])
            P.op("dve", lambda e: e.reciprocal(out=lbh, in_=lbh), writes=[B["lb"]])
            P.op("dve", lambda e: e.tensor_scalar(out=omlh, in0=lbh, scalar1=-1.0, scalar2=1.0, op0=ALU.mult, op1=ALU.add), reads=[B["lb"]], writes=[B["lb"]])
            tiles = list(range(NTILE)) if full else list(range(8))
            _, blap, blb = aux_bank()
            tgroups = [(0, 2), (2, 2), (4, 2), (6, 2)] + ([(8, 1)] if full else [])
            for gi, (tA, ntl) in enumerate(tgroups):
                rows = 64 if tA == 8 else 128
                W = ntl * 128
                _, bap, bb = bank()

                def mm(e, tA=tA, ntl=ntl, rows=rows, bap=bap):
                    ins = None
                    for i in range(ntl):
                        tt = tA + i
                        for c in range(KC):
                            ins = e.matmul(bap[0:rows, i * 256:(i + 1) * 256], lhsT=xn[:, c, tt * 128: tt * 128 + rows], rhs=wB[:, c, 0:256],
                                           start=(c == 0), stop=(c == KC - 1))
                    return ins
                P.op("pe", mm, reads=xn_b + [wBb], writes=[bb])
                pp = gi % 2
                t1 = t1x[pp]
                kT = kTx[pp]
                edT = edTx[pp]
                Bt1, BkT, Bed = B[f"t1_{pp}"], B[f"kT_{pp}"], B[f"edT_{pp}"]
                bv = bap[0:rows, 0:ntl * 256].rearrange("p (t c) -> p t c", t=ntl)
                t13 = t1[0:rows, 0:W].rearrange("p (t c) -> p t c", t=ntl)
                t_exp = P.op("act", lambda e, bv=bv, t13=t13: e.activation(out=t13, in_=bv[:, :, 0:128], func=AF.Exp, scale=-1.0), reads=[bb], writes=[Bt1])
                P.op("dve", lambda e, rows=rows, bv=bv, tA=tA, ntl=ntl: e.tensor_copy(out=vT[0:rows, tA:tA + ntl, :], in_=bv[:, :, 128:256]), reads=[bb], writes=[B["vT"]], extra=[t_exp])
                P.op("dve", lambda e, rows=rows, t1=t1, W=W: e.tensor_scalar(out=t1[0:rows, 0:W], in0=t1[0:rows, 0:W], scalar1=1.0, scalar2=None, op0=ALU.add), writes=[Bt1])
                P.op("dve", lambda e, rows=rows, t1=t1, W=W: e.reciprocal(out=t1[0:rows, 0:W], in_=t1[0:rows, 0:W]), writes=[Bt1])
                P.op("dve", lambda e, rows=rows, t1=t1, W=W: e.tensor_tensor(out=t1[0:rows, 0:W], in0=t1[0:rows, 0:W], in1=omlh[0:rows, 0:W], op=ALU.mult), reads=[B["lb"]], writes=[Bt1])
                P.op("dve", lambda e, rows=rows, t1=t1, W=W: e.tensor_tensor(out=t1[0:rows, 0:W], in0=t1[0:rows, 0:W], in1=lbh[0:rows, 0:W], op=ALU.add), reads=[B["lb"]], writes=[Bt1])
                P.op("act", lambda e, rows=rows, tA=tA, ntl=ntl, t13=t13: e.activation(out=logfT[0:rows, tA:tA + ntl, :], in_=t13, func=AF.Ln), reads=[Bt1], writes=[B["logf"]])
                P.op("dve", lambda e, rows=rows, t1=t1, kT=kT, W=W: e.tensor_scalar(out=kT[0:rows, 0:W], in0=t1[0:rows, 0:W], scalar1=-1.0, scalar2=1.0, op0=ALU.mult, op1=ALU.add),
                     reads=[Bt1], writes=[BkT])
                _, dap, db = bank()
                mk = RM4 if tA == 8 else RM

                def mmd(e, tA=tA, ntl=ntl, rows=rows, dap=dap, mk=mk):
                    ins = None
                    for i in range(ntl):
                        ins = e.matmul(dap[0:rows, i * 128:(i + 1) * 128], lhsT=cm[0:rows, mk, 0:rows], rhs=logfT[0:rows, tA + i, :], start=True, stop=True)
                    return ins
                P.op("pe", mmd, reads=[B["logf"], cst_b], writes=[db])
                P.op("act", lambda e, rows=rows, dap=dap, edT=edT, W=W: e.activation(out=edT[0:rows, 0:W], in_=dap[0:rows, 0:W], func=AF.Exp), reads=[db], writes=[Bed])
                kT3 = kT[0:rows, 0:W].rearrange("p (t c) -> p t c", t=ntl)
                ed3 = edT[0:rows, 0:W].rearrange("p (t c) -> p t c", t=ntl)
                P.op("dve", lambda e, rows=rows, tA=tA, ntl=ntl, kT3=kT3, ed3=ed3: e.scalar_tensor_tensor(out=kdec[0:rows, tA:tA + ntl, :], in0=kT3, scalar=cm[0:rows, 6, 0:1], in1=ed3,
                                                                                       op0=ALU.mult, op1=ALU.mult), reads=[BkT, Bed, cst_b], writes=[B["kdec"]])
                if tA < 8:
                    P.op("dve", lambda e, tA=tA, ntl=ntl, kT3=kT3, ed3=ed3: e.scalar_tensor_tensor(out=kdecB[:, tA:tA + ntl, :], in0=kT3, scalar=cm[:, 6, 1:2], in1=ed3,
                                                                                   op0=ALU.mult, op1=ALU.mult), reads=[BkT, Bed, cst_b], writes=[B["kdec"]])

                def mmbl(e, tA=tA, ntl=ntl, blap=blap):
                    ins = None
                    if tA == 8:
                        ins = e.matmul(blap[:, 16:32], lhsT=logfT[0:64, 8, :], rhs=cm[0:64, TRI4, 3:64:4], start=True, stop=True)
                    else:
                        for i in range(ntl):
                            tt = tA + i
                            ins = e.matmul(blap[:, 2 * tt:2 * tt + 2], lhsT=logfT[:, tt, :], rhs=cm[:, TRI, 63:128:64], start=True, stop=True)
                    return ins
                P.op("pe", mmbl, reads=[B["logf"], cst_b], writes=[blb])
            nb = 32 if full else 16
            P.op("act", lambda e, blap=blap, nb=nb: e.activation(out=Dl[:, 0:nb], in_=blap[:, 0:nb], func=AF.Exp), reads=[blb], writes=[B["Dl"]])
            for c4 in range(4):
                _, uap, ub = bank()

                def mmu(e, c4=c4, uap=uap):
                    ins = None
                    for ci in range(4):
                        c = c4 * 4 + ci
                        tt, hf = c // 2, c % 2
                        ins = e.matmul(uap[:, ci * 128:(ci + 1) * 128], lhsT=(kdecB if hf else kdec)[:, tt, :], rhs=vT[:, tt, :],
                                       start=True, stop=True)
                    return ins
                P.op("pe", mmu, reads=[B["kdec"], B["vT"]], writes=[ub])
                for ci in range(4):
                    c = c4 * 4 + ci
                    if full:
                        P.op("dve", lambda e, c=c: e.tensor_copy(out=Sbf[:, c, :], in_=Sst[:, hh, :]), reads=[B["Sst"]], writes=[B["Sbf"]])
                    P.op("dve", lambda e, c=c, ci=ci, uap=uap: e.scalar_tensor_tensor(out=Sst[:, hh, :], in0=Sst[:, hh, :], scalar=Dl[:, c:c + 1],
                                                                                  in1=uap[:, ci * 128:(ci + 1) * 128], op0=ALU.mult, op1=ALU.add),
                         reads=[B["Dl"], ub], writes=[B["Sst"]])
            if not full:
                return
            P.dma("sp", "s0ld", S0[:, :, :], s0[:, hh].rearrange("s k v -> k s v"), writes=[B["S0"]])
            P.op("act", lambda e: e.activation(out=S0bf[:, :, :], in_=S0[:, :, :], func=AF.Copy), reads=[B["S0"]], writes=[B["S0bf"]])
            for j4 in range(4):
                _, uap, ub = bank()
                for ji in range(4):
                    j = j4 * 4 + ji
                    i = j % 2
                    P.op("dve", lambda e, i=i, j=j: e.tensor_scalar(out=vmk[i][0:64, :], in0=vT[0:64, 8, :], scalar1=cm[0:64, OH, j:j + 1], scalar2=None, op0=ALU.mult),
                         reads=[B["vT"], cst_b], writes=[B[f"vmk{i}"]])
                    P.op("pe", lambda e, i=i, ji=ji, uap=uap: e.matmul(uap[:, ji * 128:(ji + 1) * 128], lhsT=kdec[:, 8, :], rhs=vmk[i][:, :], start=True, stop=True),
                         reads=[B["kdec"], B[f"vmk{i}"]], writes=[ub])
                for ji in range(4):
                    j = j4 * 4 + ji
                    P.op("dve", lambda e, j=j, ji=ji, uap=uap: e.scalar_tensor_tensor(out=S0[:, j, :], in0=S0[:, j, :], scalar=Dl[:, 16 + j:17 + j],
                                                                                  in1=uap[:, ji * 128:(ji + 1) * 128], op0=ALU.mult, op1=ALU.add),
                         reads=[B["Dl"], ub, B["S0bf"]], writes=[B["S0"]])
            P.dma("sp", "o_ss", ss_out[:, hh].rearrange("s k v -> k s v"), S0[:, :, :], reads=[B["S0"]])
            for (t0, tn) in TBS:
                _, qap, qb = bank()
                _, fap, fb = bank()
                _, gap, gb = bank()

                def mmf(e, t0=t0, tn=tn, qap=qap, fap=fap, gap=gap):
                    ins = None
                    for (ap_, wv_, off) in ((qap, wA, 0), (fap, wB, 0), (gap, wA, 128)):
                        for c in range(KC):
                            ins = e.matmul(ap_[:, 0:tn], lhsT=wv_[:, c, off:off + 128], rhs=xn[:, c, t0:t0 + tn], start=(c == 0), stop=(c == KC - 1))
                    return ins
                P.op("pe", mmf, reads=xn_b + [wAb, wBb], writes=[qb, fb, gb])
                P.op("act", lambda e, tn=tn, qap=qap: e.activation(out=qs[:, 0:tn], in_=qap[:, 0:tn], func=AF.Silu), reads=[qb], writes=[B["qs"]])
                P.op("act", lambda e, t0=t0, tn=tn, gap=gap: e.activation(out=sg[:, t0:t0 + tn], in_=gap[:, 0:tn], func=AF.Silu), reads=[gb], writes=[B["sg"]])
                P.op("act", lambda e, tn=tn, fap=fap: e.activation(out=kF[:, 0:tn], in_=fap[:, 0:tn], func=AF.Sigmoid, scale=-1.0), reads=[fb], writes=[B["kF"]])
                P.op("dve", lambda e, tn=tn: e.tensor_scalar(out=kF[:, 0:tn], in0=kF[:, 0:tn], scalar1=omlFM[:, hh:hh + 1], scalar2=None, op0=ALU.mult),
                     reads=[B["oml"]], writes=[B["kF"]])
                _, bap, bb = bank()

                def mmb(e, t0=t0, tn=tn, bap=bap):
                    ins = None
                    if tn == 64:
                        ins = e.matmul(bap[:, 0:64], lhsT=logfT[0:64, 8, :], rhs=cm[0:64, TRI4, 0:64], start=True, stop=True)
                    else:
                        for tt in range(t0 // 128, (t0 + tn) // 128):
                            o = tt * 128 - t0
                            ins = e.matmul(bap[:, o:o + 128], lhsT=logfT[:, tt, :], rhs=cm[:, TRI, :], start=True, stop=True)
                    return ins
                P.op("pe", mmb, reads=[B["logf"], cst_b], writes=[bb])
                P.op("act", lambda e, tn=tn, bap=bap: e.activation(out=Ef[:, 0:tn], in_=bap[:, 0:tn], func=AF.Exp), reads=[bb], writes=[B["Ef"]])
                P.op("act", lambda e, tn=tn, bap=bap: e.activation(out=tF[:, 0:tn], in_=bap[:, 0:tn], func=AF.Exp, scale=-1.0), reads=[bb], writes=[B["tF"]])
                P.op("dve", lambda e, t0=t0, tn=tn: e.tensor_tensor(out=qt[:, t0:t0 + tn], in0=qs[:, 0:tn], in1=Ef[:, 0:tn], op=ALU.mult),
                     reads=[B["qs"], B["Ef"]], writes=[B["qt"]])
                P.op("dve", lambda e, t0=t0, tn=tn: e.tensor_tensor(out=kt[:, t0:t0 + tn], in0=kF[:, 0:tn], in1=tF[:, 0:tn], op=ALU.mult),
                     reads=[B["kF"], B["tF"]], writes=[B["kt"]])
                _, aap, ab = bank()

                def mma(e, t0=t0, tn=tn, aap=aap):
                    ins = None
                    if tn == 64:
                        ins = e.matmul(aap[0:64, 0:64], lhsT=kt[:, 1024:1088], rhs=qt[:, 1024:1088], start=True, stop=True)
                    else:
                        for tt in range(t0 // 128, (t0 + tn) // 128):
                            o = tt * 128 - t0
                            ins = e.matmul(aap[:, o:o + 128], lhsT=kt[:, tt * 128:(tt + 1) * 128], rhs=qt[:, tt * 128:(tt + 1) * 128], start=True, stop=True)
                    return ins
                P.op("pe", mma, reads=[B["kt"], B["qt"]], writes=[ab])
                if tn == 64:
                    P.op("dve", lambda e, aap=aap: e.tensor_tensor(out=attm[0:64, 8, 0:64], in0=aap[0:64, 0:64], in1=cm[0:64, TRI4, 0:64], op=ALU.mult),
                         reads=[ab, cst_b], writes=[B["attm"]])
                else:
                    for tt in range(t0 // 128, (t0 + tn) // 128):
                        o = tt * 128 - t0
                        P.op("dve", lambda e, tt=tt, o=o, aap=aap: e.tensor_tensor(out=attm[:, tt, :], in0=aap[:, o:o + 128], in1=cm[:, TRI, :], op=ALU.mult),
                             reads=[ab, cst_b], writes=[B["attm"]])
                _, oap, ob = bank()

                def mmo(e, t0=t0, tn=tn, oap=oap):
                    ins = None
                    if tn == 64:
                        e.matmul(oap[:, 0:64], lhsT=vT[:, 8, :], rhs=attm[:, 8, 0:64], start=True, stop=False)
                        for j in range(16):
                            ins = e.matmul(oap[:, 4 * j:4 * j + 4], lhsT=S0bf[:, j, :], rhs=qt[:, 1024 + 4 * j:1028 + 4 * j], start=False, stop=(j == 15))
                    else:
                        for tt in range(t0 // 128, (t0 + tn) // 128):
                            o = tt * 128 - t0
                            e.matmul(oap[:, o:o + 128], lhsT=vT[:, tt, :], rhs=attm[:, tt, :], start=True, stop=False)
                            for hf in range(2):
                                c = tt * 2 + hf
                                ins = e.matmul(oap[:, o + hf * 64:o + hf * 64 + 64], lhsT=Sbf[:, c, :], rhs=qt[:, tt * 128 + hf * 64:tt * 128 + hf * 64 + 64],
                                               start=False, stop=(hf == 1))
                    return ins
                P.op("pe", mmo, reads=[B["vT"], B["attm"], B["Sbf"], B["S0bf"], B["qt"]], writes=[ob])
                P.op("act", lambda e, t0=t0, tn=tn, oap=oap: e.activation(out=osq[:, t0:t0 + tn], in_=oap[:, 0:tn], func=AF.Square), reads=[ob], writes=[B["osq"]])
                _, sap, sbk = bank()
                P.op("pe", lambda e, t0=t0, tn=tn, sap=sap: e.matmul(sap[:, 0:tn], lhsT=onesb[:], rhs=osq[:, t0:t0 + tn], start=True, stop=True),
                     reads=[B["osq"], cst_b], writes=[sbk])
                P.op("act", lambda e, tn=tn, sap=sap: e.activation(out=rst[:, 0:tn], in_=sap[:, 0:tn], func=AF.Sqrt, scale=1.0 / 128, bias=epsS[:, 0:1]),
                     reads=[sbk, cst_b], writes=[B["rst"]])
                P.op("dve", lambda e, tn=tn: e.reciprocal(out=rst[:, 0:tn], in_=rst[:, 0:tn]), writes=[B["rst"]])
                P.op("dve", lambda e, tn=tn, oap=oap: e.scalar_tensor_tensor(out=tF[:, 0:tn], in0=oap[:, 0:tn], scalar=vf[:, V_BNG, hh:hh + 1], in1=rst[:, 0:tn],
                                                                          op0=ALU.mult, op1=ALU.mult), reads=[ob, B["rst"], cst_b], writes=[B["tF"]])
                P.op("dve", lambda e, t0=t0, tn=tn: e.tensor_tensor(out=ot[:, t0:t0 + tn], in0=tF[:, 0:tn], in1=sg[:, t0:t0 + tn], op=ALU.mult),
                     reads=[B["tF"], B["sg"]], writes=[B["ot"]])
                out_accum(wO, wOb, lambda a, n: ot[:, a:a + n], [B["ot"]], [(t0, tn)])

        groups = None
        if mode == "B":
            P.dma("sp", "sin", Sst[:, :, :], s_in.rearrange("h k v -> k h v"), writes=[B["Sst"]])
        if mode != "B" and _DBG.get("xcore", True):
            for hh in range(16):
                head(hh, False)
            ncr = _DBG.get("ncores", NCORES)
            groups = [[2 * i, 2 * i + 1] for i in range(ncr // 2)]
            if _DBG.get("nocc"):
                P.op("dve", lambda e: e.memset(arF[:, 0:2048], 0.0), writes=[B["Sst"]])
                groups = None
            if mode == "A":
                P.dma("sp", "o_sp", sp_out.rearrange("h k v -> k h v"), Sst[:, :, :], reads=[B["Sst"]])
                return
        if mode == "fused" and _DBG.get("xcore", True) and groups is not None:
            P.dma("pool", "agi", ag_in, arF[:, 0:2048], reads=[B["Sst"]])
            ag_b = Buf()
            t_in = B["Sst"].r["agi"]
            P.chan("agc")
            P.wait("pool", t_in)
            nc.gpsimd.collective_compute("AllGather", ALU.bypass, replica_groups=groups, ins=[ag_in.opt()], outs=[ag_out.opt()]).then_inc(P.sems["agc"], 1)
            ag_b.w = ("agc", 1)
            if _DBG.get("cconly"):
                P.wait("pool", ("agc", 1))
                P.wait("dve", ("agc", 1))
                P.op("dve", lambda e: e.memset(arF[:, 0:2048], 0.0), writes=[B["Sst"]])
            else:
                P.dma("pool", "ago", arF[:, 0:2048], ag_out[0:128, :], reads=[ag_b], writes=[B["Sst"]])
            if _DBG.get("agoonly"):
                P.op("dve", lambda e: e.memset(arF[:, 0:2048], 0.0), writes=[B["Sst"]])
            elif not _DBG.get("cconly"):
                P.op("dve", lambda e: e.tensor_scalar(out=arF[:, 0:2048], in0=arF[:, 0:2048], scalar1=parS[:, 0:1], scalar2=None, op0=ALU.mult),
                     reads=[cst_b], writes=[B["Sst"]])
        for hh in range(16):
            head(hh, True)
        P.dma("sp", "o_sp", sp_out.rearrange("h k v -> k h v"), Sst[:, :, :], reads=[B["Sst"]])

    mixers = _DBG.get("mixers", True)
    if mode == "A":
        stop = "A"
    for li in range(2):
        if mode == "B" and li == 0:
            continue
        if mode != "B":
            ffn(li, 0)
        if stop == ("ffn1", li):
            break
        if mixers:
            if li == 0:
                gmlp()
            else:
                hgrn()
        if stop == ("mix", li) or (mode == "A" and li == 1):
            break
        ffn(li, 1)
        if stop == ("ffn2", li):
            break
        ple(li)
        if stop == ("ple", li):
            break

    P.phase()
    if stop is None:
        rmsnorm_to(V_FIN, lambda c: (h[:, c, :], [h_b[c]]), arF[:, 0:NT])
    P.dma("sp", "fin", yT.rearrange("(c p) n -> p c n", p=128), h[:], reads=h_b)
    for ch in [c for c in P.cnt if c.startswith("fin") or c.startswith("o_")]:
        if P.cnt[ch] > 0:
            nc.sync.wait_ge(P.sems[ch], P.cnt[ch])


def _masks():
    m = np.zeros((128, 8, 128), np.float32)
    s = np.arange(128)[:, None]
    t = np.arange(128)[None, :]
    same64 = (s // 64) == (t // 64)
    m[:, 0, :] = (same64 & (s <= t))
    m[:, 1, :] = (same64 & (s > t))
    same4 = (s // 4) == (t // 4)
    m[:, 2, :] = (same4 & (s <= t))
    m[:, 3, :] = (same4 & (s > t))
    m[:, 4, 0:16] = (s // 4 == np.arange(16)[None, :]) & (s < 64)
    m[:, 5, :] = (s <= t)
    m[:, 6, 0] = (np.arange(128) < 64)
    m[:, 6, 1] = (np.arange(128) >= 64)
    return m


def _prep_shared(inp):
    sh = {}
    for which, (kin, kout) in enumerate((("ffn1_w_in", "ffn1_w_out"), ("ffn2_w_in", "ffn2_w_out"))):
        wi = np.asarray(inp[kin])
        g = wi[:, :, :DFF].reshape(2, KC, 128, NJ, 128)
        u = wi[:, :, DFF:].reshape(2, KC, 128, NJ, 128)
        blk = np.stack([g, u], axis=4)
        blk = blk.transpose(0, 3, 2, 1, 4, 5)
        sh[f"w1_{which}"] = np.ascontiguousarray(blk).reshape(2, NJ, 128, KC * 256)
        wo = np.asarray(inp[kout])
        b = wo.reshape(2, NQ, JQ, 128, 8, 256).transpose(0, 1, 4, 3, 2, 5)
        sh[f"w2_{which}"] = np.ascontiguousarray(b).reshape(2, NQ, 8, 128, JQ * 256)
    sh["a_w_in"] = np.ascontiguousarray(inp["a_w_in"][0])
    sh["a_w_out"] = np.ascontiguousarray(inp["a_w_out"][0])
    sh["b_w_in"] = np.ascontiguousarray(inp["b_w_in"][0])
    sh["b_w_out"] = np.ascontiguousarray(inp["b_w_out"][0])
    sh["ple_wp"] = np.ascontiguousarray(inp["ple_w_proj"])
    sh["ple_wg"] = np.ascontiguousarray(inp["ple_w_gate"])
    vecs = [inp["norm_ffn1"][0], inp["norm_ffn1"][1], inp["norm_mix"][0], inp["norm_mix"][1],
            inp["norm_ffn2"][0], inp["norm_ffn2"][1], inp["norm_ple"][0], inp["norm_ple"][1],
            inp["final_norm"], inp["b_norm_g"][0], inp["b_lb_logits"][0], inp["b_lb_logits"][1]]
    v = np.stack([np.asarray(x, np.float32).reshape(KC, 128).T for x in vecs], axis=1)
    sh["vfm"] = np.ascontiguousarray(v)
    sh["cmask"] = _masks()
    gb = np.stack([np.asarray(inp["a_ln_g"][0]), np.asarray(inp["a_ln_b"][0])], 0)
    sh["a_gb"] = np.ascontiguousarray(np.broadcast_to(gb[None], (128, 2, D))).astype(np.float32)
    ws = np.asarray(inp["a_w_s"][0])
    sh["a_wsT"] = np.ascontiguousarray(ws.transpose(2, 0, 1))
    i4 = np.arange(64) % 4
    sh["a_wsS"] = np.ascontiguousarray(ws[:, i4[None, :], i4[:, None]].transpose(1, 0, 2))
    bs = np.asarray(inp["a_b_s"][0])
    sh["a_bs"] = np.ascontiguousarray(bs)
    sh["a_bsS"] = np.ascontiguousarray(bs[:, i4])
    sel = np.zeros((128, 16, 128), np.float32)
    for g_ in range(16):
        sel[g_, g_, :] = 1.0
    sh["sel16"] = sel
    lb = np.asarray(inp["b_lb_logits"])
    sh["lb_bc"] = np.ascontiguousarray(np.broadcast_to(lb[None], (128, 2, D))).astype(np.float32)
    return sh


def _prep_core(inp, c):
    s, half = c // 2, c % 2
    xp = np.asarray(inp["x_prompt"])[s, half * NPR:(half + 1) * NPR]
    xs = np.asarray(inp["x_sample"])[16 * c:16 * c + 16].reshape(NSM, D)
    m = {"xT": np.ascontiguousarray(np.concatenate([xp, xs], 0).T)}
    pp = np.asarray(inp["p_prompt"])[:, s, half * NPR:(half + 1) * NPR]
    psm = np.asarray(inp["p_sample"])[:, 16 * c:16 * c + 16].reshape(2, NSM, DPLE)
    m["pT"] = np.ascontiguousarray(np.concatenate([pp, psm], 1).transpose(0, 2, 1))
    m["s0"] = np.ascontiguousarray(np.asarray(inp["state_hgrn"])[0, 16 * c:16 * c + 16])
    m["par"] = np.full((128, 8), float(half), np.float32)
    return m


_CACHE = {}


def kernel(**inputs):
    if "nc" not in _CACHE:
        if TWO_LAUNCH:
            _CACHE["two"] = True
            _CACHE["nc"] = build_program("A")
            _CACHE["ncB"] = build_program("B")
        else:
            _CACHE["nc"] = build_program()
    nc = _CACHE["nc"]
    sh = _prep_shared(inputs)
    names = set()
    in_maps = []
    for c in range(NCORES):
        m = dict(sh)
        m.update(_prep_core(inputs, c))
        in_maps.append(m)
    res = run_bass_kernel_spmd(nc, in_maps, core_ids=list(range(NCORES)))
    R = res.results
    if _CACHE.get("two"):
        RA = R
        ncB = _CACHE["ncB"]
        for c in range(NCORES):
            in_maps[c]["xT"] = np.ascontiguousarray(RA[c]["yT"])
            in_maps[c]["s_in"] = np.ascontiguousarray(RA[c - 1]["sp_out"]) if c % 2 == 1 else np.zeros((16, 128, 128), np.float32)
        R = run_bass_kernel_spmd(ncB, in_maps, core_ids=list(range(NCORES))).results
        for c in range(NCORES):
            R[c]["vs_out"] = RA[c]["vs_out"]
    y_prompt = np.zeros((4, 2048, D), np.float32)
    y_sample = np.zeros((128, 4, D), np.float32)
    sp = np.zeros((1, 4, 16, 128, 128), np.float32)
    ss = np.zeros((1, 128, 16, 128, 128), np.float32)
    vs = np.zeros((1, 128, 4, D), np.float32)
    for c in range(NCORES):
        s, half = c // 2, c % 2
        y = R[c]["yT"].T
        y_prompt[s, half * NPR:(half + 1) * NPR] = y[:NPR]
        y_sample[16 * c:16 * c + 16] = y[NPR:].reshape(16, 4, D)
        if half == 1:
            sp[0, s] = R[c]["sp_out"]
        ss[0, 16 * c:16 * c + 16] = R[c]["ss_out"]
        vs[0, 16 * c:16 * c + 16] = R[c]["vs_out"].reshape(16, 4, D)
    return (y_prompt, y_sample, sp, ss, vs)
```

```python
import numpy as np
from contextlib import ExitStack
import ml_dtypes
import concourse.bass as bass
import concourse.mybir as mybir
from concourse.bass_utils import run_bass_kernel_spmd

F32 = mybir.dt.float32
BF16 = mybir.dt.bfloat16
AF = mybir.ActivationFunctionType
ALU = mybir.AluOpType

NCORES = 8
D = 2048
KC = 16
NT = 1088
NPR = 1024
NSM = 64
DFF = 5632
NJ = 44
NQ = 4
JQ = 11
DPLE = 256
EPS = 1e-6
TBS = [(0, 512), (512, 512), (1024, 64)]
NTILE = 9
RING = 3
SLOT = 4096

V_NF1, V_NMIX, V_NF2, V_NPLE = 0, 2, 4, 6
V_FIN, V_BNG, V_L0, V_L1 = 8, 9, 10, 11
NV = 12

_DBG = {"stop": None}
TWO_LAUNCH = False


class Buf:
    __slots__ = ("w", "r")

    def __init__(self):
        self.w = None
        self.r = {}


class Prog:
    def __init__(self, nc, es):
        self.nc = nc
        self.es = es
        self.eng = {"pe": nc.tensor, "act": nc.scalar, "dve": nc.vector, "pool": nc.gpsimd, "sp": nc.sync}
        self.sems = {}
        self.cnt = {}
        self.seen = {}
        self.nsig = 0
        self.pending = {}
        self.T = []
        for e in self.eng:
            self._mk(e)

    def phase(self):
        self.T = [(k, v) for k, v in self.cnt.items() if v > 0]
        self.pending = {e: True for e in self.eng}

    def _mk(self, name):
        self.sems[name] = self.es.enter_context(self.nc.semaphore("s_" + name))
        self.cnt[name] = 0

    def chan(self, name):
        if name not in self.sems:
            self._mk(name)
        return name

    def wait(self, eng, t):
        if t is None:
            return
        p, n = t
        if p == eng and eng == "pe":
            return
        k = (eng, p)
        if self.seen.get(k, 0) >= n:
            return
        self.seen[k] = n
        self.eng[eng].wait_ge(self.sems[p], n)

    def _deps(self, eng, reads, writes, extra, skip_phase=False):
        if self.pending.get(eng) and not skip_phase:
            self.pending[eng] = False
            for t in self.T:
                self.wait(eng, t)
        for t in extra:
            self.wait(eng, t)
        for b in reads:
            self.wait(eng, b.w)
        for b in writes:
            self.wait(eng, b.w)
            for t in list(b.r.values()):
                self.wait(eng, t)

    def _commit(self, t, reads, writes):
        for b in reads:
            b.r[t[0]] = t
        for b in writes:
            b.w = t
            b.r = {}

    def op(self, eng, fn, reads=(), writes=(), extra=()):
        self._deps(eng, reads, writes, extra)
        ins = fn(self.eng[eng])
        self.cnt[eng] += 1
        ins.then_inc(self.sems[eng], 1)
        t = (eng, self.cnt[eng])
        self._commit(t, reads, writes)
        return t

    def dma(self, q, ch, out, in_, reads=(), writes=(), extra=(), n=1, fn=None, skip_phase=False):
        if ch.endswith("*"):
            self.nsig += 1
            ch = ch[:-1] + str(self.nsig)
        self.chan(ch)
        self._deps(q, reads, writes, extra, skip_phase)
        e = self.eng[q]
        if fn is None:
            e.dma_start(out=out, in_=in_).then_inc(self.sems[ch], 16)
        else:
            n = fn(e, self.sems[ch])
        self.cnt[ch] += 16 * n
        t = (ch, self.cnt[ch])
        self._commit(t, reads, writes)
        return t


def build_program(mode="fused"):
    nc = bass.Bass("TRN2", target_bir_lowering=False)
    es = ExitStack()
    with es:
        _build(nc, es, mode)
    return nc


def _build(nc, es, mode="fused"):
    P = Prog(nc, es)
    stop = _DBG["stop"]

    def din(name, shape, dt=F32):
        return nc.dram_tensor(name, list(shape), dt, kind="ExternalInput").ap()

    def dout(name, shape, dt=F32):
        return nc.dram_tensor(name, list(shape), dt, kind="ExternalOutput").ap()

    xT = din("xT", [D, NT])
    pT = din("pT", [2, DPLE, NT])
    s0 = din("s0", [16, 16, 128, 128])
    vfm = din("vfm", [128, NV, KC])
    cmask = din("cmask", [128, 8, 128])
    par = din("par", [128, 8])
    w1 = [din(f"w1_{i}", [2, NJ, 128, KC * 256]) for i in range(2)]
    w2 = [din(f"w2_{i}", [2, NQ, 8, 128, JQ * 256]) for i in range(2)]
    a_w_in = din("a_w_in", [D, 2 * D])
    a_w_out = din("a_w_out", [D, D])
    b_w_in = din("b_w_in", [D, 4 * D])
    b_w_out = din("b_w_out", [D, D])
    ple_wp = din("ple_wp", [2, DPLE, D])
    ple_wg = din("ple_wg", [2, D, D])
    a_gb = din("a_gb", [128, 2, D])
    a_wsT = din("a_wsT", [128, 16, 128])
    a_wsS = din("a_wsS", [64, 16, 64])
    a_bs = din("a_bs", [16, 128])
    a_bsS = din("a_bsS", [16, 64])
    sel16 = din("sel16", [128, 16, 128])
    lb_bc = din("lb_bc", [128, 2, D])
    s_in = din("s_in", [16, 128, 128]) if mode == "B" else None

    ag_in = nc.dram_tensor("ag_in", [128, 2048], F32).ap()
    ag_out = nc.dram_tensor("ag_out", [256, 2048], F32).ap()
    yT = dout("yT", [D, NT])
    sp_out = dout("sp_out", [16, 128, 128])
    ss_out = dout("ss_out", [16, 16, 128, 128])
    vs_out = dout("vs_out", [NSM, D])

    def sb(name, shape, dt):
        return es.enter_context(nc.sbuf_tensor(name, list(shape), dt))

    h = sb("h", [128, KC, NT], F32)
    xn = sb("xn", [128, KC, NT], BF16)
    ring = sb("ring", [128, RING, SLOT], BF16)
    vf = sb("vf", [128, NV, KC], F32)
    cm = sb("cm", [128, 8, 128], F32)
    cmb = sb("cmb", [128, 8, 128], BF16)
    onesb = sb("onesb", [128, 128], BF16)
    parS = sb("parS", [128, 8], F32)
    epsS = sb("epsS", [128, 1], F32)
    ARF = 10432
    ARB = 16768
    arF = sb("arF", [128, ARF], F32)
    arB = sb("arB", [128, ARB], BF16)
    ps = es.enter_context(nc.psum_tensor("ps", [128, 8 * 512], F32))

    h_b = [Buf() for _ in range(KC)]
    xn_b = [Buf() for _ in range(KC)]
    ring_b = [Buf() for _ in range(RING)]
    bank_b = [Buf() for _ in range(8)]
    cst_b = Buf()
    st = {"ring": 0, "bank": 0}

    def bank():
        i = st["bank"] % 7
        st["bank"] += 1
        return i, ps[:, i * 512:(i + 1) * 512], bank_b[i]

    def aux_bank():
        return 7, ps[:, 7 * 512:8 * 512], bank_b[7]

    def wslot():
        i = st["ring"] % RING
        st["ring"] += 1
        return i, ring_b[i]

    def wload(srcs, kc, ncol):
        i, rb = wslot()
        view = ring[:, i, 0:kc * ncol].rearrange("p (k n) -> p k n", k=kc)

        def fn(e, sem):
            for (src, off, w) in srcs:
                e.dma_start(out=view[:, :, off:off + w], in_=src).then_inc(sem, 16)
            return len(srcs)
        P.dma("pool", f"ring{i}", None, None, writes=[rb], fn=fn, skip_phase=True)
        return view, rb

    P.dma("sp", "init*", h[:], xT.rearrange("(c p) n -> p c n", p=128), writes=h_b)
    P.dma("sp", "init*", vf[:], vfm, writes=[cst_b])
    P.dma("sp", "init*", cm[:], cmask, writes=[cst_b])
    P.dma("sp", "init*", parS[:], par, writes=[cst_b])
    P.op("dve", lambda e: e.tensor_copy(out=cmb[:], in_=cm[:]), reads=[cst_b], writes=[cst_b])
    P.op("dve", lambda e: e.memset(onesb[:], 1.0), writes=[cst_b])
    P.op("dve", lambda e: e.memset(epsS[:], EPS), writes=[cst_b])
    TRI, RM, TRI4, RM4, OH = 0, 1, 2, 3, 4
    if _DBG.get("earlycc"):
        ncr = _DBG.get("ncores", NCORES)
        groups0 = [[2 * i, 2 * i + 1] for i in range(ncr // 2)]
        ag0_in = nc.dram_tensor("ag0_in", [128, 1024], F32).ap()
        ag0_out = nc.dram_tensor("ag0_out", [256, 1024], F32).ap()
        P.dma("pool", "agi0", ag0_in, cm[:, :, :].rearrange("p a b -> p (a b)"), reads=[cst_b])
        P.chan("agc0")
        P.wait("pool", ("agi0", 16))
        nc.gpsimd.collective_compute("AllGather", ALU.bypass, replica_groups=groups0, ins=[ag0_in.opt()], outs=[ag0_out.opt()]).then_inc(P.sems["agc0"], 1)
        nc.gpsimd.wait_ge(P.sems["agc0"], 1)

    def rmsnorm_to(vidx, dst_fn, tmpF):
        for c in range(KC):
            P.op("act", lambda e, c=c: e.activation(out=xn[:, c, :], in_=h[:, c, :], func=AF.Square),
                 reads=[h_b[c]], writes=[xn_b[c]])
        rstd = tmpF
        rb_ = Buf()
        for (t0, tn) in TBS:
            bi, bap, bb = bank()

            def mm(e, t0=t0, tn=tn, bap=bap):
                ins = None
                for c in range(KC):
                    ins = e.matmul(bap[:, 0:tn], lhsT=onesb[:], rhs=xn[:, c, t0:t0 + tn], start=(c == 0), stop=(c == KC - 1))
                return ins
            P.op("pe", mm, reads=xn_b + [cst_b], writes=[bb])
            P.op("act", lambda e, t0=t0, tn=tn, bap=bap: e.activation(out=rstd[:, t0:t0 + tn], in_=bap[:, 0:tn], func=AF.Sqrt,
                                                                  scale=1.0 / D, bias=epsS[:, 0:1]),
                 reads=[bb, cst_b], writes=[rb_])
        P.op("dve", lambda e: e.reciprocal(out=rstd[:, :], in_=rstd[:, :]), reads=[rb_], writes=[rb_])
        for c in range(KC):
            o_ap, obufs = dst_fn(c)
            P.op("dve", lambda e, c=c, o_ap=o_ap: e.scalar_tensor_tensor(out=o_ap, in0=h[:, c, :], scalar=vf[:, vidx, c:c + 1],
                                                                        in1=rstd[:, :], op0=ALU.mult, op1=ALU.mult),
                 reads=[h_b[c], rb_, cst_b], writes=obufs)

    def norm_xn(vidx, tmpF):
        rmsnorm_to(vidx, lambda c: (xn[:, c, :], [xn_b[c]]), tmpF)

    def ffn(li, which):
        w_in = w1[which]
        w_out = w2[which]
        vidx = (V_NF1 if which == 0 else V_NF2) + li
        rstd = arF[:, 0:NT]
        sg = [arF[:, NT + 512 * i: NT + 512 * (i + 1)] for i in range(4)]
        sg_b = [Buf() for _ in range(4)]
        hid = arB[:, 0:JQ * NT].rearrange("p (j n) -> p j n", j=JQ)
        hid_b = [Buf() for _ in range(JQ)]
        P.phase()
        norm_xn(vidx, rstd)
        k = 0
        for q in range(NQ):
            for jj in range(JQ):
                j = q * JQ + jj
                wv, wb = wload([(w_in[li, j].rearrange("p (k n) -> p k n", k=KC), 0, 256)], KC, 256)
                for (t0, tn) in TBS:
                    _, gap, gb = bank()
                    _, uap, ub = bank()

                    def mm(e, wv=wv, t0=t0, tn=tn, gap=gap, uap=uap):
                        ins = None
                        for (ap_, off) in ((gap, 0), (uap, 128)):
                            for c in range(KC):
                                ins = e.matmul(ap_[:, 0:tn], lhsT=wv[:, c, off:off + 128], rhs=xn[:, c, t0:t0 + tn],
                                               start=(c == 0), stop=(c == KC - 1))
                        return ins
                    P.op("pe", mm, reads=xn_b + [wb], writes=[gb, ub])
                    s_i = k % 4
                    k += 1
                    P.op("act", lambda e, s_i=s_i, tn=tn, gap=gap: e.activation(out=sg[s_i][:, 0:tn], in_=gap[:, 0:tn], func=AF.Silu),
                         reads=[gb], writes=[sg_b[s_i]])
                    P.op("dve", lambda e, s_i=s_i, jj=jj, t0=t0, tn=tn, uap=uap: e.tensor_tensor(
                        out=hid[:, jj, t0:t0 + tn], in0=sg[s_i][:, 0:tn], in1=uap[:, 0:tn], op=ALU.mult),
                        reads=[sg_b[s_i], ub], writes=[hid_b[jj]])
            for fp in range(8):
                wv, wb = wload([(w_out[li, q, fp].rearrange("p (k n) -> p k n", k=JQ), 0, 256)], JQ, 256)
                for f2 in range(2):
                    fo = fp * 2 + f2
                    for (t0, tn) in TBS:
                        _, oap, ob = bank()

                        def mm(e, wv=wv, f2=f2, t0=t0, tn=tn, oap=oap):
                            ins = None
                            for jj in range(JQ):
                                ins = e.matmul(oap[:, 0:tn], lhsT=wv[:, jj, f2 * 128:(f2 + 1) * 128], rhs=hid[:, jj, t0:t0 + tn],
                                               start=(jj == 0), stop=(jj == JQ - 1))
                            return ins
                        P.op("pe", mm, reads=hid_b + [wb], writes=[ob])
                        P.op("dve", lambda e, fo=fo, t0=t0, tn=tn, oap=oap: e.scalar_tensor_tensor(
                            out=h[:, fo, t0:t0 + tn], in0=oap[:, 0:tn], scalar=0.5, in1=h[:, fo, t0:t0 + tn],
                            op0=ALU.mult, op1=ALU.add), reads=[ob, h_b[fo]], writes=[h_b[fo]])

    def ple(li):
        rstd = arF[:, 0:NT]
        gt = [arF[:, NT + 512 * i: NT + 512 * (i + 1)] for i in range(4)]
        gt_b = [Buf() for _ in range(4)]
        pb = arB[:, 0:2 * NT].rearrange("p (k n) -> p k n", k=2)
        pb_b = Buf()
        P.phase()
        norm_xn(V_NPLE + li, rstd)
        P.dma("pool", "pld", pb, pT[li].rearrange("(k p) n -> p k n", p=128), writes=[pb_b])
        k = 0
        for f8 in range(8):
            wv, wb = wload([(ple_wg[li, :, f8 * 256:(f8 + 1) * 256].rearrange("(k p) n -> p k n", p=128), 0, 256)], KC, 256)
            wpv, wpb = wload([(ple_wp[li, :, f8 * 256:(f8 + 1) * 256].rearrange("(k p) n -> p k n", p=128), 0, 256)], 2, 256)
            for f2 in range(2):
                fo = f8 * 2 + f2
                for (t0, tn) in TBS:
                    _, gap, gb = bank()
                    _, pap, pbk = bank()

                    def mm(e, wv=wv, wpv=wpv, f2=f2, t0=t0, tn=tn, gap=gap, pap=pap):
                        ins = None
                        for c in range(KC):
                            ins = e.matmul(gap[:, 0:tn], lhsT=wv[:, c, f2 * 128:(f2 + 1) * 128], rhs=xn[:, c, t0:t0 + tn],
                                           start=(c == 0), stop=(c == KC - 1))
                        for c in range(2):
                            ins = e.matmul(pap[:, 0:tn], lhsT=wpv[:, c, f2 * 128:(f2 + 1) * 128], rhs=pb[:, c, t0:t0 + tn],
                                           start=(c == 0), stop=(c == 1))
                        return ins
                    P.op("pe", mm, reads=xn_b + [wb, wpb, pb_b], writes=[gb, pbk])
                    s_i = k % 4
                    k += 1
                    P.op("act", lambda e, s_i=s_i, tn=tn, gap=gap: e.activation(out=gt[s_i][:, 0:tn], in_=gap[:, 0:tn], func=AF.Sigmoid),
                         reads=[gb], writes=[gt_b[s_i]])
                    P.op("dve", lambda e, s_i=s_i, tn=tn, pap=pap: e.tensor_tensor(
                        out=gt[s_i][:, 0:tn], in0=gt[s_i][:, 0:tn], in1=pap[:, 0:tn], op=ALU.mult),
                        reads=[gt_b[s_i], pbk], writes=[gt_b[s_i]])
                    P.op("dve", lambda e, s_i=s_i, fo=fo, t0=t0, tn=tn: e.tensor_tensor(
                        out=h[:, fo, t0:t0 + tn], in0=h[:, fo, t0:t0 + tn], in1=gt[s_i][:, 0:tn], op=ALU.add),
                        reads=[gt_b[s_i], h_b[fo]], writes=[h_b[fo]])

    def out_accum(wv, wb, act_ap_fn, act_bufs, tbs):
        for fo in range(KC):
            for (t0, tn) in tbs:
                _, oap, ob = bank()
                P.op("pe", lambda e, fo=fo, t0=t0, tn=tn, oap=oap: e.matmul(
                    oap[:, 0:tn], lhsT=wv[:, 0, fo * 128:(fo + 1) * 128], rhs=act_ap_fn(t0, tn), start=True, stop=True),
                    reads=act_bufs + [wb], writes=[ob])
                P.op("dve", lambda e, fo=fo, t0=t0, tn=tn, oap=oap: e.tensor_tensor(
                    out=h[:, fo, t0:t0 + tn], in0=h[:, fo, t0:t0 + tn], in1=oap[:, 0:tn], op=ALU.add),
                    reads=[ob, h_b[fo]], writes=[h_b[fo]])


    def gmlp():
        AX = mybir.AxisListType
        gbc = arF[:, 0:2048]
        bbc = arF[:, 2048:4096]
        vS = arF[:, 4096:6144]
        vfx = [arF[:, 6144 + 512 * i: 6144 + 512 * (i + 1)] for i in range(2)]
        ugx = [arF[:, 7168 + 512 * i: 7168 + 512 * (i + 1)] for i in range(2)]
        stt = arF[:, 8192:8192 + 512]
        sum1 = stt[:, 0:72].rearrange("p (t c) -> p t c", t=NTILE)
        sum2 = stt[:, 72:144].rearrange("p (t c) -> p t c", t=NTILE)
        sm = stt[:, 144:144 + 8 * NTILE].rearrange("p (t c) -> p t c", t=NTILE)
        junk = arF[:, 8704:8704 + 256]
        bsF = arF[:, 9216:9216 + 192]
        vT = arB[:, 0:10240].rearrange("p (t n) -> p t n", t=5)
        WsM = arB[:, 10240:12288].rearrange("p (g t) -> p g t", g=16)
        WsS = arB[:, 12288:13312].rearrange("p (g t) -> p g t", g=16)
        usx = [arB[:, 13312 + 512 * i: 13312 + 512 * (i + 1)] for i in range(2)]
        bsH = arB[:, 14336:14464]
        bsL = arB[:, 14464:14592]
        bsSH = arB[:, 14592:14656]
        bsSL = arB[:, 14656:14720]
        SEL = arB[:, 14720:16768].rearrange("p (g t) -> p g t", g=16)
        gb_b, vS_b, stt_b, ws_b, bs_b = Buf(), Buf(), Buf(), Buf(), Buf()
        vfx_b = [Buf(), Buf()]
        ugx_b = [Buf(), Buf()]
        usx_b = [Buf(), Buf()]
        junk_b = Buf()
        vT_b = [Buf() for _ in range(5)]
        rstd = arF[:, 4096:4096 + NT]
        P.phase()
        norm_xn(V_NMIX + 0, rstd)
        P.dma("sp", "gml*", gbc, a_gb[:, 0, :], writes=[gb_b])
        P.dma("sp", "gml*", bbc, a_gb[:, 1, :], writes=[gb_b])
        P.dma("sp", "gml*", bsF[0:16, 0:128], a_bs, writes=[bs_b])
        P.dma("sp", "gml*", bsF[0:16, 128:192], a_bsS, writes=[bs_b])
        P.dma("pool", "gmlp*", WsM[:, :, :], a_wsT, writes=[ws_b])
        P.dma("pool", "gmlp*", WsS[0:64, :, :], a_wsS, writes=[ws_b])
        P.dma("pool", "gmlp*", SEL[:, :, :], sel16, writes=[ws_b])
        for g in range(16):
            P.op("dve", lambda e, g=g: e.tensor_tensor(out=WsM[:, g, :], in0=WsM[:, g, :], in1=cmb[:, 5, :], op=ALU.mult),
                 reads=[cst_b], writes=[ws_b])
            P.op("dve", lambda e, g=g: e.tensor_tensor(out=WsS[0:64, g, :], in0=WsS[0:64, g, :], in1=cmb[0:64, 2, 0:64], op=ALU.mult),
                 reads=[cst_b], writes=[ws_b])
        P.op("dve", lambda e: e.memset(arB[:, 14336:14720], 0.0), writes=[ws_b])
        P.op("dve", lambda e: e.tensor_copy(out=bsH[0:16, :], in_=bsF[0:16, 0:128]), reads=[bs_b], writes=[ws_b])
        P.op("dve", lambda e: e.tensor_tensor(out=bsL[0:16, :], in0=bsF[0:16, 0:128], in1=bsH[0:16, :], op=ALU.subtract), reads=[bs_b], writes=[ws_b])
        P.op("dve", lambda e: e.tensor_copy(out=bsSH[0:16, :], in_=bsF[0:16, 128:192]), reads=[bs_b], writes=[ws_b])
        P.op("dve", lambda e: e.tensor_tensor(out=bsSL[0:16, :], in0=bsF[0:16, 128:192], in1=bsSH[0:16, :], op=ALU.subtract), reads=[bs_b], writes=[ws_b])

        for (tiles, tbs) in (([0, 1, 2, 3], [(0, 512)]), ([4, 5, 6, 7, 8], [(512, 512), (1024, 64)])):
            k = 0
            for cb in range(8):
                wv, wb = wload([(a_w_in[:, D + cb * 256: D + (cb + 1) * 256].rearrange("(k p) n -> p k n", p=128), 0, 256)], KC, 256)
                for lt, tt in enumerate(tiles):
                    rows = 64 if tt == 8 else 128
                    _, bap, bb = bank()

                    def mm(e, wv=wv, tt=tt, rows=rows, bap=bap):
                        ins = None
                        for c in range(KC):
                            ins = e.matmul(bap[0:rows, 0:256], lhsT=xn[:, c, tt * 128: tt * 128 + rows], rhs=wv[:, c, 0:256],
                                           start=(c == 0), stop=(c == KC - 1))
                        return ins
                    P.op("pe", mm, reads=xn_b + [wb], writes=[bb])
                    i = k % 2
                    k += 1
                    P.op("act", lambda e, i=i, rows=rows, bap=bap, tt=tt, cb=cb: e.activation(
                        out=vfx[i][0:rows, 0:256], in_=bap[0:rows, 0:256], func=AF.Gelu, accum_out=sum1[0:rows, tt, cb:cb + 1]),
                        reads=[bb], writes=[vfx_b[i], stt_b])
                    P.op("dve", lambda e, i=i, rows=rows, tt=tt, cb=cb: e.scalar_tensor_tensor(
                        out=junk[0:rows, 0:256], in0=vfx[i][0:rows, 0:256], scalar=1.0, in1=vfx[i][0:rows, 0:256], op0=ALU.mult, op1=ALU.mult,
                        accum_out=sum2[0:rows, tt, cb:cb + 1]), reads=[vfx_b[i]], writes=[junk_b, stt_b])
                    if tt == 8:
                        P.op("dve", lambda e, i=i, cb=cb: e.tensor_copy(out=vS[0:64, cb * 256:(cb + 1) * 256], in_=vfx[i][0:64, 0:256]),
                             reads=[vfx_b[i]], writes=[vS_b])
                    else:
                        P.op("dve", lambda e, i=i, lt=lt, cb=cb: e.tensor_copy(out=vT[:, lt, cb * 256:(cb + 1) * 256], in_=vfx[i][:, 0:256]),
                             reads=[vfx_b[i]], writes=[vT_b[lt]])
            for lt, tt in sorted(enumerate(tiles), key=lambda x: -x[1]):
                rows = 64 if tt == 8 else 128
                S = lambda j, tt=tt, rows=rows: sm[0:rows, tt, j:j + 1]
                P.op("dve", lambda e, tt=tt, rows=rows, S=S: e.tensor_reduce(out=S(0), in_=sum1[0:rows, tt, :], axis=AX.X, op=ALU.add),
                     reads=[stt_b], writes=[stt_b])
                P.op("dve", lambda e, tt=tt, rows=rows, S=S: e.tensor_reduce(out=S(1), in_=sum2[0:rows, tt, :], axis=AX.X, op=ALU.add),
                     reads=[stt_b], writes=[stt_b])
                P.op("dve", lambda e, S=S: e.tensor_scalar(out=S(2), in0=S(0), scalar1=1.0 / D, scalar2=None, op0=ALU.mult), reads=[stt_b], writes=[stt_b])
                P.op("dve", lambda e, S=S: e.tensor_tensor(out=S(3), in0=S(2), in1=S(2), op=ALU.mult), reads=[stt_b], writes=[stt_b])
                P.op("dve", lambda e, S=S: e.scalar_tensor_tensor(out=S(4), in0=S(1), scalar=1.0 / D, in1=S(3), op0=ALU.mult, op1=ALU.subtract),
                     reads=[stt_b], writes=[stt_b])
                P.op("act", lambda e, S=S, rows=rows: e.activation(out=S(5), in_=S(4), func=AF.Sqrt, bias=epsS[0:rows, 0:1]),
                     reads=[stt_b, cst_b], writes=[stt_b])
                P.op("dve", lambda e, S=S: e.reciprocal(out=S(5), in_=S(5)), reads=[stt_b], writes=[stt_b])
                P.op("dve", lambda e, S=S: e.scalar_tensor_tensor(out=S(6), in0=S(2), scalar=-1.0, in1=S(5), op0=ALU.mult, op1=ALU.mult),
                     reads=[stt_b], writes=[stt_b])
                if tt == 8:
                    P.op("act", lambda e, S=S: e.activation(out=vS[0:64, :], in_=vS[0:64, :], func=AF.Identity, scale=S(5), bias=S(6)),
                         reads=[stt_b], writes=[vS_b])
                    P.op("dve", lambda e: e.tensor_tensor(out=vS[0:64, :], in0=vS[0:64, :], in1=gbc[0:64, :], op=ALU.mult), reads=[gb_b], writes=[vS_b])
                    P.op("dve", lambda e: e.tensor_tensor(out=vS[0:64, :], in0=vS[0:64, :], in1=bbc[0:64, :], op=ALU.add), reads=[gb_b], writes=[vS_b])
                    P.op("dve", lambda e, lt=lt: e.tensor_copy(out=vT[0:64, lt, :], in_=vS[0:64, :]), reads=[vS_b], writes=[vT_b[lt]])
                    P.dma("sp", "o_vs", vs_out, vS[0:64, :], reads=[vS_b])
                else:
                    P.op("act", lambda e, S=S, lt=lt: e.activation(out=vS[:, :], in_=vT[:, lt, :], func=AF.Identity, scale=S(5), bias=S(6)),
                         reads=[stt_b, vT_b[lt]], writes=[vS_b])
                    P.op("dve", lambda e: e.tensor_tensor(out=vS[:, :], in0=vS[:, :], in1=gbc, op=ALU.mult), reads=[gb_b], writes=[vS_b])
                    P.op("dve", lambda e, lt=lt: e.tensor_tensor(out=vT[:, lt, :], in0=vS[:, :], in1=bbc, op=ALU.add), reads=[gb_b, vS_b], writes=[vT_b[lt]])
            k = 0
            for gp in range(8):
                wv, wb = wload([(a_w_in[:, gp * 256:(gp + 1) * 256].rearrange("(k p) n -> p k n", p=128), 0, 256)], KC, 256)
                for g2 in range(2):
                    g = gp * 2 + g2
                    wo, wob = wload([(a_w_out[g * 128:(g + 1) * 128, :].rearrange("(k p) n -> p k n", p=128), 0, D)], 1, D)
                    for (t0, tn) in tbs:
                        _, uap, ub = bank()
                        _, sap, sbk = bank()

                        def mm(e, wv=wv, g2=g2, g=g, t0=t0, tn=tn, uap=uap, sap=sap, tiles=tiles):
                            ins = None
                            for c in range(KC):
                                ins = e.matmul(uap[:, 0:tn], lhsT=wv[:, c, g2 * 128:(g2 + 1) * 128], rhs=xn[:, c, t0:t0 + tn],
                                               start=(c == 0), stop=(c == KC - 1))
                            if tn == 64:
                                lt = tiles.index(8)
                                ins = e.matmul(sap[:, 0:64], lhsT=vT[0:64, lt, g * 128:(g + 1) * 128], rhs=WsS[0:64, g, :], start=True, stop=_DBG.get("nobias", False))
                                if not _DBG.get("nobias"):
                                    e.matmul(sap[:, 0:64], lhsT=SEL[:, g, :], rhs=bsSH[:, :], start=False, stop=False)
                                    ins = e.matmul(sap[:, 0:64], lhsT=SEL[:, g, :], rhs=bsSL[:, :], start=False, stop=True)
                            else:
                                for tt in range(t0 // 128, (t0 + tn) // 128):
                                    lt = tiles.index(tt)
                                    o = tt * 128 - t0
                                    ins = e.matmul(sap[:, o:o + 128], lhsT=vT[:, lt, g * 128:(g + 1) * 128], rhs=WsM[:, g, :], start=True, stop=_DBG.get("nobias", False))
                                    if not _DBG.get("nobias"):
                                        e.matmul(sap[:, o:o + 128], lhsT=SEL[:, g, :], rhs=bsH[:, :], start=False, stop=False)
                                        ins = e.matmul(sap[:, o:o + 128], lhsT=SEL[:, g, :], rhs=bsL[:, :], start=False, stop=True)
                            return ins
                        P.op("pe", mm, reads=xn_b + vT_b + [wb, ws_b], writes=[ub, sbk])
                        i = k % 2
                        k += 1
                        P.op("act", lambda e, i=i, tn=tn, uap=uap: e.activation(out=ugx[i][:, 0:tn], in_=uap[:, 0:tn], func=AF.Gelu),
                             reads=[ub], writes=[ugx_b[i]])
                        P.op("dve", lambda e, i=i, tn=tn, sap=sap: e.tensor_tensor(out=usx[i][:, 0:tn], in0=ugx[i][:, 0:tn], in1=sap[:, 0:tn], op=ALU.mult),
                             reads=[ugx_b[i], sbk], writes=[usx_b[i]])
                        out_accum(wo, wob, lambda a, n, i=i: usx[i][:, 0:n], [usx_b[i]], [(t0, tn)])


    def hgrn():
        Sst = arF[:, 0:2048].rearrange("p (h v) -> p h v", h=16)
        S0 = arF[:, 2048:4096].rearrange("p (s v) -> p s v", s=16)
        qs = arF[:, 4096:4608]
        kF = arF[:, 4608:5120]
        Ef = arF[:, 5120:5632]
        tF = arF[:, 5632:6144]
        rst = arF[:, 6144:6656]
        logfT = arF[:, 6656:7808].rearrange("p (t k) -> p t k", t=NTILE)
        t1x = [arF[:, 7808:8064], arF[:, 8064:8320]]
        kTx = [arF[:, 8320:8576], arF[:, 8576:8832]]
        edTx = [arF[:, 8832:9088], arF[:, 9088:9344]]
        lbh = arF[:, 9344:9600]
        omlh = arF[:, 9600:9856]
        l01 = arF[:, 9856:10368]
        Dl = arF[:, 10368:10400]
        omlFM = arF[:, 10400:10416]
        qt = arB[:, 0:1088]
        kt = arB[:, 1088:2176]
        sg = arB[:, 2176:3264]
        osq = arB[:, 3264:4352]
        ot = arB[:, 4352:5440]
        kdec = arB[:, 5440:6592].rearrange("p (t k) -> p t k", t=NTILE)
        kdecB = arB[:, 13248:14400].rearrange("p (t k) -> p t k", t=NTILE)
        vT = arB[:, 6592:7744].rearrange("p (t k) -> p t k", t=NTILE)
        attm = arB[:, 7744:8896].rearrange("p (t k) -> p t k", t=NTILE)
        Sbf = arB[:, 8896:10944].rearrange("p (c v) -> p c v", c=16)
        S0bf = arB[:, 10944:12992].rearrange("p (s v) -> p s v", s=16)
        vmk = [arB[:, 12992 + 128 * i: 12992 + 128 * (i + 1)] for i in range(2)]
        B = {k: Buf() for k in ("Sst", "S0", "qs", "kF", "Ef", "tF", "rst", "logf", "t1", "kT", "edT", "lb", "l01", "Dl", "oml",
                                "qt", "kt", "sg", "osq", "ot", "kdec", "vT", "attm", "Sbf", "S0bf", "vmk0", "vmk1",
                                "t1_0", "t1_1", "kT_0", "kT_1", "edT_0", "edT_1")}
        rstd = arF[:, 4096:4096 + NT]
        P.phase()
        norm_xn(V_NMIX + 1, rstd)
        P.op("dve", lambda e: e.tensor_tensor(out=omlFM, in0=vf[:, V_L0, :], in1=vf[:, V_L1, :], op=ALU.subtract), reads=[cst_b], writes=[B["oml"]])
        P.op("act", lambda e: e.activation(out=omlFM, in_=omlFM, func=AF.Sigmoid), writes=[B["oml"]])
        P.op("dve", lambda e: e.memset(arF[:, 0:2048], 0.0), writes=[B["Sst"]])
        P.op("dve", lambda e: e.memset(arB[:, 0:14400], 0.0), writes=[B[k_] for k_ in ("kdec", "vT", "attm", "vmk0", "vmk1", "Sbf", "S0bf", "qt", "kt")])

        def head(hh, full):
            c0 = hh * 128
            wB, wBb = wload([(b_w_in[:, D + c0:D + c0 + 128].rearrange("(k p) n -> p k n", p=128), 0, 128),
                             (b_w_in[:, 2 * D + c0:2 * D + c0 + 128].rearrange("(k p) n -> p k n", p=128), 128, 128)], KC, 256)
            if full:
                wA, wAb = wload([(b_w_in[:, c0:c0 + 128].rearrange("(k p) n -> p k n", p=128), 0, 128),
                                 (b_w_in[:, 3 * D + c0:3 * D + c0 + 128].rearrange("(k p) n -> p k n", p=128), 128, 128)], KC, 256)
            if full:
                wO, wOb = wload([(b_w_out[c0:c0 + 128, :].rearrange("(k p) n -> p k n", p=128), 0, D)], 1, D)
            P.dma("sp", "lbA", l01[:, 0:128], lb_bc[:, 0, c0:c0 + 128], writes=[B["l01"]])
            P.dma("sp", "lbB", l01[:, 128:256], lb_bc[:, 0, c0:c0 + 128], writes=[B["l01"]])
            P.dma("sp", "lbC", l01[:, 256:384], lb_bc[:, 1, c0:c0 + 128], writes=[B["l01"]])
            P.dma("sp", "lbD", l01[:, 384:512], lb_bc[:, 1, c0:c0 + 128], writes=[B["l01"]])
            for ch_ in ("lbA", "lbB", "lbC", "lbD"):
                P.wait("dve", (ch_, P.cnt[ch_]))
            P.op("dve", lambda e: e.tensor_tensor(out=lbh, in0=l01[:, 0:256], in1=l01[:, 256:512], op=ALU.subtract), reads=[B["l01"]], writes=[B["lb"]])
            P.op("act", lambda e: e.activation(out=lbh, in_=lbh, func=AF.Exp), writes=[B["lb"]])
            P.op("dve", lambda e: e.tensor_scalar(out=lbh, in0=lbh, scalar1=1.0, scalar2=None, op0=ALU.add), writes=[B["lb"]])
            P.op("dve", lambda e: e.reciprocal(out=lbh, in_=lbh), writes=[B["lb"]])
            P.op("dve", lambda e: e.tensor_scalar(out=omlh, in0=lbh, scalar1=-1.0, scalar2=1.0, op0=ALU.mult, op1=ALU.add), reads=[B["lb"]], writes=[B["lb"]])
            tiles = list(range(NTILE)) if full else list(range(8))
            _, blap, blb = aux_bank()
            tgroups = [(0, 2), (2, 2), (4, 2), (6, 2)] + ([(8, 1)] if full else [])
            for gi, (tA, ntl) in enumerate(tgroups):
                rows = 64 if tA == 8 else 128
                W = ntl * 128
                _, bap, bb = bank()

                def mm(e, tA=tA, ntl=ntl, rows=rows, bap=bap):
                    ins = None
                    for i in range(ntl):
                        tt = tA + i
                        for c in range(KC):
                            ins = e.matmul(bap[0:rows, i * 256:(i + 1) * 256], lhsT=xn[:, c, tt * 128: tt * 128 + rows], rhs=wB[:, c, 0:256],
                                           start=(c == 0), stop=(c == KC - 1))
                    return ins
                P.op("pe", mm, reads=xn_b + [wBb], writes=[bb])
                pp = gi % 2
                t1 = t1x[pp]
                kT = kTx[pp]
                edT = edTx[pp]
                Bt1, BkT, Bed = B[f"t1_{pp}"], B[f"kT_{pp}"], B[f"edT_{pp}"]
                bv = bap[0:rows, 0:ntl * 256].rearrange("p (t c) -> p t c", t=ntl)
                t13 = t1[0:rows, 0:W].rearrange("p (t c) -> p t c", t=ntl)
                t_exp = P.op("act", lambda e, bv=bv, t13=t13: e.activation(out=t13, in_=bv[:, :, 0:128], func=AF.Exp, scale=-1.0), reads=[bb], writes=[Bt1])
                P.op("dve", lambda e, rows=rows, bv=bv, tA=tA, ntl=ntl: e.tensor_copy(out=vT[0:rows, tA:tA + ntl, :], in_=bv[:, :, 128:256]), reads=[bb], writes=[B["vT"]], extra=[t_exp])
                P.op("dve", lambda e, rows=rows, t1=t1, W=W: e.tensor_scalar(out=t1[0:rows, 0:W], in0=t1[0:rows, 0:W], scalar1=1.0, scalar2=None, op0=ALU.add), writes=[Bt1])
                P.op("dve", lambda e, rows=rows, t1=t1, W=W: e.reciprocal(out=t1[0:rows, 0:W], in_=t1[0:rows, 0:W]), writes=[Bt1])
                P.op("dve", lambda e, rows=rows, t1=t1, W=W: e.tensor_tensor(out=t1[0:rows, 0:W], in0=t1[0:rows, 0:W], in1=omlh[0:rows, 0:W], op=ALU.mult), reads=[B["lb"]], writes=[Bt1])
                P.op("dve", lambda e, rows=rows, t1=t1, W=W: e.tensor_tensor(out=t1[0:rows, 0:W], in0=t1[0:rows, 0:W], in1=lbh[0:rows, 0:W], op=ALU.add), reads=[B["lb"]], writes=[Bt1])
                P.op("act", lambda e, rows=rows, tA=tA, ntl=ntl, t13=t13: e.activation(out=logfT[0:rows, tA:tA + ntl, :], in_=t13, func=AF.Ln), reads=[Bt1], writes=[B["logf"]])
                P.op("dve", lambda e, rows=rows, t1=t1, kT=kT, W=W: e.tensor_scalar(out=kT[0:rows, 0:W], in0=t1[0:rows, 0:W], scalar1=-1.0, scalar2=1.0, op0=ALU.mult, op1=ALU.add),
                     reads=[Bt1], writes=[BkT])
                _, dap, db = bank()
                mk = RM4 if tA == 8 else RM

                def mmd(e, tA=tA, ntl=ntl, rows=rows, dap=dap, mk=mk):
                    ins = None
                    for i in range(ntl):
                        ins = e.matmul(dap[0:rows, i * 128:(i + 1) * 128], lhsT=cm[0:rows, mk, 0:rows], rhs=logfT[0:rows, tA + i, :], start=True, stop=True)
                    return ins
                P.op("pe", mmd, reads=[B["logf"], cst_b], writes=[db])
                P.op("act", lambda e, rows=rows, dap=dap, edT=edT, W=W: e.activation(out=edT[0:rows, 0:W], in_=dap[0:rows, 0:W], func=AF.Exp), reads=[db], writes=[Bed])
                kT3 = kT[0:rows, 0:W].rearrange("p (t c) -> p t c", t=ntl)
                ed3 = edT[0:rows, 0:W].rearrange("p (t c) -> p t c", t=ntl)
                P.op("dve", lambda e, rows=rows, tA=tA, ntl=ntl, kT3=kT3, ed3=ed3: e.scalar_tensor_tensor(out=kdec[0:rows, tA:tA + ntl, :], in0=kT3, scalar=cm[0:rows, 6, 0:1], in1=ed3,
                                                                                       op0=ALU.mult, op1=ALU.mult), reads=[BkT, Bed, cst_b], writes=[B["kdec"]])
                if tA < 8:
                    P.op("dve", lambda e, tA=tA, ntl=ntl, kT3=kT3, ed3=ed3: e.scalar_tensor_tensor(out=kdecB[:, tA:tA + ntl, :], in0=kT3, scalar=cm[:, 6, 1:2], in1=ed3,
                                                                                   op0=ALU.mult, op1=ALU.mult), reads=[BkT, Bed, cst_b], writes=[B["kdec"]])

                def mmbl(e, tA=tA, ntl=ntl, blap=blap):
                    ins = None
                    if tA == 8:
                        ins = e.matmul(blap[:, 16:32], lhsT=logfT[0:64, 8, :], rhs=cm[0:64, TRI4, 3:64:4], start=True, stop=True)
                    else:
                        for i in range(ntl):
                            tt = tA + i
                            ins = e.matmul(blap[:, 2 * tt:2 * tt + 2], lhsT=logfT[:, tt, :], rhs=cm[:, TRI, 63:128:64], start=True, stop=True)
                    return ins
                P.op("pe", mmbl, reads=[B["logf"], cst_b], writes=[blb])
            nb = 32 if full else 16
            P.op("act", lambda e, blap=blap, nb=nb: e.activation(out=Dl[:, 0:nb], in_=blap[:, 0:nb], func=AF.Exp), reads=[blb], writes=[B["Dl"]])
            for c4 in range(4):
                _, uap, ub = bank()

                def mmu(e, c4=c4, uap=uap):
                    ins = None
                    for ci in range(4):
                        c = c4 * 4 + ci
                        tt, hf = c // 2, c % 2
                        ins = e.matmul(uap[:, ci * 128:(ci + 1) * 128], lhsT=(kdecB if hf else kdec)[:, tt, :], rhs=vT[:, tt, :],
                                       start=True, stop=True)
                    return ins
                P.op("pe", mmu, reads=[B["kdec"], B["vT"]], writes=[ub])
                for ci in range(4):
                    c = c4 * 4 + ci
                    if full:
                        P.op("dve", lambda e, c=c: e.tensor_copy(out=Sbf[:, c, :], in_=Sst[:, hh, :]), reads=[B["Sst"]], writes=[B["Sbf"]])
                    P.op("dve", lambda e, c=c, ci=ci, uap=uap: e.scalar_tensor_tensor(out=Sst[:, hh, :], in0=Sst[:, hh, :], scalar=Dl[:, c:c + 1],
                                                                                  in1=uap[:, ci * 128:(ci + 1) * 128], op0=ALU.mult, op1=ALU.add),
                         reads=[B["Dl"], ub], writes=[B["Sst"]])
            if not full:
                return
            P.dma("sp", "s0ld", S0[:, :, :], s0[:, hh].rearrange("s k v -> k s v"), writes=[B["S0"]])
            P.op("act", lambda e: e.activation(out=S0bf[:, :, :], in_=S0[:, :, :], func=AF.Copy), reads=[B["S0"]], writes=[B["S0bf"]])
            for j4 in range(4):
                _, uap, ub = bank()
                for ji in range(4):
                    j = j4 * 4 + ji
                    i = j % 2
                    P.op("dve", lambda e, i=i, j=j: e.tensor_scalar(out=vmk[i][0:64, :], in0=vT[0:64, 8, :], scalar1=cm[0:64, OH, j:j + 1], scalar2=None, op0=ALU.mult),
                         reads=[B["vT"], cst_b], writes=[B[f"vmk{i}"]])
                    P.op("pe", lambda e, i=i, ji=ji, uap=uap: e.matmul(uap[:, ji * 128:(ji + 1) * 128], lhsT=kdec[:, 8, :], rhs=vmk[i][:, :], start=True, stop=True),
                         reads=[B["kdec"], B[f"vmk{i}"]], writes=[ub])
                for ji in range(4):
                    j = j4 * 4 + ji
                    P.op("dve", lambda e, j=j, ji=ji, uap=uap: e.scalar_tensor_tensor(out=S0[:, j, :], in0=S0[:, j, :], scalar=Dl[:, 16 + j:17 + j],
                                                                                  in1=uap[:, ji * 128:(ji + 1) * 128], op0=ALU.mult, op1=ALU.add),
                         reads=[B["Dl"], ub, B["S0bf"]], writes=[B["S0"]])
            P.dma("sp", "o_ss", ss_out[:, hh].rearrange("s k v -> k s v"), S0[:, :, :], reads=[B["S0"]])
            def stage1(t0, tn):
                _, qap, qb = bank()
                _, fap, fb = bank()
                _, gap, gb = bank()

                def mmf(e, t0=t0, tn=tn, qap=qap, fap=fap, gap=gap):
                    ins = None
                    for (ap_, wv_, off) in ((qap, wA, 0), (fap, wB, 0), (gap, wA, 128)):
                        for c in range(KC):
                            ins = e.matmul(ap_[:, 0:tn], lhsT=wv_[:, c, off:off + 128], rhs=xn[:, c, t0:t0 + tn], start=(c == 0), stop=(c == KC - 1))
                    return ins
                P.op("pe", mmf, reads=xn_b + [wAb, wBb], writes=[qb, fb, gb])
                P.op("act", lambda e, tn=tn, qap=qap: e.activation(out=qs[:, 0:tn], in_=qap[:, 0:tn], func=AF.Silu), reads=[qb], writes=[B["qs"]])
                P.op("act", lambda e, t0=t0, tn=tn, gap=gap: e.activation(out=sg[:, t0:t0 + tn], in_=gap[:, 0:tn], func=AF.Silu), reads=[gb], writes=[B["sg"]])
                P.op("act", lambda e, tn=tn, fap=fap: e.activation(out=kF[:, 0:tn], in_=fap[:, 0:tn], func=AF.Sigmoid, scale=-1.0), reads=[fb], writes=[B["kF"]])
                P.op("dve", lambda e, tn=tn: e.tensor_scalar(out=kF[:, 0:tn], in0=kF[:, 0:tn], scalar1=omlFM[:, hh:hh + 1], scalar2=None, op0=ALU.mult),
                     reads=[B["oml"]], writes=[B["kF"]])
                _, bap, bb = bank()

                def mmb(e, t0=t0, tn=tn, bap=bap):
                    ins = None
                    if tn == 64:
                        ins = e.matmul(bap[:, 0:64], lhsT=logfT[0:64, 8, :], rhs=cm[0:64, TRI4, 0:64], start=True, stop=True)
                    else:
                        for tt in range(t0 // 128, (t0 + tn) // 128):
                            o = tt * 128 - t0
                            ins = e.matmul(bap[:, o:o + 128], lhsT=logfT[:, tt, :], rhs=cm[:, TRI, :], start=True, stop=True)
                    return ins
                P.op("pe", mmb, reads=[B["logf"], cst_b], writes=[bb])
                P.op("act", lambda e, tn=tn, bap=bap: e.activation(out=Ef[:, 0:tn], in_=bap[:, 0:tn], func=AF.Exp), reads=[bb], writes=[B["Ef"]])
                P.op("act", lambda e, tn=tn, bap=bap: e.activation(out=tF[:, 0:tn], in_=bap[:, 0:tn], func=AF.Exp, scale=-1.0), reads=[bb], writes=[B["tF"]])
                P.op("dve", lambda e, t0=t0, tn=tn: e.tensor_tensor(out=qt[:, t0:t0 + tn], in0=qs[:, 0:tn], in1=Ef[:, 0:tn], op=ALU.mult),
                     reads=[B["qs"], B["Ef"]], writes=[B["qt"]])
                P.op("dve", lambda e, t0=t0, tn=tn: e.tensor_tensor(out=kt[:, t0:t0 + tn], in0=kF[:, 0:tn], in1=tF[:, 0:tn], op=ALU.mult),
                     reads=[B["kF"], B["tF"]], writes=[B["kt"]])
                _, aap, ab = bank()

                def mma(e, t0=t0, tn=tn, aap=aap):
                    ins = None
                    if tn == 64:
                        ins = e.matmul(aap[0:64, 0:64], lhsT=kt[:, 1024:1088], rhs=qt[:, 1024:1088], start=True, stop=True)
                    else:
                        for tt in range(t0 // 128, (t0 + tn) // 128):
                            o = tt * 128 - t0
                            ins = e.matmul(aap[:, o:o + 128], lhsT=kt[:, tt * 128:(tt + 1) * 128], rhs=qt[:, tt * 128:(tt + 1) * 128], start=True, stop=True)
                    return ins
                P.op("pe", mma, reads=[B["kt"], B["qt"]], writes=[ab])
                if tn == 64:
                    P.op("dve", lambda e, aap=aap: e.tensor_tensor(out=attm[0:64, 8, 0:64], in0=aap[0:64, 0:64], in1=cm[0:64, TRI4, 0:64], op=ALU.mult),
                         reads=[ab, cst_b], writes=[B["attm"]])
                else:
                    for tt in range(t0 // 128, (t0 + tn) // 128):
                        o = tt * 128 - t0
                        P.op("dve", lambda e, tt=tt, o=o, aap=aap: e.tensor_tensor(out=attm[:, tt, :], in0=aap[:, o:o + 128], in1=cm[:, TRI, :], op=ALU.mult),
                             reads=[ab, cst_b], writes=[B["attm"]])

            def stage2(t0, tn):
                _, oap, ob = bank()

                def mmo(e, t0=t0, tn=tn, oap=oap):
                    ins = None
                    if tn == 64:
                        e.matmul(oap[:, 0:64], lhsT=vT[:, 8, :], rhs=attm[:, 8, 0:64], start=True, stop=False)
                        for j in range(16):
                            ins = e.matmul(oap[:, 4 * j:4 * j + 4], lhsT=S0bf[:, j, :], rhs=qt[:, 1024 + 4 * j:1028 + 4 * j], start=False, stop=(j == 15))
                    else:
                        for tt in range(t0 // 128, (t0 + tn) // 128):
                            o = tt * 128 - t0
                            e.matmul(oap[:, o:o + 128], lhsT=vT[:, tt, :], rhs=attm[:, tt, :], start=True, stop=False)
                            for hf in range(2):
                                c = tt * 2 + hf
                                ins = e.matmul(oap[:, o + hf * 64:o + hf * 64 + 64], lhsT=Sbf[:, c, :], rhs=qt[:, tt * 128 + hf * 64:tt * 128 + hf * 64 + 64],
                                               start=False, stop=(hf == 1))
                    return ins
                P.op("pe", mmo, reads=[B["vT"], B["attm"], B["Sbf"], B["S0bf"], B["qt"]], writes=[ob])
                P.op("act", lambda e, t0=t0, tn=tn, oap=oap: e.activation(out=osq[:, t0:t0 + tn], in_=oap[:, 0:tn], func=AF.Square), reads=[ob], writes=[B["osq"]])
                _, sap, sbk = bank()
                P.op("pe", lambda e, t0=t0, tn=tn, sap=sap: e.matmul(sap[:, 0:tn], lhsT=onesb[:], rhs=osq[:, t0:t0 + tn], start=True, stop=True),
                     reads=[B["osq"], cst_b], writes=[sbk])
                P.op("act", lambda e, tn=tn, sap=sap: e.activation(out=rst[:, 0:tn], in_=sap[:, 0:tn], func=AF.Sqrt, scale=1.0 / 128, bias=epsS[:, 0:1]),
                     reads=[sbk, cst_b], writes=[B["rst"]])
                P.op("dve", lambda e, tn=tn: e.reciprocal(out=rst[:, 0:tn], in_=rst[:, 0:tn]), writes=[B["rst"]])
                P.op("dve", lambda e, t0=t0, tn=tn: e.tensor_tensor(out=rst[:, 0:tn], in0=rst[:, 0:tn], in1=sg[:, t0:t0 + tn], op=ALU.mult),
                     reads=[B["sg"]], writes=[B["rst"]])
                P.op("dve", lambda e, t0=t0, tn=tn, oap=oap: e.scalar_tensor_tensor(out=ot[:, t0:t0 + tn], in0=oap[:, 0:tn], scalar=vf[:, V_BNG, hh:hh + 1], in1=rst[:, 0:tn],
                                                                              op0=ALU.mult, op1=ALU.mult), reads=[ob, B["rst"], cst_b], writes=[B["ot"]])
                out_accum(wO, wOb, lambda a, n: ot[:, a:a + n], [B["ot"]], [(t0, tn)])

            stage1(*TBS[0])
            stage1(*TBS[1])
            stage2(*TBS[0])
            stage1(*TBS[2])
            stage2(*TBS[1])
            stage2(*TBS[2])

        groups = None
        if mode == "B":
            P.dma("sp", "sin", Sst[:, :, :], s_in.rearrange("h k v -> k h v"), writes=[B["Sst"]])
        if mode != "B" and _DBG.get("xcore", True):
            for hh in range(16):
                head(hh, False)
            ncr = _DBG.get("ncores", NCORES)
            groups = [[2 * i, 2 * i + 1] for i in range(ncr // 2)]
            if _DBG.get("nocc"):
                P.op("dve", lambda e: e.memset(arF[:, 0:2048], 0.0), writes=[B["Sst"]])
                groups = None
            if mode == "A":
                P.dma("sp", "o_sp", sp_out.rearrange("h k v -> k h v"), Sst[:, :, :], reads=[B["Sst"]])
                return
        if mode == "fused" and _DBG.get("xcore", True) and groups is not None:
            P.dma("pool", "agi", ag_in, arF[:, 0:2048], reads=[B["Sst"]])
            ag_b = Buf()
            t_in = B["Sst"].r["agi"]
            P.chan("agc")
            P.wait("pool", t_in)
            nc.gpsimd.collective_compute("AllGather", ALU.bypass, replica_groups=groups, ins=[ag_in.opt()], outs=[ag_out.opt()]).then_inc(P.sems["agc"], 1)
            ag_b.w = ("agc", 1)
            if _DBG.get("cconly"):
                P.wait("pool", ("agc", 1))
                P.wait("dve", ("agc", 1))
                P.op("dve", lambda e: e.memset(arF[:, 0:2048], 0.0), writes=[B["Sst"]])
            else:
                P.dma("pool", "ago", arF[:, 0:2048], ag_out[0:128, :], reads=[ag_b], writes=[B["Sst"]])
            if _DBG.get("agoonly"):
                P.op("dve", lambda e: e.memset(arF[:, 0:2048], 0.0), writes=[B["Sst"]])
            elif not _DBG.get("cconly"):
                P.op("dve", lambda e: e.tensor_scalar(out=arF[:, 0:2048], in0=arF[:, 0:2048], scalar1=parS[:, 0:1], scalar2=None, op0=ALU.mult),
                     reads=[cst_b], writes=[B["Sst"]])
        for hh in range(16):
            head(hh, True)
        P.dma("sp", "o_sp", sp_out.rearrange("h k v -> k h v"), Sst[:, :, :], reads=[B["Sst"]])

    mixers = _DBG.get("mixers", True)
    if mode == "A":
        stop = "A"
    for li in range(2):
        if mode == "B" and li == 0:
            continue
        if mode != "B":
            ffn(li, 0)
        if stop == ("ffn1", li):
            break
        if mixers:
            if li == 0:
                gmlp()
            else:
                hgrn()
        if stop == ("mix", li) or (mode == "A" and li == 1):
            break
        ffn(li, 1)
        if stop == ("ffn2", li):
            break
        ple(li)
        if stop == ("ple", li):
            break

    P.phase()
    if stop is None:
        rmsnorm_to(V_FIN, lambda c: (h[:, c, :], [h_b[c]]), arF[:, 0:NT])
    P.dma("sp", "fin", yT.rearrange("(c p) n -> p c n", p=128), h[:], reads=h_b)
    for ch in [c for c in P.cnt if c.startswith("fin") or c.startswith("o_")]:
        if P.cnt[ch] > 0:
            nc.sync.wait_ge(P.sems[ch], P.cnt[ch])


def _masks():
    m = np.zeros((128, 8, 128), np.float32)
    s = np.arange(128)[:, None]
    t = np.arange(128)[None, :]
    same64 = (s // 64) == (t // 64)
    m[:, 0, :] = (same64 & (s <= t))
    m[:, 1, :] = (same64 & (s > t))
    same4 = (s // 4) == (t // 4)
    m[:, 2, :] = (same4 & (s <= t))
    m[:, 3, :] = (same4 & (s > t))
    m[:, 4, 0:16] = (s // 4 == np.arange(16)[None, :]) & (s < 64)
    m[:, 5, :] = (s <= t)
    m[:, 6, 0] = (np.arange(128) < 64)
    m[:, 6, 1] = (np.arange(128) >= 64)
    return m


def _prep_shared(inp):
    sh = {}
    for which, (kin, kout) in enumerate((("ffn1_w_in", "ffn1_w_out"), ("ffn2_w_in", "ffn2_w_out"))):
        wi = np.asarray(inp[kin])
        g = wi[:, :, :DFF].reshape(2, KC, 128, NJ, 128)
        u = wi[:, :, DFF:].reshape(2, KC, 128, NJ, 128)
        blk = np.stack([g, u], axis=4)
        blk = blk.transpose(0, 3, 2, 1, 4, 5)
        sh[f"w1_{which}"] = np.ascontiguousarray(blk).reshape(2, NJ, 128, KC * 256)
        wo = np.asarray(inp[kout])
        b = wo.reshape(2, NQ, JQ, 128, 8, 256).transpose(0, 1, 4, 3, 2, 5)
        sh[f"w2_{which}"] = np.ascontiguousarray(b).reshape(2, NQ, 8, 128, JQ * 256)
    sh["a_w_in"] = np.ascontiguousarray(inp["a_w_in"][0])
    sh["a_w_out"] = np.ascontiguousarray(inp["a_w_out"][0])
    sh["b_w_in"] = np.ascontiguousarray(inp["b_w_in"][0])
    sh["b_w_out"] = np.ascontiguousarray(inp["b_w_out"][0])
    sh["ple_wp"] = np.ascontiguousarray(inp["ple_w_proj"])
    sh["ple_wg"] = np.ascontiguousarray(inp["ple_w_gate"])
    vecs = [inp["norm_ffn1"][0], inp["norm_ffn1"][1], inp["norm_mix"][0], inp["norm_mix"][1],
            inp["norm_ffn2"][0], inp["norm_ffn2"][1], inp["norm_ple"][0], inp["norm_ple"][1],
            inp["final_norm"], inp["b_norm_g"][0], inp["b_lb_logits"][0], inp["b_lb_logits"][1]]
    v = np.stack([np.asarray(x, np.float32).reshape(KC, 128).T for x in vecs], axis=1)
    sh["vfm"] = np.ascontiguousarray(v)
    sh["cmask"] = _masks()
    gb = np.stack([np.asarray(inp["a_ln_g"][0]), np.asarray(inp["a_ln_b"][0])], 0)
    sh["a_gb"] = np.ascontiguousarray(np.broadcast_to(gb[None], (128, 2, D))).astype(np.float32)
    ws = np.asarray(inp["a_w_s"][0])
    sh["a_wsT"] = np.ascontiguousarray(ws.transpose(2, 0, 1))
    i4 = np.arange(64) % 4
    sh["a_wsS"] = np.ascontiguousarray(ws[:, i4[None, :], i4[:, None]].transpose(1, 0, 2))
    bs = np.asarray(inp["a_b_s"][0])
    sh["a_bs"] = np.ascontiguousarray(bs)
    sh["a_bsS"] = np.ascontiguousarray(bs[:, i4])
    sel = np.zeros((128, 16, 128), np.float32)
    for g_ in range(16):
        sel[g_, g_, :] = 1.0
    sh["sel16"] = sel
    lb = np.asarray(inp["b_lb_logits"])
    sh["lb_bc"] = np.ascontiguousarray(np.broadcast_to(lb[None], (128, 2, D))).astype(np.float32)
    return sh


def _prep_core(inp, c):
    s, half = c // 2, c % 2
    xp = np.asarray(inp["x_prompt"])[s, half * NPR:(half + 1) * NPR]
    xs = np.asarray(inp["x_sample"])[16 * c:16 * c + 16].reshape(NSM, D)
    m = {"xT": np.ascontiguousarray(np.concatenate([xp, xs], 0).T)}
    pp = np.asarray(inp["p_prompt"])[:, s, half * NPR:(half + 1) * NPR]
    psm = np.asarray(inp["p_sample"])[:, 16 * c:16 * c + 16].reshape(2, NSM, DPLE)
    m["pT"] = np.ascontiguousarray(np.concatenate([pp, psm], 1).transpose(0, 2, 1))
    m["s0"] = np.ascontiguousarray(np.asarray(inp["state_hgrn"])[0, 16 * c:16 * c + 16])
    m["par"] = np.full((128, 8), float(half), np.float32)
    return m


_CACHE = {}


def kernel(**inputs):
    if "nc" not in _CACHE:
        if TWO_LAUNCH:
            _CACHE["two"] = True
            _CACHE["nc"] = build_program("A")
            _CACHE["ncB"] = build_program("B")
        else:
            _CACHE["nc"] = build_program()
    nc = _CACHE["nc"]
    sh = _prep_shared(inputs)
    names = set()
    in_maps = []
    for c in range(NCORES):
        m = dict(sh)
        m.update(_prep_core(inputs, c))
        in_maps.append(m)
    res = run_bass_kernel_spmd(nc, in_maps, core_ids=list(range(NCORES)))
    R = res.results
    if _CACHE.get("two"):
        RA = R
        ncB = _CACHE["ncB"]
        for c in range(NCORES):
            in_maps[c]["xT"] = np.ascontiguousarray(RA[c]["yT"])
            in_maps[c]["s_in"] = np.ascontiguousarray(RA[c - 1]["sp_out"]) if c % 2 == 1 else np.zeros((16, 128, 128), np.float32)
        R = run_bass_kernel_spmd(ncB, in_maps, core_ids=list(range(NCORES))).results
        for c in range(NCORES):
            R[c]["vs_out"] = RA[c]["vs_out"]
    y_prompt = np.zeros((4, 2048, D), np.float32)
    y_sample = np.zeros((128, 4, D), np.float32)
    sp = np.zeros((1, 4, 16, 128, 128), np.float32)
    ss = np.zeros((1, 128, 16, 128, 128), np.float32)
    vs = np.zeros((1, 128, 4, D), np.float32)
    for c in range(NCORES):
        s, half = c // 2, c % 2
        y = R[c]["yT"].T
        y_prompt[s, half * NPR:(half + 1) * NPR] = y[:NPR]
        y_sample[16 * c:16 * c + 16] = y[NPR:].reshape(16, 4, D)
        if half == 1:
            sp[0, s] = R[c]["sp_out"]
        ss[0, 16 * c:16 * c + 16] = R[c]["ss_out"]
        vs[0, 16 * c:16 * c + 16] = R[c]["vs_out"].reshape(16, 4, D)
    return (y_prompt, y_sample, sp, ss, vs)
```

```python
import numpy as np
from contextlib import ExitStack
import ml_dtypes
import concourse.bass as bass
import concourse.mybir as mybir
from concourse.bass_utils import run_bass_kernel_spmd

F32 = mybir.dt.float32
BF16 = mybir.dt.bfloat16
AF = mybir.ActivationFunctionType
ALU = mybir.AluOpType

NCORES = 8
D = 2048
KC = 16
NT = 1088
NPR = 1024
NSM = 64
DFF = 5632
NJ = 44
NQ = 4
JQ = 11
DPLE = 256
EPS = 1e-6
TBS = [(0, 512), (512, 512), (1024, 64)]
NTILE = 9
RING = 3
SLOT = 4096

V_NF1, V_NMIX, V_NF2, V_NPLE = 0, 2, 4, 6
V_FIN, V_BNG, V_L0, V_L1 = 8, 9, 10, 11
NV = 12

_DBG = {"stop": None}
TWO_LAUNCH = False


class Buf:
    __slots__ = ("w", "r")

    def __init__(self):
        self.w = None
        self.r = {}


class Prog:
    def __init__(self, nc, es):
        self.nc = nc
        self.es = es
        self.eng = {"pe": nc.tensor, "act": nc.scalar, "dve": nc.vector, "pool": nc.gpsimd, "sp": nc.sync}
        self.sems = {}
        self.cnt = {}
        self.seen = {}
        self.nsig = 0
        self.pending = {}
        self.T = []
        for e in self.eng:
            self._mk(e)

    def phase(self):
        self.T = [(k, v) for k, v in self.cnt.items() if v > 0]
        self.pending = {e: True for e in self.eng}

    def _mk(self, name):
        self.sems[name] = self.es.enter_context(self.nc.semaphore("s_" + name))
        self.cnt[name] = 0

    def chan(self, name):
        if name not in self.sems:
            self._mk(name)
        return name

    def wait(self, eng, t):
        if t is None:
            return
        p, n = t
        if p == eng and eng == "pe":
            return
        k = (eng, p)
        if self.seen.get(k, 0) >= n:
            return
        self.seen[k] = n
        self.eng[eng].wait_ge(self.sems[p], n)

    def _deps(self, eng, reads, writes, extra, skip_phase=False):
        if self.pending.get(eng) and not skip_phase:
            self.pending[eng] = False
            for t in self.T:
                self.wait(eng, t)
        for t in extra:
            self.wait(eng, t)
        for b in reads:
            self.wait(eng, b.w)
        for b in writes:
            self.wait(eng, b.w)
            for t in list(b.r.values()):
                self.wait(eng, t)

    def _commit(self, t, reads, writes):
        for b in reads:
            b.r[t[0]] = t
        for b in writes:
            b.w = t
            b.r = {}

    def op(self, eng, fn, reads=(), writes=(), extra=()):
        self._deps(eng, reads, writes, extra)
        ins = fn(self.eng[eng])
        self.cnt[eng] += 1
        ins.then_inc(self.sems[eng], 1)
        t = (eng, self.cnt[eng])
        self._commit(t, reads, writes)
        return t

    def dma(self, q, ch, out, in_, reads=(), writes=(), extra=(), n=1, fn=None, skip_phase=False):
        if ch.endswith("*"):
            self.nsig += 1
            ch = ch[:-1] + str(self.nsig)
        self.chan(ch)
        self._deps(q, reads, writes, extra, skip_phase)
        e = self.eng[q]
        if fn is None:
            e.dma_start(out=out, in_=in_).then_inc(self.sems[ch], 16)
        else:
            n = fn(e, self.sems[ch])
        self.cnt[ch] += 16 * n
        t = (ch, self.cnt[ch])
        self._commit(t, reads, writes)
        return t


def build_program(mode="fused"):
    nc = bass.Bass("TRN2", target_bir_lowering=False)
    es = ExitStack()
    with es:
        _build(nc, es, mode)
    return nc


def _build(nc, es, mode="fused"):
    P = Prog(nc, es)
    stop = _DBG["stop"]

    def din(name, shape, dt=F32):
        return nc.dram_tensor(name, list(shape), dt, kind="ExternalInput").ap()

    def dout(name, shape, dt=F32):
        return nc.dram_tensor(name, list(shape), dt, kind="ExternalOutput").ap()

    xT = din("xT", [D, NT])
    pT = din("pT", [2, DPLE, NT])
    s0 = din("s0", [16, 16, 128, 128])
    vfm = din("vfm", [128, NV, KC])
    cmask = din("cmask", [128, 8, 128])
    par = din("par", [128, 8])
    w1 = [din(f"w1_{i}", [2, NJ, 128, KC * 256]) for i in range(2)]
    w2 = [din(f"w2_{i}", [2, NQ, 8, 128, JQ * 256]) for i in range(2)]
    a_w_in = din("a_w_in", [D, 2 * D])
    a_w_out = din("a_w_out", [D, D])
    b_w_in = din("b_w_in", [D, 4 * D])
    b_w_out = din("b_w_out", [D, D])
    ple_wp = din("ple_wp", [2, DPLE, D])
    ple_wg = din("ple_wg", [2, D, D])
    a_gb = din("a_gb", [128, 2, D])
    a_wsT = din("a_wsT", [128, 16, 128])
    a_wsS = din("a_wsS", [64, 16, 64])
    a_bs = din("a_bs", [16, 128])
    a_bsS = din("a_bsS", [16, 64])
    sel16 = din("sel16", [128, 16, 128])
    lb_bc = din("lb_bc", [128, 2, D])
    s_in = din("s_in", [16, 128, 128]) if mode == "B" else None

    ag_in = nc.dram_tensor("ag_in", [128, 2048], F32).ap()
    ag_out = nc.dram_tensor("ag_out", [256, 2048], F32).ap()
    yT = dout("yT", [D, NT])
    sp_out = dout("sp_out", [16, 128, 128])
    ss_out = dout("ss_out", [16, 16, 128, 128])
    vs_out = dout("vs_out", [NSM, D])

    def sb(name, shape, dt):
        return es.enter_context(nc.sbuf_tensor(name, list(shape), dt))

    h = sb("h", [128, KC, NT], F32)
    xn = sb("xn", [128, KC, NT], BF16)
    ring = sb("ring", [128, RING, SLOT], BF16)
    vf = sb("vf", [128, NV, KC], F32)
    cm = sb("cm", [128, 8, 128], F32)
    cmb = sb("cmb", [128, 8, 128], BF16)
    onesb = sb("onesb", [128, 128], BF16)
    parS = sb("parS", [128, 8], F32)
    epsS = sb("epsS", [128, 1], F32)
    ARF = 10432
    ARB = 16768
    arF = sb("arF", [128, ARF], F32)
    arB = sb("arB", [128, ARB], BF16)
    ps = es.enter_context(nc.psum_tensor("ps", [128, 8 * 512], F32))

    h_b = [Buf() for _ in range(KC)]
    xn_b = [Buf() for _ in range(KC)]
    ring_b = [Buf() for _ in range(RING)]
    bank_b = [Buf() for _ in range(8)]
    cst_b = Buf()
    st = {"ring": 0, "bank": 0}

    def bank():
        i = st["bank"] % 7
        st["bank"] += 1
        return i, ps[:, i * 512:(i + 1) * 512], bank_b[i]

    def aux_bank():
        return 7, ps[:, 7 * 512:8 * 512], bank_b[7]

    def wslot():
        i = st["ring"] % RING
        st["ring"] += 1
        return i, ring_b[i]

    def wload(srcs, kc, ncol):
        i, rb = wslot()
        view = ring[:, i, 0:kc * ncol].rearrange("p (k n) -> p k n", k=kc)

        def fn(e, sem):
            for (src, off, w) in srcs:
                e.dma_start(out=view[:, :, off:off + w], in_=src).then_inc(sem, 16)
            return len(srcs)
        P.dma("pool", f"ring{i}", None, None, writes=[rb], fn=fn, skip_phase=True)
        return view, rb

    P.dma("sp", "init*", h[:], xT.rearrange("(c p) n -> p c n", p=128), writes=h_b)
    P.dma("sp", "init*", vf[:], vfm, writes=[cst_b])
    P.dma("sp", "init*", cm[:], cmask, writes=[cst_b])
    P.dma("sp", "init*", parS[:], par, writes=[cst_b])
    P.op("dve", lambda e: e.tensor_copy(out=cmb[:], in_=cm[:]), reads=[cst_b], writes=[cst_b])
    P.op("dve", lambda e: e.memset(onesb[:], 1.0), writes=[cst_b])
    P.op("dve", lambda e: e.memset(epsS[:], EPS), writes=[cst_b])
    TRI, RM, TRI4, RM4, OH = 0, 1, 2, 3, 4
    if _DBG.get("earlycc"):
        ncr = _DBG.get("ncores", NCORES)
        groups0 = [[2 * i, 2 * i + 1] for i in range(ncr // 2)]
        ag0_in = nc.dram_tensor("ag0_in", [128, 1024], F32).ap()
        ag0_out = nc.dram_tensor("ag0_out", [256, 1024], F32).ap()
        P.dma("pool", "agi0", ag0_in, cm[:, :, :].rearrange("p a b -> p (a b)"), reads=[cst_b])
        P.chan("agc0")
        P.wait("pool", ("agi0", 16))
        nc.gpsimd.collective_compute("AllGather", ALU.bypass, replica_groups=groups0, ins=[ag0_in.opt()], outs=[ag0_out.opt()]).then_inc(P.sems["agc0"], 1)
        nc.gpsimd.wait_ge(P.sems["agc0"], 1)

    def rmsnorm_to(vidx, dst_fn, tmpF):
        for c in range(KC):
            P.op("act", lambda e, c=c: e.activation(out=xn[:, c, :], in_=h[:, c, :], func=AF.Square),
                 reads=[h_b[c]], writes=[xn_b[c]])
        rstd = tmpF
        rb_ = Buf()
        for (t0, tn) in TBS:
            bi, bap, bb = bank()

            def mm(e, t0=t0, tn=tn, bap=bap):
                ins = None
                for c in range(KC):
                    ins = e.matmul(bap[:, 0:tn], lhsT=onesb[:], rhs=xn[:, c, t0:t0 + tn], start=(c == 0), stop=(c == KC - 1))
                return ins
            P.op("pe", mm, reads=xn_b + [cst_b], writes=[bb])
            P.op("act", lambda e, t0=t0, tn=tn, bap=bap: e.activation(out=rstd[:, t0:t0 + tn], in_=bap[:, 0:tn], func=AF.Sqrt,
                                                                  scale=1.0 / D, bias=epsS[:, 0:1]),
                 reads=[bb, cst_b], writes=[rb_])
        P.op("dve", lambda e: e.reciprocal(out=rstd[:, :], in_=rstd[:, :]), reads=[rb_], writes=[rb_])
        for c in range(KC):
            o_ap, obufs = dst_fn(c)
            P.op("dve", lambda e, c=c, o_ap=o_ap: e.scalar_tensor_tensor(out=o_ap, in0=h[:, c, :], scalar=vf[:, vidx, c:c + 1],
                                                                        in1=rstd[:, :], op0=ALU.mult, op1=ALU.mult),
                 reads=[h_b[c], rb_, cst_b], writes=obufs)

    def norm_xn(vidx, tmpF):
        rmsnorm_to(vidx, lambda c: (xn[:, c, :], [xn_b[c]]), tmpF)

    def ffn(li, which):
        w_in = w1[which]
        w_out = w2[which]
        vidx = (V_NF1 if which == 0 else V_NF2) + li
        rstd = arF[:, 0:NT]
        sg = [arF[:, NT + 512 * i: NT + 512 * (i + 1)] for i in range(4)]
        sg_b = [Buf() for _ in range(4)]
        hid = arB[:, 0:JQ * NT].rearrange("p (j n) -> p j n", j=JQ)
        hid_b = [Buf() for _ in range(JQ)]
        P.phase()
        norm_xn(vidx, rstd)
        k = 0
        for q in range(NQ):
            for jj in range(JQ):
                j = q * JQ + jj
                wv, wb = wload([(w_in[li, j].rearrange("p (k n) -> p k n", k=KC), 0, 256)], KC, 256)
                for (t0, tn) in TBS:
                    _, gap, gb = bank()
                    _, uap, ub = bank()

                    def mm(e, wv=wv, t0=t0, tn=tn, gap=gap, uap=uap):
                        ins = None
                        for (ap_, off) in ((gap, 0), (uap, 128)):
                            for c in range(KC):
                                ins = e.matmul(ap_[:, 0:tn], lhsT=wv[:, c, off:off + 128], rhs=xn[:, c, t0:t0 + tn],
                                               start=(c == 0), stop=(c == KC - 1))
                        return ins
                    P.op("pe", mm, reads=xn_b + [wb], writes=[gb, ub])
                    s_i = k % 4
                    k += 1
                    P.op("act", lambda e, s_i=s_i, tn=tn, gap=gap: e.activation(out=sg[s_i][:, 0:tn], in_=gap[:, 0:tn], func=AF.Silu),
                         reads=[gb], writes=[sg_b[s_i]])
                    P.op("dve", lambda e, s_i=s_i, jj=jj, t0=t0, tn=tn, uap=uap: e.tensor_tensor(
                        out=hid[:, jj, t0:t0 + tn], in0=sg[s_i][:, 0:tn], in1=uap[:, 0:tn], op=ALU.mult),
                        reads=[sg_b[s_i], ub], writes=[hid_b[jj]])
            for fp in range(8):
                wv, wb = wload([(w_out[li, q, fp].rearrange("p (k n) -> p k n", k=JQ), 0, 256)], JQ, 256)
                for f2 in range(2):
                    fo = fp * 2 + f2
                    for (t0, tn) in TBS:
                        _, oap, ob = bank()

                        def mm(e, wv=wv, f2=f2, t0=t0, tn=tn, oap=oap):
                            ins = None
                            for jj in range(JQ):
                                ins = e.matmul(oap[:, 0:tn], lhsT=wv[:, jj, f2 * 128:(f2 + 1) * 128], rhs=hid[:, jj, t0:t0 + tn],
                                               start=(jj == 0), stop=(jj == JQ - 1))
                            return ins
                        P.op("pe", mm, reads=hid_b + [wb], writes=[ob])
                        P.op("dve", lambda e, fo=fo, t0=t0, tn=tn, oap=oap: e.scalar_tensor_tensor(
                            out=h[:, fo, t0:t0 + tn], in0=oap[:, 0:tn], scalar=0.5, in1=h[:, fo, t0:t0 + tn],
                            op0=ALU.mult, op1=ALU.add), reads=[ob, h_b[fo]], writes=[h_b[fo]])

    def ple(li):
        rstd = arF[:, 0:NT]
        gt = [arF[:, NT + 512 * i: NT + 512 * (i + 1)] for i in range(4)]
        gt_b = [Buf() for _ in range(4)]
        pb = arB[:, 0:2 * NT].rearrange("p (k n) -> p k n", k=2)
        pb_b = Buf()
        P.phase()
        norm_xn(V_NPLE + li, rstd)
        P.dma("pool", "pld", pb, pT[li].rearrange("(k p) n -> p k n", p=128), writes=[pb_b])
        k = 0
        for f8 in range(8):
            wv, wb = wload([(ple_wg[li, :, f8 * 256:(f8 + 1) * 256].rearrange("(k p) n -> p k n", p=128), 0, 256)], KC, 256)
            wpv, wpb = wload([(ple_wp[li, :, f8 * 256:(f8 + 1) * 256].rearrange("(k p) n -> p k n", p=128), 0, 256)], 2, 256)
            for f2 in range(2):
                fo = f8 * 2 + f2
                for (t0, tn) in TBS:
                    _, gap, gb = bank()
                    _, pap, pbk = bank()

                    def mm(e, wv=wv, wpv=wpv, f2=f2, t0=t0, tn=tn, gap=gap, pap=pap):
                        ins = None
                        for c in range(KC):
                            ins = e.matmul(gap[:, 0:tn], lhsT=wv[:, c, f2 * 128:(f2 + 1) * 128], rhs=xn[:, c, t0:t0 + tn],
                                           start=(c == 0), stop=(c == KC - 1))
                        for c in range(2):
                            ins = e.matmul(pap[:, 0:tn], lhsT=wpv[:, c, f2 * 128:(f2 + 1) * 128], rhs=pb[:, c, t0:t0 + tn],
                                           start=(c == 0), stop=(c == 1))
                        return ins
                    P.op("pe", mm, reads=xn_b + [wb, wpb, pb_b], writes=[gb, pbk])
                    s_i = k % 4
                    k += 1
                    P.op("act", lambda e, s_i=s_i, tn=tn, gap=gap: e.activation(out=gt[s_i][:, 0:tn], in_=gap[:, 0:tn], func=AF.Sigmoid),
                         reads=[gb], writes=[gt_b[s_i]])
                    P.op("dve", lambda e, s_i=s_i, tn=tn, pap=pap: e.tensor_tensor(
                        out=gt[s_i][:, 0:tn], in0=gt[s_i][:, 0:tn], in1=pap[:, 0:tn], op=ALU.mult),
                        reads=[gt_b[s_i], pbk], writes=[gt_b[s_i]])
                    P.op("dve", lambda e, s_i=s_i, fo=fo, t0=t0, tn=tn: e.tensor_tensor(
                        out=h[:, fo, t0:t0 + tn], in0=h[:, fo, t0:t0 + tn], in1=gt[s_i][:, 0:tn], op=ALU.add),
                        reads=[gt_b[s_i], h_b[fo]], writes=[h_b[fo]])

    def out_accum(wv, wb, act_ap_fn, act_bufs, tbs):
        for fo in range(KC):
            for (t0, tn) in tbs:
                _, oap, ob = bank()
                P.op("pe", lambda e, fo=fo, t0=t0, tn=tn, oap=oap: e.matmul(
                    oap[:, 0:tn], lhsT=wv[:, 0, fo * 128:(fo + 1) * 128], rhs=act_ap_fn(t0, tn), start=True, stop=True),
                    reads=act_bufs + [wb], writes=[ob])
                P.op("dve", lambda e, fo=fo, t0=t0, tn=tn, oap=oap: e.tensor_tensor(
                    out=h[:, fo, t0:t0 + tn], in0=h[:, fo, t0:t0 + tn], in1=oap[:, 0:tn], op=ALU.add),
                    reads=[ob, h_b[fo]], writes=[h_b[fo]])


    def gmlp():
        AX = mybir.AxisListType
        gbc = arF[:, 0:2048]
        bbc = arF[:, 2048:4096]
        vS = arF[:, 4096:6144]
        vfx = [arF[:, 6144 + 512 * i: 6144 + 512 * (i + 1)] for i in range(2)]
        ugx = [arF[:, 7168 + 512 * i: 7168 + 512 * (i + 1)] for i in range(2)]
        stt = arF[:, 8192:8192 + 512]
        sum1 = stt[:, 0:72].rearrange("p (t c) -> p t c", t=NTILE)
        sum2 = stt[:, 72:144].rearrange("p (t c) -> p t c", t=NTILE)
        sm = stt[:, 144:144 + 8 * NTILE].rearrange("p (t c) -> p t c", t=NTILE)
        junk = arF[:, 8704:8704 + 256]
        bsF = arF[:, 9216:9216 + 192]
        vT = arB[:, 0:10240].rearrange("p (t n) -> p t n", t=5)
        WsM = arB[:, 10240:12288].rearrange("p (g t) -> p g t", g=16)
        WsS = arB[:, 12288:13312].rearrange("p (g t) -> p g t", g=16)
        usx = [arB[:, 13312 + 512 * i: 13312 + 512 * (i + 1)] for i in range(2)]
        bsH = arB[:, 14336:14464]
        bsL = arB[:, 14464:14592]
        bsSH = arB[:, 14592:14656]
        bsSL = arB[:, 14656:14720]
        SEL = arB[:, 14720:16768].rearrange("p (g t) -> p g t", g=16)
        gb_b, vS_b, stt_b, ws_b, bs_b = Buf(), Buf(), Buf(), Buf(), Buf()
        vfx_b = [Buf(), Buf()]
        ugx_b = [Buf(), Buf()]
        usx_b = [Buf(), Buf()]
        junk_b = Buf()
        vT_b = [Buf() for _ in range(5)]
        rstd = arF[:, 4096:4096 + NT]
        P.phase()
        norm_xn(V_NMIX + 0, rstd)
        P.dma("sp", "gml*", gbc, a_gb[:, 0, :], writes=[gb_b])
        P.dma("sp", "gml*", bbc, a_gb[:, 1, :], writes=[gb_b])
        P.dma("sp", "gml*", bsF[0:16, 0:128], a_bs, writes=[bs_b])
        P.dma("sp", "gml*", bsF[0:16, 128:192], a_bsS, writes=[bs_b])
        P.dma("pool", "gmlp*", WsM[:, :, :], a_wsT, writes=[ws_b])
        P.dma("pool", "gmlp*", WsS[0:64, :, :], a_wsS, writes=[ws_b])
        P.dma("pool", "gmlp*", SEL[:, :, :], sel16, writes=[ws_b])
        for g in range(16):
            P.op("dve", lambda e, g=g: e.tensor_tensor(out=WsM[:, g, :], in0=WsM[:, g, :], in1=cmb[:, 5, :], op=ALU.mult),
                 reads=[cst_b], writes=[ws_b])
            P.op("dve", lambda e, g=g: e.tensor_tensor(out=WsS[0:64, g, :], in0=WsS[0:64, g, :], in1=cmb[0:64, 2, 0:64], op=ALU.mult),
                 reads=[cst_b], writes=[ws_b])
        P.op("dve", lambda e: e.memset(arB[:, 14336:14720], 0.0), writes=[ws_b])
        P.op("dve", lambda e: e.tensor_copy(out=bsH[0:16, :], in_=bsF[0:16, 0:128]), reads=[bs_b], writes=[ws_b])
        P.op("dve", lambda e: e.tensor_tensor(out=bsL[0:16, :], in0=bsF[0:16, 0:128], in1=bsH[0:16, :], op=ALU.subtract), reads=[bs_b], writes=[ws_b])
        P.op("dve", lambda e: e.tensor_copy(out=bsSH[0:16, :], in_=bsF[0:16, 128:192]), reads=[bs_b], writes=[ws_b])
        P.op("dve", lambda e: e.tensor_tensor(out=bsSL[0:16, :], in0=bsF[0:16, 128:192], in1=bsSH[0:16, :], op=ALU.subtract), reads=[bs_b], writes=[ws_b])

        for (tiles, tbs) in (([0, 1, 2, 3], [(0, 512)]), ([4, 5, 6, 7, 8], [(512, 512), (1024, 64)])):
            k = 0
            for cb in range(8):
                wv, wb = wload([(a_w_in[:, D + cb * 256: D + (cb + 1) * 256].rearrange("(k p) n -> p k n", p=128), 0, 256)], KC, 256)
                for lt, tt in enumerate(tiles):
                    rows = 64 if tt == 8 else 128
                    _, bap, bb = bank()

                    def mm(e, wv=wv, tt=tt, rows=rows, bap=bap):
                        ins = None
                        for c in range(KC):
                            ins = e.matmul(bap[0:rows, 0:256], lhsT=xn[:, c, tt * 128: tt * 128 + rows], rhs=wv[:, c, 0:256],
                                           start=(c == 0), stop=(c == KC - 1))
                        return ins
                    P.op("pe", mm, reads=xn_b + [wb], writes=[bb])
                    i = k % 2
                    k += 1
                    P.op("act", lambda e, i=i, rows=rows, bap=bap, tt=tt, cb=cb: e.activation(
                        out=vfx[i][0:rows, 0:256], in_=bap[0:rows, 0:256], func=AF.Gelu, accum_out=sum1[0:rows, tt, cb:cb + 1]),
                        reads=[bb], writes=[vfx_b[i], stt_b])
                    P.op("dve", lambda e, i=i, rows=rows, tt=tt, cb=cb: e.scalar_tensor_tensor(
                        out=junk[0:rows, 0:256], in0=vfx[i][0:rows, 0:256], scalar=1.0, in1=vfx[i][0:rows, 0:256], op0=ALU.mult, op1=ALU.mult,
                        accum_out=sum2[0:rows, tt, cb:cb + 1]), reads=[vfx_b[i]], writes=[junk_b, stt_b])
                    if tt == 8:
                        P.op("dve", lambda e, i=i, cb=cb: e.tensor_copy(out=vS[0:64, cb * 256:(cb + 1) * 256], in_=vfx[i][0:64, 0:256]),
                             reads=[vfx_b[i]], writes=[vS_b])
                    else:
                        P.op("dve", lambda e, i=i, lt=lt, cb=cb: e.tensor_copy(out=vT[:, lt, cb * 256:(cb + 1) * 256], in_=vfx[i][:, 0:256]),
                             reads=[vfx_b[i]], writes=[vT_b[lt]])
            for lt, tt in sorted(enumerate(tiles), key=lambda x: -x[1]):
                rows = 64 if tt == 8 else 128
                S = lambda j, tt=tt, rows=rows: sm[0:rows, tt, j:j + 1]
                P.op("dve", lambda e, tt=tt, rows=rows, S=S: e.tensor_reduce(out=S(0), in_=sum1[0:rows, tt, :], axis=AX.X, op=ALU.add),
                     reads=[stt_b], writes=[stt_b])
                P.op("dve", lambda e, tt=tt, rows=rows, S=S: e.tensor_reduce(out=S(1), in_=sum2[0:rows, tt, :], axis=AX.X, op=ALU.add),
                     reads=[stt_b], writes=[stt_b])
                P.op("dve", lambda e, S=S: e.tensor_scalar(out=S(2), in0=S(0), scalar1=1.0 / D, scalar2=None, op0=ALU.mult), reads=[stt_b], writes=[stt_b])
                P.op("dve", lambda e, S=S: e.tensor_tensor(out=S(3), in0=S(2), in1=S(2), op=ALU.mult), reads=[stt_b], writes=[stt_b])
                P.op("dve", lambda e, S=S: e.scalar_tensor_tensor(out=S(4), in0=S(1), scalar=1.0 / D, in1=S(3), op0=ALU.mult, op1=ALU.subtract),
                     reads=[stt_b], writes=[stt_b])
                P.op("act", lambda e, S=S, rows=rows: e.activation(out=S(5), in_=S(4), func=AF.Sqrt, bias=epsS[0:rows, 0:1]),
                     reads=[stt_b, cst_b], writes=[stt_b])
                P.op("dve", lambda e, S=S: e.reciprocal(out=S(5), in_=S(5)), reads=[stt_b], writes=[stt_b])
                P.op("dve", lambda e, S=S: e.scalar_tensor_tensor(out=S(6), in0=S(2), scalar=-1.0, in1=S(5), op0=ALU.mult, op1=ALU.mult),
                     reads=[stt_b], writes=[stt_b])
                if tt == 8:
                    P.op("act", lambda e, S=S: e.activation(out=vS[0:64, :], in_=vS[0:64, :], func=AF.Identity, scale=S(5), bias=S(6)),
                         reads=[stt_b], writes=[vS_b])
                    P.op("dve", lambda e: e.tensor_tensor(out=vS[0:64, :], in0=vS[0:64, :], in1=gbc[0:64, :], op=ALU.mult), reads=[gb_b], writes=[vS_b])
                    P.op("dve", lambda e: e.tensor_tensor(out=vS[0:64, :], in0=vS[0:64, :], in1=bbc[0:64, :], op=ALU.add), reads=[gb_b], writes=[vS_b])
                    P.op("dve", lambda e, lt=lt: e.tensor_copy(out=vT[0:64, lt, :], in_=vS[0:64, :]), reads=[vS_b], writes=[vT_b[lt]])
                    P.dma("sp", "o_vs", vs_out, vS[0:64, :], reads=[vS_b])
                else:
                    P.op("act", lambda e, S=S, lt=lt: e.activation(out=vS[:, :], in_=vT[:, lt, :], func=AF.Identity, scale=S(5), bias=S(6)),
                         reads=[stt_b, vT_b[lt]], writes=[vS_b])
                    P.op("dve", lambda e: e.tensor_tensor(out=vS[:, :], in0=vS[:, :], in1=gbc, op=ALU.mult), reads=[gb_b], writes=[vS_b])
                    P.op("dve", lambda e, lt=lt: e.tensor_tensor(out=vT[:, lt, :], in0=vS[:, :], in1=bbc, op=ALU.add), reads=[gb_b, vS_b], writes=[vT_b[lt]])
            k = 0
            for gp in range(8):
                wv, wb = wload([(a_w_in[:, gp * 256:(gp + 1) * 256].rearrange("(k p) n -> p k n", p=128), 0, 256)], KC, 256)
                for g2 in range(2):
                    g = gp * 2 + g2
                    wo, wob = wload([(a_w_out[g * 128:(g + 1) * 128, :].rearrange("(k p) n -> p k n", p=128), 0, D)], 1, D)
                    for (t0, tn) in tbs:
                        _, uap, ub = bank()
                        _, sap, sbk = bank()

                        def mm(e, wv=wv, g2=g2, g=g, t0=t0, tn=tn, uap=uap, sap=sap, tiles=tiles):
                            ins = None
                            for c in range(KC):
                                ins = e.matmul(uap[:, 0:tn], lhsT=wv[:, c, g2 * 128:(g2 + 1) * 128], rhs=xn[:, c, t0:t0 + tn],
                                               start=(c == 0), stop=(c == KC - 1))
                            if tn == 64:
                                lt = tiles.index(8)
                                ins = e.matmul(sap[:, 0:64], lhsT=vT[0:64, lt, g * 128:(g + 1) * 128], rhs=WsS[0:64, g, :], start=True, stop=_DBG.get("nobias", False))
                                if not _DBG.get("nobias"):
                                    e.matmul(sap[:, 0:64], lhsT=SEL[:, g, :], rhs=bsSH[:, :], start=False, stop=False)
                                    ins = e.matmul(sap[:, 0:64], lhsT=SEL[:, g, :], rhs=bsSL[:, :], start=False, stop=True)
                            else:
                                for tt in range(t0 // 128, (t0 + tn) // 128):
                                    lt = tiles.index(tt)
                                    o = tt * 128 - t0
                                    ins = e.matmul(sap[:, o:o + 128], lhsT=vT[:, lt, g * 128:(g + 1) * 128], rhs=WsM[:, g, :], start=True, stop=_DBG.get("nobias", False))
                                    if not _DBG.get("nobias"):
                                        e.matmul(sap[:, o:o + 128], lhsT=SEL[:, g, :], rhs=bsH[:, :], start=False, stop=False)
                                        ins = e.matmul(sap[:, o:o + 128], lhsT=SEL[:, g, :], rhs=bsL[:, :], start=False, stop=True)
                            return ins
                        P.op("pe", mm, reads=xn_b + vT_b + [wb, ws_b], writes=[ub, sbk])
                        i = k % 2
                        k += 1
                        P.op("act", lambda e, i=i, tn=tn, uap=uap: e.activation(out=ugx[i][:, 0:tn], in_=uap[:, 0:tn], func=AF.Gelu),
                             reads=[ub], writes=[ugx_b[i]])
                        P.op("dve", lambda e, i=i, tn=tn, sap=sap: e.tensor_tensor(out=usx[i][:, 0:tn], in0=ugx[i][:, 0:tn], in1=sap[:, 0:tn], op=ALU.mult),
                             reads=[ugx_b[i], sbk], writes=[usx_b[i]])
                        out_accum(wo, wob, lambda a, n, i=i: usx[i][:, 0:n], [usx_b[i]], [(t0, tn)])


    def hgrn():
        Sst = arF[:, 0:2048].rearrange("p (h v) -> p h v", h=16)
        S0 = arF[:, 2048:4096].rearrange("p (s v) -> p s v", s=16)
        qs = arF[:, 4096:4608]
        kF = arF[:, 4608:5120]
        Ef = arF[:, 5120:5632]
        tF = arF[:, 5632:6144]
        rst = arF[:, 6144:6656]
        logfT = arF[:, 6656:7808].rearrange("p (t k) -> p t k", t=NTILE)
        t1x = [arF[:, 7808:8064], arF[:, 8064:8320]]
        kTx = [arF[:, 8320:8576], arF[:, 8576:8832]]
        edTx = [arF[:, 8832:9088], arF[:, 9088:9344]]
        lbh = arF[:, 9344:9600]
        omlh = arF[:, 9600:9856]
        l01 = arF[:, 9856:10368]
        Dl = arF[:, 10368:10400]
        omlFM = arF[:, 10400:10416]
        qt = arB[:, 0:1088]
        kt = arB[:, 1088:2176]
        sg = arB[:, 2176:3264]
        osq = arB[:, 3264:4352]
        ot = arB[:, 4352:5440]
        kdec = arB[:, 5440:6592].rearrange("p (t k) -> p t k", t=NTILE)
        kdecB = arB[:, 13248:14400].rearrange("p (t k) -> p t k", t=NTILE)
        vT = arB[:, 6592:7744].rearrange("p (t k) -> p t k", t=NTILE)
        attm = arB[:, 7744:8896].rearrange("p (t k) -> p t k", t=NTILE)
        Sbf = arB[:, 8896:10944].rearrange("p (c v) -> p c v", c=16)
        S0bf = arB[:, 10944:12992].rearrange("p (s v) -> p s v", s=16)
        vmk = [arB[:, 12992 + 128 * i: 12992 + 128 * (i + 1)] for i in range(2)]
        B = {k: Buf() for k in ("Sst", "S0", "qs", "kF", "Ef", "tF", "rst", "logf", "t1", "kT", "edT", "lb", "l01", "Dl", "oml",
                                "qt", "kt", "sg", "osq", "ot", "kdec", "vT", "attm", "Sbf", "S0bf", "vmk0", "vmk1",
                                "t1_0", "t1_1", "kT_0", "kT_1", "edT_0", "edT_1")}
        rstd = arF[:, 4096:4096 + NT]
        P.phase()
        norm_xn(V_NMIX + 1, rstd)
        P.op("dve", lambda e: e.tensor_tensor(out=omlFM, in0=vf[:, V_L0, :], in1=vf[:, V_L1, :], op=ALU.subtract), reads=[cst_b], writes=[B["oml"]])
        P.op("act", lambda e: e.activation(out=omlFM, in_=omlFM, func=AF.Sigmoid), writes=[B["oml"]])
        P.op("dve", lambda e: e.memset(arF[:, 0:2048], 0.0), writes=[B["Sst"]])
        P.op("dve", lambda e: e.memset(arB[:, 0:14400], 0.0), writes=[B[k_] for k_ in ("kdec", "vT", "attm", "vmk0", "vmk1", "Sbf", "S0bf", "qt", "kt")])

        def head(hh, full):
            c0 = hh * 128
            wB, wBb = wload([(b_w_in[:, D + c0:D + c0 + 128].rearrange("(k p) n -> p k n", p=128), 0, 128),
                             (b_w_in[:, 2 * D + c0:2 * D + c0 + 128].rearrange("(k p) n -> p k n", p=128), 128, 128)], KC, 256)
            if full:
                wA, wAb = wload([(b_w_in[:, c0:c0 + 128].rearrange("(k p) n -> p k n", p=128), 0, 128),
                                 (b_w_in[:, 3 * D + c0:3 * D + c0 + 128].rearrange("(k p) n -> p k n", p=128), 128, 128)], KC, 256)
            if full:
                wO, wOb = wload([(b_w_out[c0:c0 + 128, :].rearrange("(k p) n -> p k n", p=128), 0, D)], 1, D)
            P.dma("sp", "lbA", l01[:, 0:128], lb_bc[:, 0, c0:c0 + 128], writes=[B["l01"]])
            P.dma("sp", "lbB", l01[:, 128:256], lb_bc[:, 0, c0:c0 + 128], writes=[B["l01"]])
            P.dma("sp", "lbC", l01[:, 256:384], lb_bc[:, 1, c0:c0 + 128], writes=[B["l01"]])
            P.dma("sp", "lbD", l01[:, 384:512], lb_bc[:, 1, c0:c0 + 128], writes=[B["l01"]])
            for ch_ in ("lbA", "lbB", "lbC", "lbD"):
                P.wait("dve", (ch_, P.cnt[ch_]))
            P.op("dve", lambda e: e.tensor_tensor(out=lbh, in0=l01[:, 0:256], in1=l01[:, 256:512], op=ALU.subtract), reads=[B["l01"]], writes=[B["lb"]])
            P.op("act", lambda e: e.activation(out=lbh, in_=lbh, func=AF.Exp), writes=[B["lb"]])
            P.op("dve", lambda e: e.tensor_scalar(out=lbh, in0=lbh, scalar1=1.0, scalar2=None, op0=ALU.add), writes=[B["lb"]])
            P.op("dve", lambda e: e.reciprocal(out=lbh, in_=lbh), writes=[B["lb"]])
            P.op("dve", lambda e: e.tensor_scalar(out=omlh, in0=lbh, scalar1=-1.0, scalar2=1.0, op0=ALU.mult, op1=ALU.add), reads=[B["lb"]], writes=[B["lb"]])
            tiles = list(range(NTILE)) if full else list(range(8))
            _, blap, blb = aux_bank()
            tgroups = [(0, 2), (2, 2), (4, 2), (6, 2)] + ([(8, 1)] if full else [])
            for gi, (tA, ntl) in enumerate(tgroups):
                rows = 64 if tA == 8 else 128
                W = ntl * 128
                _, bap, bb = bank()

                def mm(e, tA=tA, ntl=ntl, rows=rows, bap=bap):
                    ins = None
                    for i in range(ntl):
                        tt = tA + i
                        for c in range(KC):
                            ins = e.matmul(bap[0:rows, i * 256:(i + 1) * 256], lhsT=xn[:, c, tt * 128: tt * 128 + rows], rhs=wB[:, c, 0:256],
                                           start=(c == 0), stop=(c == KC - 1))
                    return ins
                P.op("pe", mm, reads=xn_b + [wBb], writes=[bb])
                pp = gi % 2
                t1 = t1x[pp]
                kT = kTx[pp]
                edT = edTx[pp]
                Bt1, BkT, Bed = B[f"t1_{pp}"], B[f"kT_{pp}"], B[f"edT_{pp}"]
                bv = bap[0:rows, 0:ntl * 256].rearrange("p (t c) -> p t c", t=ntl)
                t13 = t1[0:rows, 0:W].rearrange("p (t c) -> p t c", t=ntl)
                t_exp = P.op("act", lambda e, bv=bv, t13=t13: e.activation(out=t13, in_=bv[:, :, 0:128], func=AF.Exp, scale=-1.0), reads=[bb], writes=[Bt1])
                P.op("act", lambda e, rows=rows, bv=bv, tA=tA, ntl=ntl: e.activation(out=vT[0:rows, tA:tA + ntl, :], in_=bv[:, :, 128:256], func=AF.Copy), reads=[bb], writes=[B["vT"]], extra=[t_exp])
                P.op("dve", lambda e, rows=rows, t1=t1, W=W: e.tensor_scalar(out=t1[0:rows, 0:W], in0=t1[0:rows, 0:W], scalar1=1.0, scalar2=None, op0=ALU.add), writes=[Bt1])
                P.op("dve", lambda e, rows=rows, t1=t1, W=W: e.reciprocal(out=t1[0:rows, 0:W], in_=t1[0:rows, 0:W]), writes=[Bt1])
                P.op("dve", lambda e, rows=rows, t1=t1, W=W: e.tensor_tensor(out=t1[0:rows, 0:W], in0=t1[0:rows, 0:W], in1=omlh[0:rows, 0:W], op=ALU.mult), reads=[B["lb"]], writes=[Bt1])
                P.op("dve", lambda e, rows=rows, t1=t1, W=W: e.tensor_tensor(out=t1[0:rows, 0:W], in0=t1[0:rows, 0:W], in1=lbh[0:rows, 0:W], op=ALU.add), reads=[B["lb"]], writes=[Bt1])
                P.op("act", lambda e, rows=rows, tA=tA, ntl=ntl, t13=t13: e.activation(out=logfT[0:rows, tA:tA + ntl, :], in_=t13, func=AF.Ln), reads=[Bt1], writes=[B["logf"]])
                P.op("dve", lambda e, rows=rows, t1=t1, kT=kT, W=W: e.tensor_scalar(out=kT[0:rows, 0:W], in0=t1[0:rows, 0:W], scalar1=-1.0, scalar2=1.0, op0=ALU.mult, op1=ALU.add),
                     reads=[Bt1], writes=[BkT])
                _, dap, db = bank()
                mk = RM4 if tA == 8 else RM

                def mmd(e, tA=tA, ntl=ntl, rows=rows, dap=dap, mk=mk):
                    ins = None
                    for i in range(ntl):
                        ins = e.matmul(dap[0:rows, i * 128:(i + 1) * 128], lhsT=cm[0:rows, mk, 0:rows], rhs=logfT[0:rows, tA + i, :], start=True, stop=True)
                    return ins
                P.op("pe", mmd, reads=[B["logf"], cst_b], writes=[db])
                P.op("act", lambda e, rows=rows, dap=dap, edT=edT, W=W: e.activation(out=edT[0:rows, 0:W], in_=dap[0:rows, 0:W], func=AF.Exp), reads=[db], writes=[Bed])
                kT3 = kT[0:rows, 0:W].rearrange("p (t c) -> p t c", t=ntl)
                ed3 = edT[0:rows, 0:W].rearrange("p (t c) -> p t c", t=ntl)
                P.op("dve", lambda e, rows=rows, tA=tA, ntl=ntl, kT3=kT3, ed3=ed3: e.scalar_tensor_tensor(out=kdec[0:rows, tA:tA + ntl, :], in0=kT3, scalar=cm[0:rows, 6, 0:1], in1=ed3,
                                                                                       op0=ALU.mult, op1=ALU.mult), reads=[BkT, Bed, cst_b], writes=[B["kdec"]])
                if tA < 8:
                    P.op("dve", lambda e, tA=tA, ntl=ntl, kT3=kT3, ed3=ed3: e.scalar_tensor_tensor(out=kdecB[:, tA:tA + ntl, :], in0=kT3, scalar=cm[:, 6, 1:2], in1=ed3,
                                                                                   op0=ALU.mult, op1=ALU.mult), reads=[BkT, Bed, cst_b], writes=[B["kdec"]])

                def mmbl(e, tA=tA, ntl=ntl, blap=blap):
                    ins = None
                    if tA == 8:
                        ins = e.matmul(blap[:, 16:32], lhsT=logfT[0:64, 8, :], rhs=cm[0:64, TRI4, 3:64:4], start=True, stop=True)
                    else:
                        for i in range(ntl):
                            tt = tA + i
                            ins = e.matmul(blap[:, 2 * tt:2 * tt + 2], lhsT=logfT[:, tt, :], rhs=cm[:, TRI, 63:128:64], start=True, stop=True)
                    return ins
                P.op("pe", mmbl, reads=[B["logf"], cst_b], writes=[blb])
            nb = 32 if full else 16
            P.op("act", lambda e, blap=blap, nb=nb: e.activation(out=Dl[:, 0:nb], in_=blap[:, 0:nb], func=AF.Exp), reads=[blb], writes=[B["Dl"]])
            for c4 in range(4):
                _, uap, ub = bank()

                def mmu(e, c4=c4, uap=uap):
                    ins = None
                    for ci in range(4):
                        c = c4 * 4 + ci
                        tt, hf = c // 2, c % 2
                        ins = e.matmul(uap[:, ci * 128:(ci + 1) * 128], lhsT=(kdecB if hf else kdec)[:, tt, :], rhs=vT[:, tt, :],
                                       start=True, stop=True)
                    return ins
                P.op("pe", mmu, reads=[B["kdec"], B["vT"]], writes=[ub])
                for ci in range(4):
                    c = c4 * 4 + ci
                    if full:
                        P.op("dve", lambda e, c=c: e.tensor_copy(out=Sbf[:, c, :], in_=Sst[:, hh, :]), reads=[B["Sst"]], writes=[B["Sbf"]])
                    P.op("dve", lambda e, c=c, ci=ci, uap=uap: e.scalar_tensor_tensor(out=Sst[:, hh, :], in0=Sst[:, hh, :], scalar=Dl[:, c:c + 1],
                                                                                  in1=uap[:, ci * 128:(ci + 1) * 128], op0=ALU.mult, op1=ALU.add),
                         reads=[B["Dl"], ub], writes=[B["Sst"]])
            if not full:
                return
            P.dma("sp", "s0ld", S0[:, :, :], s0[:, hh].rearrange("s k v -> k s v"), writes=[B["S0"]])
            P.op("act", lambda e: e.activation(out=S0bf[:, :, :], in_=S0[:, :, :], func=AF.Copy), reads=[B["S0"]], writes=[B["S0bf"]])
            for j4 in range(4):
                _, uap, ub = bank()
                for ji in range(4):
                    j = j4 * 4 + ji
                    i = j % 2
                    P.op("dve", lambda e, i=i, j=j: e.tensor_scalar(out=vmk[i][0:64, :], in0=vT[0:64, 8, :], scalar1=cm[0:64, OH, j:j + 1], scalar2=None, op0=ALU.mult),
                         reads=[B["vT"], cst_b], writes=[B[f"vmk{i}"]])
                    P.op("pe", lambda e, i=i, ji=ji, uap=uap: e.matmul(uap[:, ji * 128:(ji + 1) * 128], lhsT=kdec[:, 8, :], rhs=vmk[i][:, :], start=True, stop=True),
                         reads=[B["kdec"], B[f"vmk{i}"]], writes=[ub])
                for ji in range(4):
                    j = j4 * 4 + ji
                    P.op("dve", lambda e, j=j, ji=ji, uap=uap: e.scalar_tensor_tensor(out=S0[:, j, :], in0=S0[:, j, :], scalar=Dl[:, 16 + j:17 + j],
                                                                                  in1=uap[:, ji * 128:(ji + 1) * 128], op0=ALU.mult, op1=ALU.add),
                         reads=[B["Dl"], ub, B["S0bf"]], writes=[B["S0"]])
            P.dma("sp", "o_ss", ss_out[:, hh].rearrange("s k v -> k s v"), S0[:, :, :], reads=[B["S0"]])
            def stage1(t0, tn):
                _, qap, qb = bank()
                _, fap, fb = bank()
                _, gap, gb = bank()

                def mmf(e, t0=t0, tn=tn, qap=qap, fap=fap, gap=gap):
                    ins = None
                    for (ap_, wv_, off) in ((qap, wA, 0), (fap, wB, 0), (gap, wA, 128)):
                        for c in range(KC):
                            ins = e.matmul(ap_[:, 0:tn], lhsT=wv_[:, c, off:off + 128], rhs=xn[:, c, t0:t0 + tn], start=(c == 0), stop=(c == KC - 1))
                    return ins
                P.op("pe", mmf, reads=xn_b + [wAb, wBb], writes=[qb, fb, gb])
                P.op("act", lambda e, tn=tn, qap=qap: e.activation(out=qs[:, 0:tn], in_=qap[:, 0:tn], func=AF.Silu), reads=[qb], writes=[B["qs"]])
                P.op("act", lambda e, t0=t0, tn=tn, gap=gap: e.activation(out=sg[:, t0:t0 + tn], in_=gap[:, 0:tn], func=AF.Silu), reads=[gb], writes=[B["sg"]])
                P.op("act", lambda e, tn=tn, fap=fap: e.activation(out=kF[:, 0:tn], in_=fap[:, 0:tn], func=AF.Sigmoid, scale=-1.0), reads=[fb], writes=[B["kF"]])
                P.op("dve", lambda e, tn=tn: e.tensor_scalar(out=kF[:, 0:tn], in0=kF[:, 0:tn], scalar1=omlFM[:, hh:hh + 1], scalar2=None, op0=ALU.mult),
                     reads=[B["oml"]], writes=[B["kF"]])
                _, bap, bb = bank()

                def mmb(e, t0=t0, tn=tn, bap=bap):
                    ins = None
                    if tn == 64:
                        ins = e.matmul(bap[:, 0:64], lhsT=logfT[0:64, 8, :], rhs=cm[0:64, TRI4, 0:64], start=True, stop=True)
                    else:
                        for tt in range(t0 // 128, (t0 + tn) // 128):
                            o = tt * 128 - t0
                            ins = e.matmul(bap[:, o:o + 128], lhsT=logfT[:, tt, :], rhs=cm[:, TRI, :], start=True, stop=True)
                    return ins
                P.op("pe", mmb, reads=[B["logf"], cst_b], writes=[bb])
                P.op("act", lambda e, tn=tn, bap=bap: e.activation(out=Ef[:, 0:tn], in_=bap[:, 0:tn], func=AF.Exp), reads=[bb], writes=[B["Ef"]])
                P.op("act", lambda e, tn=tn, bap=bap: e.activation(out=tF[:, 0:tn], in_=bap[:, 0:tn], func=AF.Exp, scale=-1.0), reads=[bb], writes=[B["tF"]])
                P.op("dve", lambda e, t0=t0, tn=tn: e.tensor_tensor(out=qt[:, t0:t0 + tn], in0=qs[:, 0:tn], in1=Ef[:, 0:tn], op=ALU.mult),
                     reads=[B["qs"], B["Ef"]], writes=[B["qt"]])
                P.op("dve", lambda e, t0=t0, tn=tn: e.tensor_tensor(out=kt[:, t0:t0 + tn], in0=kF[:, 0:tn], in1=tF[:, 0:tn], op=ALU.mult),
                     reads=[B["kF"], B["tF"]], writes=[B["kt"]])
                _, aap, ab = bank()

                def mma(e, t0=t0, tn=tn, aap=aap):
                    ins = None
                    if tn == 64:
                        ins = e.matmul(aap[0:64, 0:64], lhsT=kt[:, 1024:1088], rhs=qt[:, 1024:1088], start=True, stop=True)
                    else:
                        for tt in range(t0 // 128, (t0 + tn) // 128):
                            o = tt * 128 - t0
                            ins = e.matmul(aap[:, o:o + 128], lhsT=kt[:, tt * 128:(tt + 1) * 128], rhs=qt[:, tt * 128:(tt + 1) * 128], start=True, stop=True)
                    return ins
                P.op("pe", mma, reads=[B["kt"], B["qt"]], writes=[ab])
                if tn == 64:
                    P.op("dve", lambda e, aap=aap: e.tensor_tensor(out=attm[0:64, 8, 0:64], in0=aap[0:64, 0:64], in1=cm[0:64, TRI4, 0:64], op=ALU.mult),
                         reads=[ab, cst_b], writes=[B["attm"]])
                else:
                    for tt in range(t0 // 128, (t0 + tn) // 128):
                        o = tt * 128 - t0
                        P.op("dve", lambda e, tt=tt, o=o, aap=aap: e.tensor_tensor(out=attm[:, tt, :], in0=aap[:, o:o + 128], in1=cm[:, TRI, :], op=ALU.mult),
                             reads=[ab, cst_b], writes=[B["attm"]])

            def stage2(t0, tn):
                _, oap, ob = bank()

                def mmo(e, t0=t0, tn=tn, oap=oap):
                    ins = None
                    if tn == 64:
                        e.matmul(oap[:, 0:64], lhsT=vT[:, 8, :], rhs=attm[:, 8, 0:64], start=True, stop=False)
                        for j in range(16):
                            ins = e.matmul(oap[:, 4 * j:4 * j + 4], lhsT=S0bf[:, j, :], rhs=qt[:, 1024 + 4 * j:1028 + 4 * j], start=False, stop=(j == 15))
                    else:
                        for tt in range(t0 // 128, (t0 + tn) // 128):
                            o = tt * 128 - t0
                            e.matmul(oap[:, o:o + 128], lhsT=vT[:, tt, :], rhs=attm[:, tt, :], start=True, stop=False)
                            for hf in range(2):
                                c = tt * 2 + hf
                                ins = e.matmul(oap[:, o + hf * 64:o + hf * 64 + 64], lhsT=Sbf[:, c, :], rhs=qt[:, tt * 128 + hf * 64:tt * 128 + hf * 64 + 64],
                                               start=False, stop=(hf == 1))
                    return ins
                P.op("pe", mmo, reads=[B["vT"], B["attm"], B["Sbf"], B["S0bf"], B["qt"]], writes=[ob])
                P.op("act", lambda e, t0=t0, tn=tn, oap=oap: e.activation(out=osq[:, t0:t0 + tn], in_=oap[:, 0:tn], func=AF.Square), reads=[ob], writes=[B["osq"]])
                _, sap, sbk = bank()
                P.op("pe", lambda e, t0=t0, tn=tn, sap=sap: e.matmul(sap[:, 0:tn], lhsT=onesb[:], rhs=osq[:, t0:t0 + tn], start=True, stop=True),
                     reads=[B["osq"], cst_b], writes=[sbk])
                P.op("act", lambda e, tn=tn, sap=sap: e.activation(out=rst[:, 0:tn], in_=sap[:, 0:tn], func=AF.Sqrt, scale=1.0 / 128, bias=epsS[:, 0:1]),
                     reads=[sbk, cst_b], writes=[B["rst"]])
                P.op("dve", lambda e, tn=tn: e.reciprocal(out=rst[:, 0:tn], in_=rst[:, 0:tn]), writes=[B["rst"]])
                P.op("dve", lambda e, t0=t0, tn=tn: e.tensor_tensor(out=rst[:, 0:tn], in0=rst[:, 0:tn], in1=sg[:, t0:t0 + tn], op=ALU.mult),
                     reads=[B["sg"]], writes=[B["rst"]])
                P.op("dve", lambda e, t0=t0, tn=tn, oap=oap: e.scalar_tensor_tensor(out=ot[:, t0:t0 + tn], in0=oap[:, 0:tn], scalar=vf[:, V_BNG, hh:hh + 1], in1=rst[:, 0:tn],
                                                                              op0=ALU.mult, op1=ALU.mult), reads=[ob, B["rst"], cst_b], writes=[B["ot"]])
                out_accum(wO, wOb, lambda a, n: ot[:, a:a + n], [B["ot"]], [(t0, tn)])

            stage1(*TBS[0])
            stage1(*TBS[1])
            stage2(*TBS[0])
            stage1(*TBS[2])
            stage2(*TBS[1])
            stage2(*TBS[2])

        groups = None
        if mode == "B":
            P.dma("sp", "sin", Sst[:, :, :], s_in.rearrange("h k v -> k h v"), writes=[B["Sst"]])
        if mode != "B" and _DBG.get("xcore", True):
            for hh in range(16):
                head(hh, False)
            ncr = _DBG.get("ncores", NCORES)
            groups = [[2 * i, 2 * i + 1] for i in range(ncr // 2)]
            if _DBG.get("nocc"):
                P.op("dve", lambda e: e.memset(arF[:, 0:2048], 0.0), writes=[B["Sst"]])
                groups = None
            if mode == "A":
                P.dma("sp", "o_sp", sp_out.rearrange("h k v -> k h v"), Sst[:, :, :], reads=[B["Sst"]])
                return
        if mode == "fused" and _DBG.get("xcore", True) and groups is not None:
            P.dma("pool", "agi", ag_in, arF[:, 0:2048], reads=[B["Sst"]])
            ag_b = Buf()
            t_in = B["Sst"].r["agi"]
            P.chan("agc")
            P.wait("pool", t_in)
            nc.gpsimd.collective_compute("AllGather", ALU.bypass, replica_groups=groups, ins=[ag_in.opt()], outs=[ag_out.opt()]).then_inc(P.sems["agc"], 1)
            ag_b.w = ("agc", 1)
            if _DBG.get("cconly"):
                P.wait("pool", ("agc", 1))
                P.wait("dve", ("agc", 1))
                P.op("dve", lambda e: e.memset(arF[:, 0:2048], 0.0), writes=[B["Sst"]])
            else:
                P.dma("pool", "ago", arF[:, 0:2048], ag_out[0:128, :], reads=[ag_b], writes=[B["Sst"]])
            if _DBG.get("agoonly"):
                P.op("dve", lambda e: e.memset(arF[:, 0:2048], 0.0), writes=[B["Sst"]])
            elif not _DBG.get("cconly"):
                P.op("dve", lambda e: e.tensor_scalar(out=arF[:, 0:2048], in0=arF[:, 0:2048], scalar1=parS[:, 0:1], scalar2=None, op0=ALU.mult),
                     reads=[cst_b], writes=[B["Sst"]])
        for hh in range(16):
            head(hh, True)
        P.dma("sp", "o_sp", sp_out.rearrange("h k v -> k h v"), Sst[:, :, :], reads=[B["Sst"]])

    mixers = _DBG.get("mixers", True)
    if mode == "A":
        stop = "A"
    for li in range(2):
        if mode == "B" and li == 0:
            continue
        if mode != "B":
            ffn(li, 0)
        if stop == ("ffn1", li):
            break
        if mixers:
            if li == 0:
                gmlp()
            else:
                hgrn()
        if stop == ("mix", li) or (mode == "A" and li == 1):
            break
        ffn(li, 1)
        if stop == ("ffn2", li):
            break
        ple(li)
        if stop == ("ple", li):
            break

    P.phase()
    if stop is None:
        rmsnorm_to(V_FIN, lambda c: (h[:, c, :], [h_b[c]]), arF[:, 0:NT])
    P.dma("sp", "fin", yT.rearrange("(c p) n -> p c n", p=128), h[:], reads=h_b)
    for ch in [c for c in P.cnt if c.startswith("fin") or c.startswith("o_")]:
        if P.cnt[ch] > 0:
            nc.sync.wait_ge(P.sems[ch], P.cnt[ch])


def _masks():
    m = np.zeros((128, 8, 128), np.float32)
    s = np.arange(128)[:, None]
    t = np.arange(128)[None, :]
    same64 = (s // 64) == (t // 64)
    m[:, 0, :] = (same64 & (s <= t))
    m[:, 1, :] = (same64 & (s > t))
    same4 = (s // 4) == (t // 4)
    m[:, 2, :] = (same4 & (s <= t))
    m[:, 3, :] = (same4 & (s > t))
    m[:, 4, 0:16] = (s // 4 == np.arange(16)[None, :]) & (s < 64)
    m[:, 5, :] = (s <= t)
    m[:, 6, 0] = (np.arange(128) < 64)
    m[:, 6, 1] = (np.arange(128) >= 64)
    return m


def _prep_shared(inp):
    sh = {}
    for which, (kin, kout) in enumerate((("ffn1_w_in", "ffn1_w_out"), ("ffn2_w_in", "ffn2_w_out"))):
        wi = np.asarray(inp[kin])
        g = wi[:, :, :DFF].reshape(2, KC, 128, NJ, 128)
        u = wi[:, :, DFF:].reshape(2, KC, 128, NJ, 128)
        blk = np.stack([g, u], axis=4)
        blk = blk.transpose(0, 3, 2, 1, 4, 5)
        sh[f"w1_{which}"] = np.ascontiguousarray(blk).reshape(2, NJ, 128, KC * 256)
        wo = np.asarray(inp[kout])
        b = wo.reshape(2, NQ, JQ, 128, 8, 256).transpose(0, 1, 4, 3, 2, 5)
        sh[f"w2_{which}"] = np.ascontiguousarray(b).reshape(2, NQ, 8, 128, JQ * 256)
    sh["a_w_in"] = np.ascontiguousarray(inp["a_w_in"][0])
    sh["a_w_out"] = np.ascontiguousarray(inp["a_w_out"][0])
    sh["b_w_in"] = np.ascontiguousarray(inp["b_w_in"][0])
    sh["b_w_out"] = np.ascontiguousarray(inp["b_w_out"][0])
    sh["ple_wp"] = np.ascontiguousarray(inp["ple_w_proj"])
    sh["ple_wg"] = np.ascontiguousarray(inp["ple_w_gate"])
    vecs = [inp["norm_ffn1"][0], inp["norm_ffn1"][1], inp["norm_mix"][0], inp["norm_mix"][1],
            inp["norm_ffn2"][0], inp["norm_ffn2"][1], inp["norm_ple"][0], inp["norm_ple"][1],
            inp["final_norm"], inp["b_norm_g"][0], inp["b_lb_logits"][0], inp["b_lb_logits"][1]]
    v = np.stack([np.asarray(x, np.float32).reshape(KC, 128).T for x in vecs], axis=1)
    sh["vfm"] = np.ascontiguousarray(v)
    sh["cmask"] = _masks()
    gb = np.stack([np.asarray(inp["a_ln_g"][0]), np.asarray(inp["a_ln_b"][0])], 0)
    sh["a_gb"] = np.ascontiguousarray(np.broadcast_to(gb[None], (128, 2, D))).astype(np.float32)
    ws = np.asarray(inp["a_w_s"][0])
    sh["a_wsT"] = np.ascontiguousarray(ws.transpose(2, 0, 1))
    i4 = np.arange(64) % 4
    sh["a_wsS"] = np.ascontiguousarray(ws[:, i4[None, :], i4[:, None]].transpose(1, 0, 2))
    bs = np.asarray(inp["a_b_s"][0])
    sh["a_bs"] = np.ascontiguousarray(bs)
    sh["a_bsS"] = np.ascontiguousarray(bs[:, i4])
    sel = np.zeros((128, 16, 128), np.float32)
    for g_ in range(16):
        sel[g_, g_, :] = 1.0
    sh["sel16"] = sel
    lb = np.asarray(inp["b_lb_logits"])
    sh["lb_bc"] = np.ascontiguousarray(np.broadcast_to(lb[None], (128, 2, D))).astype(np.float32)
    return sh


def _prep_core(inp, c):
    s, half = c // 2, c % 2
    xp = np.asarray(inp["x_prompt"])[s, half * NPR:(half + 1) * NPR]
    xs = np.asarray(inp["x_sample"])[16 * c:16 * c + 16].reshape(NSM, D)
    m = {"xT": np.ascontiguousarray(np.concatenate([xp, xs], 0).T)}
    pp = np.asarray(inp["p_prompt"])[:, s, half * NPR:(half + 1) * NPR]
    psm = np.asarray(inp["p_sample"])[:, 16 * c:16 * c + 16].reshape(2, NSM, DPLE)
    m["pT"] = np.ascontiguousarray(np.concatenate([pp, psm], 1).transpose(0, 2, 1))
    m["s0"] = np.ascontiguousarray(np.asarray(inp["state_hgrn"])[0, 16 * c:16 * c + 16])
    m["par"] = np.full((128, 8), float(half), np.float32)
    return m


_CACHE = {}


def kernel(**inputs):
    if "nc" not in _CACHE:
        if TWO_LAUNCH:
            _CACHE["two"] = True
            _CACHE["nc"] = build_program("A")
            _CACHE["ncB"] = build_program("B")
        else:
            _CACHE["nc"] = build_program()
    nc = _CACHE["nc"]
    sh = _prep_shared(inputs)
    names = set()
    in_maps = []
    for c in range(NCORES):
        m = dict(sh)
        m.update(_prep_core(inputs, c))
        in_maps.append(m)
    res = run_bass_kernel_spmd(nc, in_maps, core_ids=list(range(NCORES)))
    R = res.results
    if _CACHE.get("two"):
        RA = R
        ncB = _CACHE["ncB"]
        for c in range(NCORES):
            in_maps[c]["xT"] = np.ascontiguousarray(RA[c]["yT"])
            in_maps[c]["s_in"] = np.ascontiguousarray(RA[c - 1]["sp_out"]) if c % 2 == 1 else np.zeros((16, 128, 128), np.float32)
        R = run_bass_kernel_spmd(ncB, in_maps, core_ids=list(range(NCORES))).results
        for c in range(NCORES):
            R[c]["vs_out"] = RA[c]["vs_out"]
    y_prompt = np.zeros((4, 2048, D), np.float32)
    y_sample = np.zeros((128, 4, D), np.float32)
    sp = np.zeros((1, 4, 16, 128, 128), np.float32)
    ss = np.zeros((1, 128, 16, 128, 128), np.float32)
    vs = np.zeros((1, 128, 4, D), np.float32)
    for c in range(NCORES):
        s, half = c // 2, c % 2
        y = R[c]["yT"].T
        y_prompt[s, half * NPR:(half + 1) * NPR] = y[:NPR]
        y_sample[16 * c:16 * c + 16] = y[NPR:].reshape(16, 4, D)
        if half == 1:
            sp[0, s] = R[c]["sp_out"]
        ss[0, 16 * c:16 * c + 16] = R[c]["ss_out"]
        vs[0, 16 * c:16 * c + 16] = R[c]["vs_out"].reshape(16, 4, D)
    return (y_prompt, y_sample, sp, ss, vs)
```
